# Optimizing a Trainium2 kernel written in Bass

```python
import math
import jax, jax.numpy as jnp
from jax import lax
import numpy as np

D_MODEL = 1024
BATCH = 8
SEQ = 4096
DEPTH = 4
DEC_BATCH = 16
DEC_SEQ = 4096
PAST_LEN = 128

N_META = 16
CHUNK = 128
PAD = CHUNK - N_META
SSD_HEADS = 16
SSD_HEAD_DIM = 64
SSD_WIDTH = SSD_HEADS * SSD_HEAD_DIM
SSD_GROUPS = 2
SSD_STATE = 128
CONV_WIDTH = 5
CONV_CH = SSD_WIDTH + 2 * SSD_GROUPS * SSD_STATE
RET_HEADS = 8
RET_QK_DIM = 64
RET_V_DIM = 128
RET_WIDTH = RET_HEADS * RET_V_DIM
ROPE_BASE = 10000.0
MIX_WIDTH = SSD_WIDTH + RET_WIDTH
D_FF = -(-8 * D_MODEL // (3 * 256)) * 256
EPS = 1e-6
SPLITS = (SSD_WIDTH, CONV_CH, 2 * SSD_HEADS, RET_HEADS * RET_QK_DIM, RET_HEADS * RET_QK_DIM, RET_WIDTH, RET_WIDTH)
N_IN = sum(SPLITS)
SPLIT_IDX = tuple(int(v) for v in np.cumsum(SPLITS)[:-1])

kernel_name = "hymba_style_ssd_retention_encoder"

F32 = jnp.float32


def _rms(x):
    x = x.astype(F32)
    return x * lax.rsqrt(jnp.mean(jnp.square(x), axis=-1, keepdims=True) + EPS)


def _rmsnorm(x, w):
    return (_rms(x) * w.astype(F32)).astype(x.dtype)


def _to_chunks(a):
    b, l = a.shape[:2]
    a = a.reshape((b, l // CHUNK, CHUNK) + a.shape[2:])
    return jnp.moveaxis(a, 1, 0)


def _from_chunks(a):
    a = jnp.moveaxis(a, 0, 1)
    return a.reshape((a.shape[0], a.shape[1] * a.shape[2]) + a.shape[3:])


def _dwconv(x, w):
    return lax.conv_general_dilated(x, w.astype(x.dtype)[:, None, :], window_strides=(1,),
                                    padding=[(CONV_WIDTH // 2, CONV_WIDTH // 2)],
                                    dimension_numbers=("NWC", "WIO", "NWC"),
                                    feature_group_count=x.shape[-1])


def _rotary(x, cos, sin):
    x1, x2 = jnp.split(x.astype(F32), 2, axis=-1)
    c, s = cos[None, :, None, :], sin[None, :, None, :]
    return jnp.concatenate([x1 * c - x2 * s, x2 * c + x1 * s], axis=-1)


def _ssd_scan(x, dt, A, B, C, include_diag):
    b, l, h, p = x.shape
    g, n = B.shape[2], B.shape[3]
    j = h // g
    dt = dt.astype(F32).reshape(b, l, g, j)
    dA = dt * A.astype(F32).reshape(g, j)
    xdt = x.astype(F32).reshape(b, l, g, j, p) * dt[..., None]
    tri = jnp.tril(jnp.ones((CHUNK, CHUNK), bool), 0 if include_diag else -1)[None, :, :, None, None]

    def step(state, inp):
        xdt_c, dA_c, B_c, C_c = inp
        acs = jnp.cumsum(dA_c, axis=1)
        seg = acs[:, :, None] - acs[:, None, :]
        Lm = jnp.exp(jnp.where(tri, seg, -jnp.inf))
        cb = jnp.einsum('blgn,bsgn->blsg', C_c, B_c)
        y = jnp.einsum('blsg,blsgj,bsgjp->blgjp', cb, Lm, xdt_c)
        y = y + jnp.einsum('blgn,bgjpn->blgjp', C_c, state) * jnp.exp(acs)[..., None]
        w_end = jnp.exp(acs[:, -1:] - acs)
        state = state * jnp.exp(acs[:, -1])[..., None, None] + jnp.einsum('bsgn,bsgj,bsgjp->bgjpn', B_c, w_end, xdt_c)
        return state, y

    init = jnp.zeros((b, g, j, p, n), F32)
    _, y = lax.scan(step, init, (_to_chunks(xdt), _to_chunks(dA), _to_chunks(B.astype(F32)), _to_chunks(C.astype(F32))))
    return _from_chunks(y).reshape(b, l, h, p)


def _retention_scan(q, k, v, log_gamma, include_diag):
    b, l, h, d = q.shape
    e = v.shape[-1]
    lg = log_gamma.astype(F32)
    idx = jnp.arange(CHUNK, dtype=F32)
    rel = idx[:, None] - idx[None, :]
    mask = rel >= 0 if include_diag else rel > 0
    dmat = jnp.where(mask[..., None], jnp.exp(jnp.where(mask, rel, 0.0)[..., None] * lg), 0.0)
    xi = jnp.exp((idx + 1.0)[:, None] * lg)
    zeta = jnp.exp((CHUNK - 1.0 - idx)[:, None] * lg)
    g_chunk = jnp.exp(CHUNK * lg)

    def step(R, inp):
        qc, kc, vc = inp
        s = jnp.einsum('blhd,bshd->blsh', qc, kc) * dmat
        o = jnp.einsum('blsh,bshe->blhe', s, vc) + jnp.einsum('blhd,bhde->blhe', qc * xi[None, :, :, None], R)
        R = R * g_chunk[:, None, None] + jnp.einsum('bshd,bshe->bhde', kc * zeta[None, :, :, None], vc)
        return R, o

    init = jnp.zeros((b, h, d, e), F32)
    _, o = lax.scan(step, init, (_to_chunks(q.astype(F32)), _to_chunks(k.astype(F32)), _to_chunks(v.astype(F32))))
    return _from_chunks(o)


def _layer(h, valid, cos, sin, g_mix_pre, g_mix_post, g_ffn_pre, g_ffn_post, w_in, conv_w, conv_b,
           dt_bias, a_log, d_skip, ssd_norm, ret_log_decay, w_out, w_gate, w_up, w_down):
    b, l, _ = h.shape
    flip = lambda a: jnp.flip(a, axis=1)
    u = _rmsnorm(h, g_mix_pre) * valid[None, :, None]
    proj = u @ w_in
    z, xbc, dt_raw, q, k, v, gate = jnp.split(proj, SPLIT_IDX, axis=-1)
    xbc = jax.nn.silu(_dwconv(xbc, conv_w) + conv_b)
    xs, bs, cs = jnp.split(xbc, [SSD_WIDTH, SSD_WIDTH + SSD_GROUPS * SSD_STATE], axis=-1)
    xs = xs.reshape(b, l, SSD_HEADS, SSD_HEAD_DIM)
    bs = bs.reshape(b, l, SSD_GROUPS, SSD_STATE)
    cs = cs.reshape(b, l, SSD_GROUPS, SSD_STATE)
    dt = jax.nn.softplus(dt_raw.reshape(b, l, 2, SSD_HEADS).astype(F32) + dt_bias.astype(F32)) * valid.astype(F32)[None, :, None, None]
    a = -jnp.exp(a_log.astype(F32))
    y_f = _ssd_scan(xs, dt[:, :, 0], a[0], bs, cs, True)
    y_b = flip(_ssd_scan(flip(xs), flip(dt[:, :, 1]), a[1], flip(bs), flip(cs), False))
    y = y_f + y_b + xs.astype(F32) * d_skip.astype(F32)[:, None]
    y = (y.reshape(b, l, SSD_WIDTH) * jax.nn.silu(z.astype(F32))).reshape(b, l, SSD_GROUPS, SSD_WIDTH // SSD_GROUPS)
    y_ssd = _rms(y).reshape(b, l, SSD_WIDTH) * ssd_norm.astype(F32)
    q = _rotary(q.reshape(b, l, RET_HEADS, RET_QK_DIM), cos, sin)
    k = _rotary(k.reshape(b, l, RET_HEADS, RET_QK_DIM), cos, sin) * (RET_QK_DIM ** -0.5)
    v = v.reshape(b, l, RET_HEADS, RET_V_DIM)
    r = _retention_scan(q, k, v, ret_log_decay[0], True) + flip(_retention_scan(flip(q), flip(k), flip(v), ret_log_decay[1], False))
    mu = jnp.mean(r, axis=-1, keepdims=True)
    r = (r - mu) * lax.rsqrt(jnp.mean(jnp.square(r - mu), axis=-1, keepdims=True) + EPS)
    y_ret = r.reshape(b, l, RET_WIDTH) * jax.nn.silu(gate.astype(F32))
    mix = jnp.concatenate([y_ssd, y_ret], axis=-1).astype(h.dtype) @ w_out
    h = h + _rmsnorm(mix, g_mix_post)
    f = _rmsnorm(h, g_ffn_pre)
    f = (jax.nn.silu(f @ w_gate) * (f @ w_up)) @ w_down
    return h + _rmsnorm(f, g_ffn_post)


def _encode(x, meta_tokens, norm_mix_pre, norm_mix_post, norm_ffn_pre, norm_ffn_post, w_in, conv_w, conv_b,
            dt_bias, a_log, d_skip, ssd_norm, ret_log_decay, w_out, w_gate, w_up, w_down):
    b = x.shape[0]
    lead = jnp.concatenate([jnp.zeros((b, PAD, D_MODEL), x.dtype),
                            jnp.broadcast_to(meta_tokens.astype(x.dtype), (b, N_META, D_MODEL))], axis=1)
    h = jnp.concatenate([lead, x], axis=1)
    L = h.shape[1]
    valid = jnp.concatenate([jnp.zeros((PAD,), x.dtype), jnp.ones((L - PAD,), x.dtype)])
    pos = jnp.arange(L, dtype=F32)
    inv_freq = ROPE_BASE ** (-jnp.arange(0, RET_QK_DIM, 2, dtype=F32) / RET_QK_DIM)
    ang = pos[:, None] * inv_freq[None, :]
    cos, sin = jnp.cos(ang), jnp.sin(ang)
    for i in range(DEPTH):
        h = _layer(h, valid, cos, sin, norm_mix_pre[i], norm_mix_post[i], norm_ffn_pre[i], norm_ffn_post[i],
                   w_in[i], conv_w[i], conv_b[i], dt_bias[i], a_log[i], d_skip[i], ssd_norm[i], ret_log_decay[i],
                   w_out[i], w_gate[i], w_up[i], w_down[i])
    return h[:, PAD + N_META:]


def setup_inputs(seed: int = 0) -> dict:
    key = jax.random.key(seed)
    ks = jax.random.split(key, 20)
    nrm = lambda k, shape, scale: jax.random.normal(k, shape, F32) * scale
    u_dt = jax.random.uniform(ks[10], (DEPTH, 2, SSD_HEADS), F32)
    dt0 = jnp.exp(u_dt * (math.log(0.1) - math.log(0.001)) + math.log(0.001))
    base_decay = jnp.log(1.0 - 2.0 ** (-5.0 - jnp.arange(RET_HEADS, dtype=F32)))
    return {
        "x_prompt": nrm(ks[0], (BATCH, SEQ, D_MODEL), 1.0),
        "x_sample": nrm(ks[1], (DEC_BATCH, DEC_SEQ, D_MODEL), 1.0),
        "meta_tokens": nrm(ks[2], (N_META, D_MODEL), 1.0),
        "norm_mix_pre": 1.0 + nrm(ks[3], (DEPTH, D_MODEL), 0.02),
        "norm_mix_post": 1.0 + nrm(ks[4], (DEPTH, D_MODEL), 0.02),
        "norm_ffn_pre": 1.0 + nrm(ks[5], (DEPTH, D_MODEL), 0.02),
        "norm_ffn_post": 1.0 + nrm(ks[6], (DEPTH, D_MODEL), 0.02),
        "w_in": nrm(ks[7], (DEPTH, D_MODEL, N_IN), D_MODEL ** -0.5),
        "conv_w": nrm(ks[8], (DEPTH, CONV_WIDTH, CONV_CH), CONV_WIDTH ** -0.5),
        "conv_b": nrm(ks[9], (DEPTH, CONV_CH), 0.02),
        "dt_bias": dt0 + jnp.log(-jnp.expm1(-dt0)),
        "a_log": jnp.log(jax.random.uniform(ks[11], (DEPTH, 2, SSD_HEADS), F32, 1.0, 16.0)),
        "d_skip": 1.0 + nrm(ks[12], (DEPTH, SSD_HEADS), 0.02),
        "ssd_norm": 1.0 + nrm(ks[13], (DEPTH, SSD_WIDTH), 0.02),
        "ret_log_decay": base_decay * (1.0 + nrm(ks[14], (DEPTH, 2, RET_HEADS), 0.05)),
        "w_out": nrm(ks[15], (DEPTH, MIX_WIDTH, D_MODEL), MIX_WIDTH ** -0.5),
        "w_gate": nrm(ks[16], (DEPTH, D_MODEL, D_FF), D_MODEL ** -0.5),
        "w_up": nrm(ks[17], (DEPTH, D_MODEL, D_FF), D_MODEL ** -0.5),
        "w_down": nrm(ks[18], (DEPTH, D_FF, D_MODEL), D_FF ** -0.5),
    }


def reference(x_prompt, x_sample, meta_tokens, norm_mix_pre, norm_mix_post, norm_ffn_pre, norm_ffn_post,
              w_in, conv_w, conv_b, dt_bias, a_log, d_skip, ssd_norm, ret_log_decay, w_out, w_gate, w_up, w_down):
    y_prompt = _encode(x_prompt, meta_tokens, norm_mix_pre, norm_mix_post, norm_ffn_pre, norm_ffn_post, w_in,
                       conv_w, conv_b, dt_bias, a_log, d_skip, ssd_norm, ret_log_decay, w_out, w_gate, w_up, w_down)
    y_sample = _encode(x_sample, meta_tokens, norm_mix_pre, norm_mix_post, norm_ffn_pre, norm_ffn_post, w_in,
                       conv_w, conv_b, dt_bias, a_log, d_skip, ssd_norm, ret_log_decay, w_out, w_gate, w_up, w_down)
    return (y_prompt, y_sample)
```

```python
import numpy as np
from contextlib import ExitStack
import concourse.bass as bass
import concourse.mybir as mybir
from concourse.bass_utils import run_bass_kernel_spmd
import ml_dtypes

F32 = mybir.dt.float32
BF16 = mybir.dt.bfloat16
AF = mybir.ActivationFunctionType
ALU = mybir.AluOpType
AX = mybir.AxisListType

D = 1024
NIN = 5664
DFF = 2816
NMETA = 16
PADR = 112
EPS = 1e-6
OZ, OX, OB, OC, ODT, OQ, OK_, OV, OG = 0, 1024, 2048, 2304, 2560, 2592, 3104, 3616, 4640
R_XS, R_ZS, R_GS, R_V, R_YF, R_RF, R_KR, R_BT, R_QT, R_KT, R_BTT, R_CTT = (
    0, 1024, 2048, 3072, 4096, 5120, 6144, 6656, 6912, 7424, 7936, 8192)
RECW = 8448
P_GPRE, P_GPOST, P_FPRE, P_FPOST, P_SSDN, P_DSK, P_CW, P_CB, P_DTB, P_ALOG, P_LGB, P_LGP = (
    0, 1024, 2048, 3072, 4096, 5120, 6144, 6204, 6216, 6248, 6280, 6296)
PRMW = 6304
C_ID, C_U, C_VGE, C_SGT, C_REL, C_LROW, C_VALID, C_P, C_127P = 0, 128, 256, 384, 512, 640, 768, 769, 770
CSTW = 772
ND = 8
DEBUG_P1_ONLY = False
DEBUG_STOP = 99
DBG_DC = 99
DBG_A = 99
DBG_S = 99
DBG_3 = 99
DBG_YW = 0


class Tile:
    def __init__(self, h):
        self.h = h
        self.w = None
        self.r = {}

    def __getitem__(self, k):
        return self.h[k]


class Eng:
    def __init__(self, name, h, sem):
        self.name, self.h, self.sem = name, h, sem
        self.cnt = 0
        self.seen = {}


class Kern:
    def __init__(self, nc, es):
        self.nc, self.es = nc, es
        self.E = {}
        for n, a in (("pe", "tensor"), ("act", "scalar"), ("dve", "vector"), ("pool", "gpsimd"), ("sp", "sync")):
            self.E[n] = Eng(n, getattr(nc, a), es.enter_context(nc.semaphore("s_" + n)))
        self.dq = {}
        for q in ("sp", "pool"):
            sems = [es.enter_context(nc.semaphore("d_%s%d" % (q, i))) for i in range(ND)]
            self.dq[q] = dict(sems=sems, vals=[0] * ND, nxt=0)
        self.bsem = es.enter_context(nc.semaphore("bar"))
        self.bcnt = 0
        self.nid = 0

    def _wait(self, eng, toks):
        best = {}
        for key, sem, val in toks:
            if key not in best or best[key][1] < val:
                best[key] = (sem, val)
        for key, (sem, val) in best.items():
            if eng.seen.get(key, 0) >= val:
                continue
            if key == eng.name:
                if eng.name == "pe":
                    continue
                if eng.name != "pool" and eng.cnt - val >= 2:
                    continue
            eng.h.wait_ge(sem, val)
            eng.seen[key] = val

    @staticmethod
    def _deps(reads, writes):
        toks = []
        for t in reads:
            if t.w:
                toks.append(t.w)
        for t in writes:
            if t.w:
                toks.append(t.w)
            toks.extend(t.r.values())
        return toks

    def _mark(self, tok, reads, writes):
        for t in reads:
            t.r[tok[0]] = tok
        for t in writes:
            t.w = tok
            t.r = {}

    def op(self, en, fn, reads=(), writes=()):
        eng = self.E[en]
        self._wait(eng, self._deps(reads, writes))
        ins = fn(eng.h)
        eng.cnt += 1
        ins.then_inc(eng.sem, 1)
        self._mark((en, eng.sem, eng.cnt), reads, writes)

    def mm(self, items, reads=(), writes=()):
        eng = self.E["pe"]
        self._wait(eng, self._deps(reads, writes))
        ins = None
        for (o, l, r, st, sp) in items:
            ins = eng.h.matmul(o, lhsT=l, rhs=r, start=st, stop=sp)
        eng.cnt += 1
        ins.then_inc(eng.sem, 1)
        self._mark(("pe", eng.sem, eng.cnt), reads, writes)

    def tr(self, items, ident, reads=(), writes=()):
        eng = self.E["pe"]
        self._wait(eng, self._deps(list(reads) + [ident], writes))
        ins = None
        for (o, i) in items:
            ins = eng.h.transpose(out=o, in_=i, identity=ident[:])
        eng.cnt += 1
        ins.then_inc(eng.sem, 1)
        self._mark(("pe", eng.sem, eng.cnt), list(reads) + [ident], writes)

    def dma(self, q, out_ap, in_ap, reads=(), writes=()):
        eng = self.E[q]
        d = self.dq[q]
        j = d["nxt"]
        d["nxt"] = (j + 1) % ND
        key = "d_%s%d" % (q, j)
        toks = self._deps(reads, writes)
        if d["vals"][j] > 0:
            toks.append((key, d["sems"][j], d["vals"][j]))
        self._wait(eng, toks)
        ins = eng.h.dma_start(out=out_ap, in_=in_ap)
        d["vals"][j] += 16
        ins.then_inc(d["sems"][j], 16)
        self._mark((key, d["sems"][j], d["vals"][j]), reads, writes)

    def barrier(self):
        sp = self.E["sp"]
        toks = [(n, e.sem, e.cnt) for n, e in self.E.items() if e.cnt > 0]
        for q, d in self.dq.items():
            for j in range(ND):
                if d["vals"][j] > 0:
                    toks.append(("d_%s%d" % (q, j), d["sems"][j], d["vals"][j]))
        self._wait(sp, toks)
        self.bcnt += 1
        sp.h.sem_inc(self.bsem, 1)
        for n, e in self.E.items():
            if n != "sp":
                e.h.wait_ge(self.bsem, self.bcnt)
            for key, sem, val in toks:
                if e.seen.get(key, 0) < val:
                    e.seen[key] = val


def build(NSEQ, NCH, DEPTH):
    nc = bass.Bass("TRN2", target_bir_lowering=False)
    L = NCH * 128
    S = L - 128
    x_d = nc.dram_tensor("x", [NSEQ, S, D], F32, kind="ExternalInput").ap()
    meta_d = nc.dram_tensor("meta", [NMETA, D], F32, kind="ExternalInput").ap()
    win_d = nc.dram_tensor("w_in", [DEPTH, D, NIN], F32, kind="ExternalInput").ap()
    wout_d = nc.dram_tensor("w_out", [DEPTH, 2 * D, D], F32, kind="ExternalInput").ap()
    wg_d = nc.dram_tensor("w_gate", [DEPTH, D, DFF], F32, kind="ExternalInput").ap()
    wu_d = nc.dram_tensor("w_up", [DEPTH, D, DFF], F32, kind="ExternalInput").ap()
    wd_d = nc.dram_tensor("w_down", [DEPTH, DFF, D], F32, kind="ExternalInput").ap()
    prm_d = nc.dram_tensor("prm", [DEPTH, 128, PRMW], F32, kind="ExternalInput").ap()
    cst_d = nc.dram_tensor("cst", [128, CSTW], F32, kind="ExternalInput").ap()
    rope_d = nc.dram_tensor("rope", [L, 256], F32, kind="ExternalInput").ap()
    y_d = nc.dram_tensor("y", [NSEQ, S, D], F32, kind="ExternalOutput").ap()
    hres_d = nc.dram_tensor("hres", [NSEQ, L, D], F32, kind="Internal").ap()
    hmid_d = nc.dram_tensor("hmid", [NSEQ, L, D], F32, kind="Internal").ap()
    rec_d = nc.dram_tensor("rec", [NSEQ, NCH, 128, RECW], BF16, kind="Internal").ap()
    dts_d = nc.dram_tensor("dts", [NSEQ, NCH, 128, 32], F32, kind="Internal").ap()

    with ExitStack() as es0:
        K = Kern(nc, es0)
        op, mm, tr, dma = K.op, K.mm, K.tr, K.dma

        def mk(es, name, shape, dt):
            K.nid += 1
            return Tile(es.enter_context(nc.sbuf_tensor("%s_%d" % (name, K.nid), shape, dt)))

        cst = mk(es0, "cst", [128, CSTW], F32)
        identb = mk(es0, "identb", [128, 128], BF16)
        onesb = mk(es0, "onesb", [128, 128], BF16)
        Ub = mk(es0, "Ub", [128, 128], BF16)
        Vb = mk(es0, "Vb", [128, 128], BF16)
        MNf = mk(es0, "MNf", [128, 4, 128], BF16)
        MNb = mk(es0, "MNb", [128, 4, 128], BF16)
        est = ExitStack()
        zt = mk(est, "zt", [128, D], F32)
        banks = [Tile(es0.enter_context(nc.psum_tensor("bank%d" % i, [128, 512], F32))) for i in range(8)]
        bstate = dict(i=0)

        def bank():
            b = banks[bstate["i"] % 8]
            bstate["i"] += 1
            return b

        dma("sp", cst[:], cst_d[:, :], writes=[cst])
        op("dve", lambda e: e.tensor_copy(out=identb[:], in_=cst[:, C_ID:C_ID + 128]), [cst], [identb])
        op("dve", lambda e: e.tensor_copy(out=Ub[:], in_=cst[:, C_U:C_U + 128]), [cst], [Ub])
        op("dve", lambda e: e.tensor_copy(out=Vb[:], in_=cst[:, C_VGE:C_VGE + 128]), [cst], [Vb])
        op("pool", lambda e: e.memset(onesb[:], 1.0), [], [onesb])
        op("dve", lambda e: e.tensor_scalar(out=MNf[:], in0=cst[:, C_SGT:C_SGT + 128].unsqueeze(1).broadcast_to([128, 4, 128]),
                                            scalar1=-30000.0, scalar2=None, op0=ALU.mult), [cst], [MNf])
        op("dve", lambda e: e.tensor_scalar(out=MNb[:], in0=cst[:, C_U:C_U + 128].unsqueeze(1).broadcast_to([128, 4, 128]),
                                            scalar1=-30000.0, scalar2=None, op0=ALU.mult), [cst], [MNb])
        op("pool", lambda e: e.memset(zt[:], 0.0), [], [zt])
        for s in range(NSEQ):
            dma("sp", hres_d[s, 0:PADR, :], zt[0:PADR, :], reads=[zt])
            dma("sp", hres_d[s, PADR:128, :], meta_d[:, :])
        K.barrier()
        est.close()

        def h_src(layer, s, c):
            if layer == 0 and c > 0:
                return x_d[s, (c - 1) * 128:c * 128, :]
            return hres_d[s, c * 128:(c + 1) * 128, :]

        def load_w(es, name, src2d, kt, ncols):
            w = mk(es, name, [128, kt, ncols], BF16)
            for k in range(kt):
                c0 = 0
                while c0 < ncols:
                    cw = min(2048, ncols - c0)
                    dma("pool", w[:, k, c0:c0 + cw], src2d[k * 128:(k + 1) * 128, c0:c0 + cw], writes=[w])
                    c0 += cw
            return w

        def rms_scale(ss, tmp):
            op("dve", lambda e: e.tensor_scalar(out=tmp[:], in0=ss[:], scalar1=1.0 / D, scalar2=EPS,
                                                op0=ALU.mult, op1=ALU.add), [ss], [tmp])
            op("act", lambda e: e.sqrt(out=tmp[:], in_=tmp[:]), [tmp], [tmp])
            op("dve", lambda e: e.reciprocal(out=ss[:], in_=tmp[:]), [tmp], [ss])

        def dir_consts(es, prm, PO, d):
            lgB = prm[:, PO + P_LGB + 8 * d:PO + P_LGB + 8 * d + 8]
            lgP = prm[:, PO + P_LGP + 4 * d:PO + P_LGP + 4 * d + 4]
            dmT = mk(es, "dmT", [128, 8, 128], F32)
            xiT = mk(es, "xiT", [128, 4, 128], F32)
            zeta = mk(es, "zeta", [128, 8], F32)
            gch = mk(es, "gch", [128, 4], F32)
            negA = mk(es, "negA", [128, 16], F32)
            sc = mk(es, "sc", [128, 16], F32)
            if d == 0:
                op("dve", lambda e: e.tensor_copy(out=sc[:, 0:8], in_=lgB), [prm], [sc])
                op("dve", lambda e: e.tensor_copy(out=sc[:, 8:12], in_=lgP), [prm], [sc])
                op("dve", lambda e: e.tensor_copy(out=sc[:, 12:16], in_=lgP), [prm], [sc])
            else:
                op("dve", lambda e: e.tensor_scalar(out=sc[:, 0:8], in0=lgB, scalar1=-1.0, scalar2=None,
                                                    op0=ALU.mult), [prm], [sc])
                op("dve", lambda e: e.tensor_scalar(out=sc[:, 8:12], in0=lgP, scalar1=-1.0, scalar2=None,
                                                    op0=ALU.mult), [prm], [sc])
                op("dve", lambda e: e.tensor_scalar(out=sc[:, 12:16], in0=lgP, scalar1=128.0, scalar2=None,
                                                    op0=ALU.mult), [prm], [sc])
            msk = cst[:, C_U:C_U + 128] if d == 0 else cst[:, C_SGT:C_SGT + 128]
            if DBG_DC >= 1:
                for i in range(8):
                    h = 2 * (i % 4) + i // 4
                    op("act", lambda e, h=h, i=i: e.activation(out=dmT[:, i, :], in_=cst[:, C_REL:C_REL + 128], func=AF.Exp,
                                                               scale=sc[:, h:h + 1]), [cst, sc], [dmT])
            if DBG_DC >= 2:
                op("dve", lambda e: e.tensor_tensor(out=dmT[:], in0=dmT[:],
                                                    in1=msk.unsqueeze(1).broadcast_to([128, 8, 128]), op=ALU.mult),
                   [dmT, cst], [dmT])
            if DBG_DC >= 3:
                for j in range(4):
                    op("act", lambda e, j=j: e.activation(out=xiT[:, j, :], in_=cst[:, C_LROW:C_LROW + 128], func=AF.Exp,
                                                          scale=sc[:, 8 + j:9 + j], bias=sc[:, 12 + j:13 + j]),
                       [cst, sc], [xiT])
            zc = C_127P if d == 0 else C_P
            if DBG_DC >= 4:
                op("act", lambda e: e.activation(out=zeta[:], in_=lgB, func=AF.Exp, scale=cst[:, zc:zc + 1]),
                   [prm, cst], [zeta])
            if DBG_DC >= 5:
                op("act", lambda e: e.activation(out=gch[:], in_=lgP, func=AF.Exp, scale=128.0), [prm], [gch])
            if DBG_DC >= 6:
                op("act", lambda e: e.activation(out=negA[:], in_=prm[:, PO + P_ALOG + 16 * d:PO + P_ALOG + 16 * d + 16],
                                                 func=AF.Exp), [prm], [negA])
                op("dve", lambda e: e.tensor_scalar(out=negA[:], in0=negA[:], scalar1=-1.0, scalar2=None, op0=ALU.mult),
                   [negA], [negA])
            return dmT, xiT, zeta, gch, negA

        def scan_chunk(T, d, rec, dt, on_y, on_o):
            (dmT, xiT, zeta, gch, negA) = T["dc"]
            st, stb, Rs, Rb = T["state"], T["stateb"], T["R"], T["Rb"]
            dAb, G, nacs, cbm, E, eaT, MT, yin, xdt, xw, decB = (T[k] for k in (
                "dAb", "G", "nacs", "cbm", "E", "eaT", "MT", "yin", "xdt", "xw", "decB"))
            sTm, qxT, kz = T["sTm"], T["qxT"], T["kz"]
            Mb = Ub if d == 0 else Vb
            MN = MNf if d == 0 else MNb
            wend = T["wend"]
            mcol = C_U if d == 0 else C_SGT
            lsel = 127 if d == 0 else 0
            op("dve", lambda e: e.tensor_tensor(out=dAb[:], in0=dt[:, 16 * d:16 * d + 16], in1=negA[:], op=ALU.mult),
               [dt, negA], [dAb])
            op("pool", lambda e: e.tensor_tensor(out=G[:], in0=dAb[:].unsqueeze(2).broadcast_to([128, 16, 128]),
                                                 in1=Mb[:].unsqueeze(1).broadcast_to([128, 16, 128]), op=ALU.mult),
               [dAb, Mb], [G])
            bk = bank()
            mm([(bk[:, 0:16], Mb[:], dAb[:], True, True), (bk[:, 16:32], onesb[:], dAb[:], True, True)],
               [Mb, onesb, dAb], [bk])
            op("dve", lambda e, bk=bk: e.tensor_scalar(out=nacs[:], in0=bk[:, 0:16], scalar1=-1.0, scalar2=None,
                                                       op0=ALU.mult), [bk], [nacs])
            op("dve", lambda e, bk=bk: e.tensor_tensor(out=wend[:], in0=bk[:, 16:32], in1=nacs[:], op=ALU.add),
               [bk, nacs], [wend])
            op("act", lambda e, bk=bk: e.activation(out=eaT[:], in_=bk[:, 0:16], func=AF.Exp), [bk], [eaT])
            op("act", lambda e, bk=bk: e.activation(out=decB[:], in_=bk[:, 16:32], func=AF.Exp), [bk], [decB])
            op("act", lambda e: e.activation(out=wend[:], in_=wend[:], func=AF.Exp), [wend], [wend])
            if DBG_S < 2:
                return
            op("pool", lambda e: e.tensor_tensor(
                out=xdt[:].rearrange("p (h d) -> p h d", d=64),
                in0=rec[:, R_XS:R_XS + 1024].rearrange("p (h d) -> p h d", d=64),
                in1=dt[:, 16 * d:16 * d + 16].unsqueeze(2).broadcast_to([128, 16, 64]), op=ALU.mult),
               [rec, dt], [xdt])
            bk = bank()
            mm([(bk[:, g * 128:(g + 1) * 128], rec[:, R_BTT + g * 128:R_BTT + (g + 1) * 128],
                 rec[:, R_CTT + g * 128:R_CTT + (g + 1) * 128], True, True) for g in range(2)], [rec], [bk])
            op("dve", lambda e: e.tensor_tensor(
                out=cbm[:], in0=bk[:, 0:256].rearrange("p (g l) -> p g l", g=2),
                in1=cst[:, mcol:mcol + 128].unsqueeze(1).broadcast_to([128, 2, 128]), op=ALU.mult),
               [bk, cst], [cbm])
            if DBG_S < 3:
                return
            for g in range(2):
                for hf in range(2):
                    h0 = g * 8 + hf * 4
                    bk = bank()
                    mm([(bk[:], onesb[:], G[:, h0:h0 + 4, :].rearrange("p h l -> p (h l)"), True, False),
                        (bk[:], identb[:], MN[:].rearrange("p h l -> p (h l)"), False, True)],
                       [onesb, G, identb, MN], [bk])
                    for hh in range(4):
                        op("act", lambda e, hh=hh, bk=bk, h0=h0: e.activation(
                            out=E[:, h0 + hh, :], in_=bk[:, hh * 128:(hh + 1) * 128], func=AF.Exp,
                            bias=nacs[:, h0 + hh:h0 + hh + 1]), [bk, nacs], [E])
                    op("dve", lambda e, h0=h0, g=g: e.tensor_tensor(
                        out=MT[:, h0:h0 + 4, :], in0=E[:, h0:h0 + 4, :],
                        in1=cbm[:, g, :].unsqueeze(1).broadcast_to([128, 4, 128]), op=ALU.mult),
                       [E, cbm], [MT])
            if DBG_S < 4:
                return
            op("dve", lambda e: e.tensor_tensor(
                out=xw[:].rearrange("p (h d) -> p h d", d=64), in0=xdt[:].rearrange("p (h d) -> p h d", d=64),
                in1=wend[:].unsqueeze(2).broadcast_to([128, 16, 64]), op=ALU.mult), [xdt, wend], [xw])
            ybanks = []
            for g in range(2):
                bk = bank()
                items = []
                for h in range(8):
                    hg = g * 8 + h
                    items.append((bk[:, h * 64:(h + 1) * 64], MT[:, hg, :], xdt[:, hg * 64:(hg + 1) * 64], True, True))
                mm(items, [MT, xdt], [bk])
                bo = bank()
                mm([(bo[:], rec[:, R_CTT + g * 128:R_CTT + (g + 1) * 128], stb[:, g, :], True, True)], [rec, stb], [bo])
                yi = yin[g]
                op("act", lambda e, bo=bo, yi=yi: e.copy(out=yi[:], in_=bo[:]), [bo], [yi])
                op("dve", lambda e, yi=yi, g=g: e.tensor_tensor(
                    out=yi[:].rearrange("p (h d) -> p h d", d=64), in0=yi[:].rearrange("p (h d) -> p h d", d=64),
                    in1=eaT[:, g * 8:(g + 1) * 8].unsqueeze(2).broadcast_to([128, 8, 64]), op=ALU.mult), [yi, eaT], [yi])
                on_y(g, bk, yi)
            for g in range(2):
                bk = bank()
                mm([(bk[:], rec[:, R_BT + g * 128:R_BT + (g + 1) * 128], xw[:, g * 512:(g + 1) * 512], True, True)],
                   [rec, xw], [bk])
                op("dve", lambda e: e.tensor_tensor(
                    out=st[:, g, :].rearrange("p (h d) -> p h d", d=64),
                    in0=st[:, g, :].rearrange("p (h d) -> p h d", d=64),
                    in1=decB[:, g * 8:(g + 1) * 8].unsqueeze(2).broadcast_to([128, 8, 64]), op=ALU.mult),
                   [st, decB], [st])
                op("dve", lambda e: e.tensor_tensor(out=st[:, g, :], in0=st[:, g, :], in1=bk[:], op=ALU.add),
                   [st, bk], [st])
            op("act", lambda e: e.copy(out=stb[:].rearrange("p g n -> p (g n)"),
                                       in_=st[:].rearrange("p g n -> p (g n)")), [st], [stb])
            if DBG_S < 5:
                return
            op("pool", lambda e: e.tensor_tensor(
                out=qxT[:], in0=rec[:, R_QT:R_QT + 512].rearrange("p (j l) -> p j l", j=4), in1=xiT[:], op=ALU.mult),
               [rec, xiT], [qxT])
            op("pool", lambda e: e.tensor_tensor(
                out=kz[:].rearrange("p (h d) -> p h d", d=64),
                in0=rec[:, R_KR:R_KR + 512].rearrange("p (h d) -> p h d", d=64),
                in1=zeta[:].unsqueeze(2).broadcast_to([128, 8, 64]), op=ALU.mult), [rec, zeta], [kz])
            obanks = []
            for b in range(2):
                bk = bank()
                items = []
                for j in range(4):
                    items.append((bk[:, j * 128:(j + 1) * 128],
                                  rec[b * 64:(b + 1) * 64, R_KT + j * 128:R_KT + (j + 1) * 128],
                                  rec[b * 64:(b + 1) * 64, R_QT + j * 128:R_QT + (j + 1) * 128], True, True))
                mm(items, [rec], [bk])
                op("dve", lambda e, b=b, bk=bk: e.tensor_tensor(
                    out=sTm[:, b * 4:b * 4 + 4, :], in0=bk[:].rearrange("p (h l) -> p h l", h=4),
                    in1=dmT[:, b * 4:b * 4 + 4, :], op=ALU.mult), [bk, dmT], [sTm])
            for b in range(2):
                bk = bank()
                items = []
                for j in range(4):
                    h = 2 * j + b
                    items.append((bk[:, j * 128:(j + 1) * 128], sTm[:, b * 4 + j, :],
                                  rec[:, R_V + h * 128:R_V + (h + 1) * 128], True, False))
                    items.append((bk[:, j * 128:(j + 1) * 128], qxT[b * 64:(b + 1) * 64, j, :],
                                  Rb[b * 64:(b + 1) * 64, j, :], False, True))
                mm(items, [sTm, rec, qxT, Rb], [bk])
                on_o(b, bk)
            if DBG_S < 6:
                return
            bk = bank()
            items = []
            for h in range(8):
                j, b = h // 2, h % 2
                items.append((bk[b * 64:(b + 1) * 64, j * 128:(j + 1) * 128], kz[:, h * 64:(h + 1) * 64],
                              rec[:, R_V + h * 128:R_V + (h + 1) * 128], True, True))
            mm(items, [kz, rec], [bk])
            op("dve", lambda e: e.tensor_tensor(out=Rs[:], in0=Rs[:],
                                                in1=gch[:].unsqueeze(2).broadcast_to([128, 4, 128]), op=ALU.mult),
               [Rs, gch], [Rs])
            op("dve", lambda e: e.tensor_tensor(out=Rs[:].rearrange("p j e -> p (j e)"),
                                                in0=Rs[:].rearrange("p j e -> p (j e)"), in1=bk[:], op=ALU.add),
               [Rs, bk], [Rs])
            op("act", lambda e: e.copy(out=Rb[:].rearrange("p j e -> p (j e)"),
                                       in_=Rs[:].rearrange("p j e -> p (j e)")), [Rs], [Rb])

        def scan_tiles(es):
            T = {}
            T["state"] = mk(es, "state", [128, 2, 512], F32)
            T["stateb"] = mk(es, "stateb", [128, 2, 512], BF16)
            T["R"] = mk(es, "R", [128, 4, 128], F32)
            T["Rb"] = mk(es, "Rb", [128, 4, 128], BF16)
            T["dAb"] = mk(es, "dAb", [128, 16], BF16)
            T["G"] = mk(es, "G", [128, 16, 128], BF16)
            T["nacs"] = mk(es, "nacs", [128, 16], F32)
            T["cbm"] = mk(es, "cbm", [128, 2, 128], F32)
            T["E"] = mk(es, "E", [128, 16, 128], BF16)
            T["eaT"] = mk(es, "eaT", [128, 16], F32)
            T["yin"] = [mk(es, "yin", [128, 512], F32) for _ in range(2)]
            T["MT"] = mk(es, "MT", [128, 16, 128], BF16)
            T["xdt"] = mk(es, "xdt", [128, 1024], BF16)
            T["xw"] = mk(es, "xw", [128, 1024], BF16)
            T["decB"] = mk(es, "decB", [128, 16], F32)
            T["wend"] = mk(es, "wend", [128, 16], F32)
            T["sTm"] = mk(es, "sTm", [128, 8, 128], BF16)
            T["qxT"] = mk(es, "qxT", [128, 4, 128], BF16)
            T["kz"] = mk(es, "kz", [128, 512], BF16)
            return T

        def reset_state(T):
            op("pool", lambda e: e.memset(T["state"][:], 0.0), [], [T["state"]])
            op("pool", lambda e: e.memset(T["stateb"][:], 0.0), [], [T["stateb"]])
            op("pool", lambda e: e.memset(T["R"][:], 0.0), [], [T["R"]])
            op("pool", lambda e: e.memset(T["Rb"][:], 0.0), [], [T["Rb"]])

        def phase1(layer):
            with ExitStack() as es:
                prm = mk(es, "prm1", [128, PRMW - P_CW + 1024], F32)
                PO = 1024 - P_CW
                dma("sp", prm[:, 0:1024], prm_d[layer, :, P_GPRE:P_GPRE + 1024], writes=[prm])
                dma("sp", prm[:, 1024:], prm_d[layer, :, P_CW:PRMW], writes=[prm])

                if DEBUG_STOP == 8:
                    K.barrier()
                    return
                w_in = load_w(es, "w_in", win_d[layer], 8, NIN)
                if DEBUG_STOP == 9:
                    K.barrier()
                    return
                T = scan_tiles(es)
                T["dc"] = dir_consts(es, prm, PO, 0)
                if DEBUG_STOP == 10:
                    K.barrier()
                    return
                hb = [mk(es, "hb", [128, D], F32) for _ in range(2)]
                junk = mk(es, "junk", [128, D], F32)
                ss = [mk(es, "ss", [128, 1], F32) for _ in range(2)]
                tmp1 = [mk(es, "tmp1", [128, 1], F32) for _ in range(2)]
                u_bf = mk(es, "u_bf", [128, D], BF16)
                ext = [mk(es, "ext", [128, 8, 192], BF16) for _ in range(3)]
                acc = [mk(es, "acc", [128, 3, 128], F32) for _ in range(2)]
                xT = mk(es, "xT", [128, 8, 128], BF16)
                recs1 = [mk(es, "rec", [128, RECW], BF16) for _ in range(2)]
                ropet = [mk(es, "ropet", [128, 256], F32) for _ in range(2)]
                dtt = mk(es, "dtt", [128, 32], F32)
                dtmp = mk(es, "dtmp", [128, 32], F32)
                q_rot = mk(es, "q_rot", [128, 512], BF16)
                pT = Tile(None)

                def stageA(s, c):
                    b = hb[c % 2]
                    s1, t1 = ss[c % 2], tmp1[c % 2]
                    dma("sp", b[:], h_src(layer, s, c), writes=[b])
                    op("act", lambda e: e.activation(out=junk[:], in_=b[:], func=AF.Square, accum_out=s1[:]),
                       [b], [junk, s1])
                    rms_scale(s1, t1)
                    if c == 0:
                        op("dve", lambda e: e.tensor_tensor(out=s1[:], in0=s1[:], in1=cst[:, C_VALID:C_VALID + 1],
                                                            op=ALU.mult), [s1, cst], [s1])
                    if DBG_A < 2:
                        return
                    op("dve", lambda e: e.scalar_tensor_tensor(out=u_bf[:], in0=b[:], scalar=s1[:, 0:1],
                                                               in1=prm[:, 0:1024], op0=ALU.mult, op1=ALU.mult),
                       [b, s1, prm], [u_bf])
                    if DBG_A < 3:
                        return
                    bk = bank()
                    bkb = bk[:].bitcast(BF16).rearrange("p (k t) -> p k t", k=8)
                    tr([(bkb[:, k, :], u_bf[:, k * 128:(k + 1) * 128]) for k in range(8)], identb, [u_bf], [bk])
                    if DBG_A < 4:
                        return
                    e_c = ext[c % 3]
                    op("act", lambda e: e.copy(out=e_c[:, :, 32:160], in_=bkb), [bk], [e_c])
                    if DBG_A < 5:
                        return
                    if c > 0:
                        e_p = ext[(c - 1) % 3]
                        op("pool", lambda e: e.tensor_copy(out=e_p[:, :, 160:192], in_=e_c[:, :, 32:64]), [e_c], [e_p])
                        op("pool", lambda e: e.tensor_copy(out=e_c[:, :, 0:32], in_=e_p[:, :, 128:160]), [e_p], [e_c])
                    else:
                        op("pool", lambda e: e.memset(e_c[:, :, 0:32], 0.0), [], [e_c])
                    if c == NCH - 1:
                        op("pool", lambda e: e.memset(e_c[:, :, 160:192], 0.0), [], [e_c])

                def proj_tok(e_c, c0, n):
                    bk = bank()
                    mm([(bk[:, 0:n], e_c[:, k, 32:160], w_in[:, k, c0:c0 + n], k == 0, k == 7) for k in range(8)],
                       [e_c, w_in], [bk])
                    return bk

                def stageB(s, c):
                    rec = recs1[c % 2]
                    e_c = ext[c % 3]
                    rp = ropet[c % 2]
                    dma("sp", rp[:], rope_d[c * 128:(c + 1) * 128, :], writes=[rp])
                    bk = proj_tok(e_c, ODT, 32)
                    op("dve", lambda e: e.tensor_tensor(out=dtmp[:], in0=bk[:, 0:32],
                                                        in1=prm[:, PO + P_DTB:PO + P_DTB + 32], op=ALU.add),
                       [bk, prm], [dtmp])
                    op("act", lambda e: e.activation(out=dtmp[:], in_=dtmp[:], func=AF.Exp), [dtmp], [dtmp])
                    op("act", lambda e: e.activation(out=dtt[:], in_=dtmp[:], func=AF.Ln, bias=1.0), [dtmp], [dtt])
                    if c == 0:
                        op("dve", lambda e: e.tensor_scalar(out=dtt[:], in0=dtt[:], scalar1=cst[:, C_VALID:C_VALID + 1],
                                                            scalar2=None, op0=ALU.mult), [dtt, cst], [dtt])
                    dma("sp", dts_d[s, c], dtt[:], reads=[dtt])
                    for (c0, r0, fn) in ((OZ, R_ZS, AF.Silu), (OZ + 512, R_ZS + 512, AF.Silu),
                                         (OG, R_GS, AF.Silu), (OG + 512, R_GS + 512, AF.Silu),
                                         (OV, R_V, AF.Copy), (OV + 512, R_V + 512, AF.Copy)):
                        bk = proj_tok(e_c, c0, 512)
                        op("act", lambda e, r0=r0, fn=fn, bk=bk: e.activation(out=rec[:, r0:r0 + 512], in_=bk[:], func=fn),
                           [bk], [rec])
                    for qi, (c0, dst, r0) in enumerate(((OQ, q_rot, 0), (OK_, rec, R_KR))):
                        bk = proj_tok(e_c, c0, 512)
                        b3 = bk[:].rearrange("p (h d) -> p h d", d=64)
                        cc = rp[:, qi * 128:qi * 128 + 64]
                        sg = rp[:, qi * 128 + 64:qi * 128 + 128]
                        A3 = junk[:, 0:512].rearrange("p (h d) -> p h d", d=64)
                        B3 = junk[:, 512:1024].rearrange("p (h d) -> p h d", d=64)
                        op("dve", lambda e: e.tensor_tensor(out=A3, in0=b3, in1=cc.unsqueeze(1).broadcast_to([128, 8, 64]),
                                                            op=ALU.mult), [bk, rp], [junk])
                        op("dve", lambda e: e.tensor_tensor(out=B3[:, :, 0:32], in0=b3[:, :, 32:64],
                                                            in1=sg[:, 0:32].unsqueeze(1).broadcast_to([128, 8, 32]),
                                                            op=ALU.mult), [bk, rp], [junk])
                        op("dve", lambda e: e.tensor_tensor(out=B3[:, :, 32:64], in0=b3[:, :, 0:32],
                                                            in1=sg[:, 32:64].unsqueeze(1).broadcast_to([128, 8, 32]),
                                                            op=ALU.mult), [bk, rp], [junk])
                        op("pool", lambda e, dst=dst, r0=r0: e.tensor_tensor(out=dst[:, r0:r0 + 512], in0=junk[:, 0:512],
                                                                             in1=junk[:, 512:1024], op=ALU.add),
                           [junk], [dst])
                    bk = bank()
                    bkb = bk[:].bitcast(BF16).rearrange("p (k t) -> p k t", k=8)
                    tr([(bkb[:, j, :], q_rot[:, j * 128:(j + 1) * 128]) for j in range(4)] +
                       [(bkb[:, 4 + j, :], rec[:, R_KR + j * 128:R_KR + (j + 1) * 128]) for j in range(4)],
                       identb, [q_rot, rec], [bk])
                    op("act", lambda e: e.copy(out=rec[:, R_QT:R_QT + 1024], in_=bk[:].bitcast(BF16)), [bk], [rec])
                    for jb in range(4):
                        bk = bank()
                        b3 = bk[:, 0:396].rearrange("p (j t) -> p j t", j=3)
                        items = []
                        for jj in range(3):
                            j = jb * 3 + jj
                            for k in range(8):
                                items.append((b3[:, jj, :], w_in[:, k, OX + j * 128:OX + (j + 1) * 128],
                                              e_c[:, k, 30:162], k == 0, k == 7))
                        mm(items, [e_c, w_in], [bk])
                        a = acc[jb % 2]
                        for jj in range(3):
                            j = jb * 3 + jj
                            cw = PO + P_CW + j * 5
                            op("dve", lambda e, jj=jj, cw=cw: e.tensor_scalar(
                                out=a[:, jj, :], in0=b3[:, jj, 0:128], scalar1=prm[:, cw:cw + 1], scalar2=None,
                                op0=ALU.mult), [bk, prm], [a])
                            for t in range(1, 5):
                                op("dve", lambda e, jj=jj, cw=cw, t=t: e.scalar_tensor_tensor(
                                    out=a[:, jj, :], in0=b3[:, jj, t:t + 128], scalar=prm[:, cw + t:cw + t + 1],
                                    in1=a[:, jj, :], op0=ALU.mult, op1=ALU.add), [bk, prm, a], [a])
                            cb = PO + P_CB + j
                            if j < 8:
                                dst_t, dst_ap = xT, xT[:, j, :]
                            elif j < 10:
                                dst_t, dst_ap = rec, rec[:, R_BTT + (j - 8) * 128:R_BTT + (j - 7) * 128]
                            else:
                                dst_t, dst_ap = rec, rec[:, R_CTT + (j - 10) * 128:R_CTT + (j - 9) * 128]
                            op("act", lambda e, jj=jj, cb=cb, dst_ap=dst_ap: e.activation(
                                out=dst_ap, in_=a[:, jj, :], func=AF.Silu, bias=prm[:, cb:cb + 1]), [a, prm], [dst_t])
                    bk = bank()
                    bkb = bk[:].bitcast(BF16).rearrange("p (k t) -> p k t", k=8)
                    tr([(bkb[:, j, :], xT[:, j, :]) for j in range(8)], identb, [xT], [bk])
                    op("act", lambda e: e.copy(out=rec[:, R_XS:R_XS + 1024], in_=bk[:].bitcast(BF16)), [bk], [rec])
                    bk = bank()
                    bkb = bk[:].bitcast(BF16)
                    tr([(bkb[:, g * 128:(g + 1) * 128], rec[:, R_BTT + g * 128:R_BTT + (g + 1) * 128]) for g in range(2)],
                       identb, [rec], [bk])
                    op("dve", lambda e: e.tensor_copy(out=rec[:, R_BT:R_BT + 256], in_=bkb[:, 0:256]), [bk], [rec])
                    if DEBUG_STOP == 12:
                        dma("sp", rec_d[s, c], rec[:], reads=[rec])
                        return

                    def on_y(g, bk, yi):
                        op("dve", lambda e: e.tensor_tensor(out=rec[:, R_YF + g * 512:R_YF + (g + 1) * 512],
                                                            in0=bk[:], in1=yi[:], op=ALU.add), [bk, yi], [rec])

                    def on_o(b_, bk):
                        op("act", lambda e: e.copy(out=rec[:, R_RF + b_ * 512:R_RF + (b_ + 1) * 512], in_=bk[:]),
                           [bk], [rec])
                    scan_chunk(T, 0, rec, dtt, on_y, on_o)
                    dma("sp", rec_d[s, c], rec[:], reads=[rec])

                for s in range(NSEQ):
                    reset_state(T)
                    stageA(s, 0)
                    for c in range(NCH):
                        if c + 1 < NCH:
                            stageA(s, c + 1)
                        if DEBUG_STOP != 11:
                            stageB(s, c)
                K.barrier()

        def phase2(layer):
            with ExitStack() as es:
                prm = mk(es, "prm2", [128, 3 * 1024 + PRMW - P_CW], F32)
                PO = 3072 - P_CW
                dma("sp", prm[:, 0:1024], prm_d[layer, :, P_GPOST:P_GPOST + 1024], writes=[prm])
                dma("sp", prm[:, 1024:3072], prm_d[layer, :, P_SSDN:P_SSDN + 2048], writes=[prm])
                dma("sp", prm[:, 3072:], prm_d[layer, :, P_CW:PRMW], writes=[prm])
                w_out = load_w(es, "w_out", wout_d[layer], 16, D)
                T = scan_tiles(es)
                T["dc"] = dir_consts(es, prm, PO, 1)
                recs = [mk(es, "rec2", [128, RECW], BF16) for _ in range(2)]
                dts = [mk(es, "dt2", [128, 32], F32) for _ in range(2)]
                hb = [mk(es, "hb2", [128, D], F32) for _ in range(2)]
                yt = mk(es, "yt", [128, D], F32)
                yt2 = mk(es, "yt2", [128, D], F32)
                rt = mk(es, "rt", [128, D], F32)
                junk = mk(es, "junk2", [128, D], F32)
                ycat = mk(es, "ycat", [128, 2 * D], BF16)
                ycT = mk(es, "ycT", [128, 16, 128], BF16)
                st8 = mk(es, "st8", [128, 8], F32)
                sq8 = mk(es, "sq8", [128, 8], F32)
                mu8 = mk(es, "mu8", [128, 8], F32)
                ss2 = mk(es, "ss2", [128, 2], F32)
                tm2 = mk(es, "tm2", [128, 2], F32)
                ssm = mk(es, "ssm", [128, 2], F32)
                ss1 = mk(es, "ss1m", [128, 1], F32)
                tm1 = mk(es, "tm1m", [128, 1], F32)
                hout = [mk(es, "hout", [128, D], F32) for _ in range(2)]

                def loads(s, c):
                    r = recs[c % 2]
                    dma("sp", r[:], rec_d[s, c], writes=[r])
                    dma("sp", dts[c % 2][:], dts_d[s, c], writes=[dts[c % 2]])
                    dma("sp", hb[c % 2][:], h_src(layer, s, c), writes=[hb[c % 2]])

                def chunk(s, c):
                    rec, dt, hbt = recs[c % 2], dts[c % 2], hb[c % 2]
                    def on_y(g, bk, yi):
                        sl = slice(g * 512, (g + 1) * 512)
                        op("dve", lambda e: e.tensor_tensor(out=yt[:, sl], in0=bk[:], in1=yi[:], op=ALU.add), [bk, yi], [yt])
                        op("pool", lambda e: e.tensor_tensor(out=yt[:, sl], in0=yt[:, sl],
                                                             in1=rec[:, R_YF + g * 512:R_YF + (g + 1) * 512],
                                                             op=ALU.add), [yt, rec], [yt])
                        op("pool", lambda e: e.tensor_tensor(out=yt2[:, sl], in0=rec[:, R_XS + g * 512:R_XS + (g + 1) * 512],
                                                             in1=prm[:, 2048 + g * 512:2048 + (g + 1) * 512],
                                                             op=ALU.mult), [rec, prm], [yt2])
                        op("dve", lambda e: e.tensor_tensor(out=yt[:, sl], in0=yt[:, sl], in1=yt2[:, sl], op=ALU.add),
                           [yt, yt2], [yt])
                        op("dve", lambda e: e.tensor_tensor(out=yt[:, sl], in0=yt[:, sl],
                                                            in1=rec[:, R_ZS + g * 512:R_ZS + (g + 1) * 512],
                                                            op=ALU.mult), [yt, rec], [yt])
                        op("act", lambda e: e.activation(out=junk[:, sl], in_=yt[:, sl], func=AF.Square,
                                                         accum_out=ss2[:, g:g + 1]), [yt], [junk, ss2])

                    def on_o(b_, bk):
                        sl = slice(b_ * 512, (b_ + 1) * 512)
                        op("dve", lambda e: e.tensor_tensor(out=rt[:, sl], in0=bk[:],
                                                            in1=rec[:, R_RF + b_ * 512:R_RF + (b_ + 1) * 512],
                                                            op=ALU.add), [bk, rec], [rt])
                    scan_chunk(T, 1, rec, dt, on_y, on_o)
                    op("dve", lambda e: e.tensor_scalar(out=tm2[:], in0=ss2[:], scalar1=1.0 / 512, scalar2=EPS,
                                                        op0=ALU.mult, op1=ALU.add), [ss2], [tm2])
                    op("act", lambda e: e.sqrt(out=tm2[:], in_=tm2[:]), [tm2], [tm2])
                    op("dve", lambda e: e.reciprocal(out=ss2[:], in_=tm2[:]), [tm2], [ss2])
                    for g in range(2):
                        sl = slice(g * 512, (g + 1) * 512)
                        op("dve", lambda e, sl=sl, g=g: e.scalar_tensor_tensor(
                            out=ycat[:, sl], in0=yt[:, sl], scalar=ss2[:, g:g + 1],
                            in1=prm[:, 1024 + g * 512:1024 + (g + 1) * 512], op0=ALU.mult, op1=ALU.mult),
                           [yt, ss2, prm], [ycat])
                    r3 = rt[:].rearrange("p (h e) -> p h e", e=128)
                    j3 = junk[:].rearrange("p (h e) -> p h e", e=128)
                    op("dve", lambda e: e.tensor_reduce(out=st8[:], in_=r3, axis=AX.X, op=ALU.add), [rt], [st8])
                    op("pool", lambda e: e.tensor_tensor(out=junk[:], in0=rt[:], in1=rt[:], op=ALU.mult), [rt], [junk])
                    op("dve", lambda e: e.tensor_reduce(out=sq8[:], in_=j3, axis=AX.X, op=ALU.add), [junk], [sq8])
                    op("dve", lambda e: e.tensor_scalar(out=mu8[:], in0=st8[:], scalar1=1.0 / 128, scalar2=None,
                                                        op0=ALU.mult), [st8], [mu8])
                    op("dve", lambda e: e.tensor_tensor(out=st8[:], in0=mu8[:], in1=mu8[:], op=ALU.mult), [mu8], [st8])
                    op("dve", lambda e: e.scalar_tensor_tensor(out=sq8[:], in0=sq8[:], scalar=1.0 / 128, in1=st8[:],
                                                               op0=ALU.mult, op1=ALU.subtract), [sq8, st8], [sq8])
                    op("dve", lambda e: e.tensor_scalar(out=sq8[:], in0=sq8[:], scalar1=EPS, scalar2=None, op0=ALU.add),
                       [sq8], [sq8])
                    op("act", lambda e: e.sqrt(out=sq8[:], in_=sq8[:]), [sq8], [sq8])
                    op("dve", lambda e: e.reciprocal(out=st8[:], in_=sq8[:]), [sq8], [st8])
                    op("dve", lambda e: e.tensor_tensor(out=r3, in0=r3, in1=mu8[:].unsqueeze(2).broadcast_to([128, 8, 128]),
                                                        op=ALU.subtract), [rt, mu8], [rt])
                    op("dve", lambda e: e.tensor_tensor(out=r3, in0=r3, in1=st8[:].unsqueeze(2).broadcast_to([128, 8, 128]),
                                                        op=ALU.mult), [rt, st8], [rt])
                    for b in range(2):
                        op("pool", lambda e, b=b: e.tensor_tensor(
                            out=ycat[:, 1024:2048].rearrange("p (j b e) -> p j b e", b=2, e=128)[:, :, b, :],
                            in0=rt[:, b * 512:(b + 1) * 512].rearrange("p (j e) -> p j e", e=128),
                            in1=rec[:, R_GS:R_GS + 1024].rearrange("p (j b e) -> p j b e", b=2, e=128)[:, :, b, :],
                            op=ALU.mult), [rt, rec], [ycat])
                    for hf in range(2):
                        bk = bank()
                        bkb = bk[:].bitcast(BF16).rearrange("p (k t) -> p k t", k=8)
                        tr([(bkb[:, k, :], ycat[:, (hf * 8 + k) * 128:(hf * 8 + k + 1) * 128]) for k in range(8)],
                           identb, [ycat], [bk])
                        op("act", lambda e, hf=hf, bkb=bkb: e.copy(out=ycT[:, hf * 8:hf * 8 + 8, :], in_=bkb), [bk], [ycT])
                    mb = []
                    for nh in range(2):
                        bk = bank()
                        mm([(bk[:], ycT[:, k, :], w_out[:, k, nh * 512:(nh + 1) * 512], k == 0, k == 15) for k in range(16)],
                           [ycT, w_out], [bk])
                        op("act", lambda e, nh=nh, bk=bk: e.activation(out=junk[:, nh * 512:(nh + 1) * 512], in_=bk[:],
                                                                       func=AF.Square, accum_out=ssm[:, nh:nh + 1]),
                           [bk], [junk, ssm])
                        mb.append(bk)
                    op("dve", lambda e: e.tensor_tensor(out=ss1[:], in0=ssm[:, 0:1], in1=ssm[:, 1:2], op=ALU.add),
                       [ssm], [ss1])
                    rms_scale(ss1, tm1)
                    ho = hout[c % 2]
                    for nh in range(2):
                        sl = slice(nh * 512, (nh + 1) * 512)
                        op("dve", lambda e, sl=sl, nh=nh: e.scalar_tensor_tensor(
                            out=ho[:, sl], in0=mb[nh][:], scalar=ss1[:, 0:1], in1=prm[:, sl], op0=ALU.mult, op1=ALU.mult),
                           [mb[nh], ss1, prm], [ho])
                    op("pool", lambda e: e.tensor_tensor(out=ho[:], in0=ho[:], in1=hbt[:], op=ALU.add), [ho, hbt], [ho])
                    dma("sp", hmid_d[s, c * 128:(c + 1) * 128, :], ho[:], reads=[ho])

                for s in range(NSEQ):
                    reset_state(T)
                    loads(s, NCH - 1)
                    for c in range(NCH - 1, -1, -1):
                        if c - 1 >= 0:
                            loads(s, c - 1)
                        chunk(s, c)
                K.barrier()

        def phase3(layer, last):
            with ExitStack() as es:
                prm = mk(es, "prm3", [128, 2048], F32)
                dma("sp", prm[:], prm_d[layer, :, P_FPRE:P_FPRE + 2048], writes=[prm])
                w_g = load_w(es, "w_g", wg_d[layer], 8, DFF)
                w_u = load_w(es, "w_u", wu_d[layer], 8, DFF)
                w_d = load_w(es, "w_d", wd_d[layer], 22, D)
                hb = [mk(es, "hb3", [128, D], F32) for _ in range(2)]
                junk = mk(es, "junk3", [128, D], F32)
                ss = [mk(es, "ss3", [128, 1], F32) for _ in range(2)]
                tmp1 = [mk(es, "tmp3", [128, 1], F32) for _ in range(2)]
                f_bf = mk(es, "f_bf", [128, D], BF16)
                fT = mk(es, "fT", [128, 8, 128], BF16)
                sg = [mk(es, "sg", [128, 512], F32) for _ in range(2)]
                act = mk(es, "act", [128, DFF], BF16)
                actT = mk(es, "actT", [128, 22, 128], BF16)
                ssm = mk(es, "ssm3", [128, 2], F32)
                hout = [mk(es, "hout3", [128, D], F32) for _ in range(2)]

                def load(s, c):
                    dma("sp", hb[c % 2][:], hmid_d[s, c * 128:(c + 1) * 128, :], writes=[hb[c % 2]])

                def chunk(s, c):
                    b = hb[c % 2]
                    s1, t1 = ss[c % 2], tmp1[c % 2]
                    op("act", lambda e: e.activation(out=junk[:], in_=b[:], func=AF.Square, accum_out=s1[:]), [b], [junk, s1])
                    rms_scale(s1, t1)
                    op("dve", lambda e: e.scalar_tensor_tensor(out=f_bf[:], in0=b[:], scalar=s1[:, 0:1], in1=prm[:, 0:1024],
                                                               op0=ALU.mult, op1=ALU.mult), [b, s1, prm], [f_bf])
                    bk = bank()
                    bkb = bk[:].bitcast(BF16).rearrange("p (k t) -> p k t", k=8)
                    tr([(bkb[:, k, :], f_bf[:, k * 128:(k + 1) * 128]) for k in range(8)], identb, [f_bf], [bk])
                    op("act", lambda e: e.copy(out=fT[:], in_=bkb), [bk], [fT])
                    if DBG_3 < 2:
                        return
                    for blk in range(6):
                        c0 = blk * 512
                        n = min(512, DFF - c0)
                        bg = bank()
                        mm([(bg[:, 0:n], fT[:, k, :], w_g[:, k, c0:c0 + n], k == 0, k == 7) for k in range(8)], [fT, w_g], [bg])
                        bu = bank()
                        mm([(bu[:, 0:n], fT[:, k, :], w_u[:, k, c0:c0 + n], k == 0, k == 7) for k in range(8)], [fT, w_u], [bu])
                        sgt = sg[blk % 2]
                        op("act", lambda e, n=n, bg=bg, sgt=sgt: e.activation(out=sgt[:, 0:n], in_=bg[:, 0:n], func=AF.Silu),
                           [bg], [sgt])
                        op("dve", lambda e, n=n, c0=c0, bu=bu, sgt=sgt: e.tensor_tensor(out=act[:, c0:c0 + n], in0=bu[:, 0:n],
                                                                                     in1=sgt[:, 0:n], op=ALU.mult),
                           [bu, sgt], [act])
                    if DBG_3 < 3:
                        return
                    for tb in range(3):
                        k0 = tb * 8
                        nk = min(8, 22 - k0)
                        bk = bank()
                        bkb = bk[:].bitcast(BF16).rearrange("p (k t) -> p k t", k=8)
                        tr([(bkb[:, k, :], act[:, (k0 + k) * 128:(k0 + k + 1) * 128]) for k in range(nk)], identb, [act], [bk])
                        if tb % 2 == 0:
                            op("act", lambda e, k0=k0, nk=nk, bkb=bkb: e.copy(out=actT[:, k0:k0 + nk, :], in_=bkb[:, 0:nk, :]),
                               [bk], [actT])
                        else:
                            op("dve", lambda e, k0=k0, nk=nk, bkb=bkb: e.tensor_copy(out=actT[:, k0:k0 + nk, :], in_=bkb[:, 0:nk, :]),
                               [bk], [actT])
                    if DBG_3 < 4:
                        return
                    mb = []
                    for nh in range(2):
                        bk = bank()
                        mm([(bk[:], actT[:, k, :], w_d[:, k, nh * 512:(nh + 1) * 512], k == 0, k == 21) for k in range(22)],
                           [actT, w_d], [bk])
                        op("act", lambda e, nh=nh, bk=bk: e.activation(out=junk[:, nh * 512:(nh + 1) * 512], in_=bk[:],
                                                                       func=AF.Square, accum_out=ssm[:, nh:nh + 1]),
                           [bk], [junk, ssm])
                        mb.append(bk)
                    op("dve", lambda e: e.tensor_tensor(out=s1[:], in0=ssm[:, 0:1], in1=ssm[:, 1:2], op=ALU.add), [ssm], [s1])
                    rms_scale(s1, t1)
                    ho = hout[c % 2]
                    for nh in range(2):
                        sl = slice(nh * 512, (nh + 1) * 512)
                        op("dve", lambda e, sl=sl, nh=nh: e.scalar_tensor_tensor(
                            out=ho[:, sl], in0=mb[nh][:], scalar=s1[:, 0:1], in1=prm[:, 1024 + nh * 512:1024 + (nh + 1) * 512],
                            op0=ALU.mult, op1=ALU.mult), [mb[nh], s1, prm], [ho])
                    op("pool", lambda e: e.tensor_tensor(out=ho[:], in0=ho[:], in1=b[:], op=ALU.add), [ho, b], [ho])
                    if DBG_3 < 5:
                        return
                    if last:
                        if c > 0:
                            if DBG_3 == 5:
                                dma("sp", hres_d[s, c * 128:(c + 1) * 128, :], ho[:], reads=[ho])
                            elif DBG_3 == 6:
                                dma("sp", hres_d[s, c * 128:(c + 1) * 128, :], ho[:], reads=[ho])
                                dma("sp", y_d[s, (c - 1) * 128:c * 128, :], ho[:], reads=[ho])
                            elif DBG_3 == 7:
                                dma("pool", y_d[s, (c - 1) * 128:c * 128, :], ho[:], reads=[ho])
                            elif DBG_3 == 8:
                                dma("sp", y_d[s, (c - 1) * 128:c * 128, :], ho[:], reads=[ho])
                            else:
                                dma("sp", y_d[s, (c - 1) * 128:c * 128, :], ho[:], reads=[ho])
                    else:
                        dma("sp", hres_d[s, c * 128:(c + 1) * 128, :], ho[:], reads=[ho])

                for s in range(NSEQ):
                    c_list = list(range(NCH)) if not last else list(range(1, NCH))
                    load(s, c_list[0])
                    for i, c in enumerate(c_list):
                        if i + 1 < len(c_list):
                            load(s, c_list[i + 1])
                        chunk(s, c)
                K.barrier()
                if last and DBG_3 == 9:
                    with ExitStack() as esd:
                        ct = mk(esd, "ydiag", [128, D], F32)
                        op("pool", lambda e: e.memset(ct[:], 1.0), [], [ct])
                        for s in range(NSEQ):
                            for c in range(1, NCH):
                                dma("sp", y_d[s, (c - 1) * 128:c * 128, :], ct[:], reads=[ct])
                        K.barrier()
                K.barrier()

        for layer in range(DEPTH):
            if DEBUG_STOP == 0:
                break
            phase1(layer)
            if DEBUG_P1_ONLY or DEBUG_STOP == 1:
                break
            phase2(layer)
            if DEBUG_STOP == 2:
                break
            phase3(layer, layer == DEPTH - 1)
        if DBG_YW:
            with ExitStack() as esd:
                ct = mk(esd, "ydiag2", [128, D], F32)
                op("pool", lambda e: e.memset(ct[:], 1.0), [], [ct])
                for s in range(NSEQ):
                    for c in range(1, NCH):
                        dma("sp", y_d[s, (c - 1) * 128:c * 128, :], ct[:], reads=[ct])
                K.barrier()
    return nc


def _consts():
    c = np.zeros((128, CSTW), np.float32)
    p = np.arange(128)
    c[:, C_ID:C_ID + 128] = np.eye(128)
    c[:, C_U:C_U + 128] = (p[:, None] <= p[None, :])
    c[:, C_VGE:C_VGE + 128] = (p[:, None] >= p[None, :])
    c[:, C_SGT:C_SGT + 128] = (p[:, None] > p[None, :])
    c[:, C_REL:C_REL + 128] = (p[None, :] - p[:, None])
    c[:, C_LROW:C_LROW + 128] = p[None, :]
    c[:, C_VALID] = (p >= PADR)
    c[:, C_P] = p
    c[:, C_127P] = 127 - p
    return c


def _rope(L):
    pos = np.arange(L, dtype=np.float32)
    inv = (np.float32(10000.0) ** (-np.arange(0, 64, 2, dtype=np.float32) / np.float32(64))).astype(np.float32)
    ang = (pos[:, None] * inv[None, :]).astype(np.float32)
    cs, sn = np.cos(ang).astype(np.float32), np.sin(ang).astype(np.float32)
    t = np.zeros((L, 256), np.float32)
    t[:, 0:32] = cs; t[:, 32:64] = cs
    t[:, 64:96] = -sn; t[:, 96:128] = sn
    t[:, 128:256] = t[:, 0:128] * np.float32(0.125)
    return t


def _params(DEPTH, norm_mix_pre, norm_mix_post, norm_ffn_pre, norm_ffn_post, conv_w, conv_b, dt_bias, a_log,
            d_skip, ssd_norm, ret_log_decay):
    P = np.zeros((DEPTH, 128, PRMW), np.float32)
    for l in range(DEPTH):
        P[l, :, P_GPRE:P_GPRE + 1024] = norm_mix_pre[l][None, :]
        P[l, :, P_GPOST:P_GPOST + 1024] = norm_mix_post[l][None, :]
        P[l, :, P_FPRE:P_FPRE + 1024] = norm_ffn_pre[l][None, :]
        P[l, :, P_FPOST:P_FPOST + 1024] = norm_ffn_post[l][None, :]
        P[l, :, P_SSDN:P_SSDN + 1024] = ssd_norm[l][None, :]
        P[l, :, P_DSK:P_DSK + 1024] = np.repeat(d_skip[l], 64)[None, :]
        cw = conv_w[l].reshape(5, 12, 128)
        P[l, :, P_CW:P_CW + 60] = cw.transpose(2, 1, 0).reshape(128, 60)
        P[l, :, P_CB:P_CB + 12] = conv_b[l].reshape(12, 128).T
        P[l, :, P_DTB:P_DTB + 32] = dt_bias[l].reshape(32)[None, :]
        P[l, :, P_ALOG:P_ALOG + 32] = a_log[l].reshape(32)[None, :]
        P[l, :, P_LGB:P_LGB + 16] = ret_log_decay[l].reshape(16)[None, :]
        for d in range(2):
            lg = ret_log_decay[l, d]
            P[l, 0:64, P_LGP + 4 * d:P_LGP + 4 * d + 4] = lg[0::2][None, :]
            P[l, 64:128, P_LGP + 4 * d:P_LGP + 4 * d + 4] = lg[1::2][None, :]
    return P


_NC_CACHE = {}


def run(xs_all, meta_tokens, small, w_in, w_out, w_gate, w_up, w_down, n_cores, NSEQ, DEPTH):
    S = xs_all.shape[1]
    NCH = S // 128 + 1
    key = (NSEQ, NCH, DEPTH)
    if key not in _NC_CACHE:
        _NC_CACHE[key] = build(NSEQ, NCH, DEPTH)
    nc = _NC_CACHE[key]
    prm = _params(DEPTH, *small)
    cst = _consts()
    rope = _rope(NCH * 128)
    f = lambda a: np.ascontiguousarray(a, dtype=np.float32)
    in_maps = []
    for i in range(n_cores):
        in_maps.append({"x": f(xs_all[i * NSEQ:(i + 1) * NSEQ]), "meta": f(meta_tokens), "w_in": f(w_in), "w_out": f(w_out),
                        "w_gate": f(w_gate), "w_up": f(w_up), "w_down": f(w_down), "prm": prm, "cst": cst, "rope": rope})
    res = run_bass_kernel_spmd(nc, in_maps, core_ids=list(range(n_cores)))
    return np.concatenate([np.asarray(r["y"]) for r in res.results], axis=0)


def kernel(x_prompt, x_sample, meta_tokens, norm_mix_pre, norm_mix_post, norm_ffn_pre, norm_ffn_post,
           w_in, conv_w, conv_b, dt_bias, a_log, d_skip, ssd_norm, ret_log_decay, w_out, w_gate, w_up, w_down):
    x_prompt = np.asarray(x_prompt); x_sample = np.asarray(x_sample)
    nb = x_prompt.shape[0]
    xs_all = np.concatenate([x_prompt, x_sample], axis=0)
    small = [np.asarray(a, dtype=np.float32) for a in (norm_mix_pre, norm_mix_post, norm_ffn_pre, norm_ffn_post, conv_w,
                                                        conv_b, dt_bias, a_log, d_skip, ssd_norm, ret_log_decay)]
    y = run(xs_all, np.asarray(meta_tokens), small, np.asarray(w_in), np.asarray(w_out), np.asarray(w_gate),
            np.asarray(w_up), np.asarray(w_down), 8, xs_all.shape[0] // 8, np.asarray(w_in).shape[0])
    return (np.ascontiguousarray(y[:nb]), np.ascontiguousarray(y[nb:]))
```

```python
import numpy as np
from contextlib import ExitStack
import concourse.bass as bass
import concourse.mybir as mybir
from concourse.bass_utils import run_bass_kernel_spmd
import ml_dtypes

F32 = mybir.dt.float32
BF16 = mybir.dt.bfloat16
AF = mybir.ActivationFunctionType
ALU = mybir.AluOpType
AX = mybir.AxisListType

D = 1024
NIN = 5664
DFF = 2816
NMETA = 16
PADR = 112
EPS = 1e-6
OZ, OX, OB, OC, ODT, OQ, OK_, OV, OG = 0, 1024, 2048, 2304, 2560, 2592, 3104, 3616, 4640
R_XS, R_ZS, R_GS, R_V, R_YF, R_RF, R_KR, R_BT, R_QT, R_KT, R_BTT, R_CTT = (
    0, 1024, 2048, 3072, 4096, 5120, 6144, 6656, 6912, 7424, 7936, 8192)
RECW = 8448
P_GPRE, P_GPOST, P_FPRE, P_FPOST, P_SSDN, P_DSK, P_CW, P_CB, P_DTB, P_ALOG, P_LGB, P_LGP = (
    0, 1024, 2048, 3072, 4096, 5120, 6144, 6204, 6216, 6248, 6280, 6296)
PRMW = 6304
C_ID, C_U, C_VGE, C_SGT, C_REL, C_LROW, C_VALID, C_P, C_127P = 0, 128, 256, 384, 512, 640, 768, 769, 770
CSTW = 772
ND = 8
DEBUG_P1_ONLY = False
DEBUG_STOP = 99
DBG_DC = 99
DBG_A = 99
DBG_S = 99
DBG_3 = 99
DBG_YW = 0


class Tile:
    def __init__(self, h):
        self.h = h
        self.w = None
        self.r = {}

    def __getitem__(self, k):
        return self.h[k]


class Eng:
    def __init__(self, name, h, sem):
        self.name, self.h, self.sem = name, h, sem
        self.cnt = 0
        self.seen = {}


class Kern:
    def __init__(self, nc, es):
        self.nc, self.es = nc, es
        self.E = {}
        for n, a in (("pe", "tensor"), ("act", "scalar"), ("dve", "vector"), ("pool", "gpsimd"), ("sp", "sync")):
            self.E[n] = Eng(n, getattr(nc, a), es.enter_context(nc.semaphore("s_" + n)))
        self.dq = {}
        for q in ("sp", "pool"):
            sems = [es.enter_context(nc.semaphore("d_%s%d" % (q, i))) for i in range(ND)]
            self.dq[q] = dict(sems=sems, vals=[0] * ND, nxt=0)
        self.bsem = es.enter_context(nc.semaphore("bar"))
        self.bcnt = 0
        self.nid = 0

    def _wait(self, eng, toks):
        best = {}
        for key, sem, val in toks:
            if key not in best or best[key][1] < val:
                best[key] = (sem, val)
        for key, (sem, val) in best.items():
            if eng.seen.get(key, 0) >= val:
                continue
            if key == eng.name:
                if eng.name == "pe":
                    continue
                if eng.name != "pool" and eng.cnt - val >= 2:
                    continue
            eng.h.wait_ge(sem, val)
            eng.seen[key] = val

    @staticmethod
    def _deps(reads, writes):
        toks = []
        for t in reads:
            if t.w:
                toks.append(t.w)
        for t in writes:
            if t.w:
                toks.append(t.w)
            toks.extend(t.r.values())
        return toks

    def _mark(self, tok, reads, writes):
        for t in reads:
            t.r[tok[0]] = tok
        for t in writes:
            t.w = tok
            t.r = {}

    def op(self, en, fn, reads=(), writes=()):
        eng = self.E[en]
        self._wait(eng, self._deps(reads, writes))
        ins = fn(eng.h)
        eng.cnt += 1
        ins.then_inc(eng.sem, 1)
        self._mark((en, eng.sem, eng.cnt), reads, writes)

    def mm(self, items, reads=(), writes=()):
        eng = self.E["pe"]
        self._wait(eng, self._deps(reads, writes))
        ins = None
        for (o, l, r, st, sp) in items:
            ins = eng.h.matmul(o, lhsT=l, rhs=r, start=st, stop=sp)
        eng.cnt += 1
        ins.then_inc(eng.sem, 1)
        self._mark(("pe", eng.sem, eng.cnt), reads, writes)

    def tr(self, items, ident, reads=(), writes=()):
        eng = self.E["pe"]
        self._wait(eng, self._deps(list(reads) + [ident], writes))
        ins = None
        for (o, i) in items:
            ins = eng.h.transpose(out=o, in_=i, identity=ident[:])
        eng.cnt += 1
        ins.then_inc(eng.sem, 1)
        self._mark(("pe", eng.sem, eng.cnt), list(reads) + [ident], writes)

    def dma(self, q, out_ap, in_ap, reads=(), writes=()):
        eng = self.E[q]
        d = self.dq[q]
        j = d["nxt"]
        d["nxt"] = (j + 1) % ND
        key = "d_%s%d" % (q, j)
        toks = self._deps(reads, writes)
        if d["vals"][j] > 0:
            toks.append((key, d["sems"][j], d["vals"][j]))
        self._wait(eng, toks)
        ins = eng.h.dma_start(out=out_ap, in_=in_ap)
        d["vals"][j] += 16
        ins.then_inc(d["sems"][j], 16)
        self._mark((key, d["sems"][j], d["vals"][j]), reads, writes)

    def barrier(self):
        sp = self.E["sp"]
        toks = [(n, e.sem, e.cnt) for n, e in self.E.items() if e.cnt > 0]
        for q, d in self.dq.items():
            for j in range(ND):
                if d["vals"][j] > 0:
                    toks.append(("d_%s%d" % (q, j), d["sems"][j], d["vals"][j]))
        self._wait(sp, toks)
        self.bcnt += 1
        sp.h.sem_inc(self.bsem, 1)
        for n, e in self.E.items():
            if n != "sp":
                e.h.wait_ge(self.bsem, self.bcnt)
            for key, sem, val in toks:
                if e.seen.get(key, 0) < val:
                    e.seen[key] = val


def build(NSEQ, NCH, DEPTH):
    nc = bass.Bass("TRN2", target_bir_lowering=False)
    L = NCH * 128
    S = L - 128
    x_d = nc.dram_tensor("x", [NSEQ, S, D], F32, kind="ExternalInput").ap()
    meta_d = nc.dram_tensor("meta", [NMETA, D], F32, kind="ExternalInput").ap()
    win_d = nc.dram_tensor("w_in", [DEPTH, D, NIN], F32, kind="ExternalInput").ap()
    wout_d = nc.dram_tensor("w_out", [DEPTH, 2 * D, D], F32, kind="ExternalInput").ap()
    wg_d = nc.dram_tensor("w_gate", [DEPTH, D, DFF], F32, kind="ExternalInput").ap()
    wu_d = nc.dram_tensor("w_up", [DEPTH, D, DFF], F32, kind="ExternalInput").ap()
    wd_d = nc.dram_tensor("w_down", [DEPTH, DFF, D], F32, kind="ExternalInput").ap()
    prm_d = nc.dram_tensor("prm", [DEPTH, 128, PRMW], F32, kind="ExternalInput").ap()
    cst_d = nc.dram_tensor("cst", [128, CSTW], F32, kind="ExternalInput").ap()
    rope_d = nc.dram_tensor("rope", [L, 256], F32, kind="ExternalInput").ap()
    y_d = nc.dram_tensor("y", [NSEQ, S, D], F32, kind="ExternalOutput").ap()
    hres_d = nc.dram_tensor("hres", [NSEQ, L, D], F32, kind="Internal").ap()
    hmid_d = nc.dram_tensor("hmid", [NSEQ, L, D], F32, kind="Internal").ap()
    rec_d = nc.dram_tensor("rec", [NSEQ, NCH, 128, RECW], BF16, kind="Internal").ap()
    dts_d = nc.dram_tensor("dts", [NSEQ, NCH, 128, 32], F32, kind="Internal").ap()

    with ExitStack() as es0:
        K = Kern(nc, es0)
        op, mm, tr, dma = K.op, K.mm, K.tr, K.dma

        def mk(es, name, shape, dt):
            K.nid += 1
            return Tile(es.enter_context(nc.sbuf_tensor("%s_%d" % (name, K.nid), shape, dt)))

        cst = mk(es0, "cst", [128, CSTW], F32)
        identb = mk(es0, "identb", [128, 128], BF16)
        onesb = mk(es0, "onesb", [128, 128], BF16)
        Ub = mk(es0, "Ub", [128, 128], BF16)
        Vb = mk(es0, "Vb", [128, 128], BF16)
        MNf = mk(es0, "MNf", [128, 4, 128], BF16)
        MNb = mk(es0, "MNb", [128, 4, 128], BF16)
        est = ExitStack()
        zt = mk(est, "zt", [128, D], F32)
        banks = [Tile(es0.enter_context(nc.psum_tensor("bank%d" % i, [128, 512], F32))) for i in range(8)]
        bstate = dict(i=0)

        def bank():
            b = banks[bstate["i"] % 8]
            bstate["i"] += 1
            return b

        dma("sp", cst[:], cst_d[:, :], writes=[cst])
        op("dve", lambda e: e.tensor_copy(out=identb[:], in_=cst[:, C_ID:C_ID + 128]), [cst], [identb])
        op("dve", lambda e: e.tensor_copy(out=Ub[:], in_=cst[:, C_U:C_U + 128]), [cst], [Ub])
        op("dve", lambda e: e.tensor_copy(out=Vb[:], in_=cst[:, C_VGE:C_VGE + 128]), [cst], [Vb])
        op("pool", lambda e: e.memset(onesb[:], 1.0), [], [onesb])
        op("dve", lambda e: e.tensor_scalar(out=MNf[:], in0=cst[:, C_SGT:C_SGT + 128].unsqueeze(1).broadcast_to([128, 4, 128]),
                                            scalar1=-30000.0, scalar2=None, op0=ALU.mult), [cst], [MNf])
        op("dve", lambda e: e.tensor_scalar(out=MNb[:], in0=cst[:, C_U:C_U + 128].unsqueeze(1).broadcast_to([128, 4, 128]),
                                            scalar1=-30000.0, scalar2=None, op0=ALU.mult), [cst], [MNb])
        op("pool", lambda e: e.memset(zt[:], 0.0), [], [zt])
        for s in range(NSEQ):
            dma("sp", hres_d[s, 0:PADR, :], zt[0:PADR, :], reads=[zt])
            dma("sp", hres_d[s, PADR:128, :], meta_d[:, :])
        K.barrier()
        est.close()

        def h_src(layer, s, c):
            if layer == 0 and c > 0:
                return x_d[s, (c - 1) * 128:c * 128, :]
            return hres_d[s, c * 128:(c + 1) * 128, :]

        def load_w(es, name, src2d, kt, ncols):
            w = mk(es, name, [128, kt, ncols], BF16)
            for k in range(kt):
                c0 = 0
                while c0 < ncols:
                    cw = min(2048, ncols - c0)
                    dma("pool", w[:, k, c0:c0 + cw], src2d[k * 128:(k + 1) * 128, c0:c0 + cw], writes=[w])
                    c0 += cw
            return w

        def rms_scale(ss, tmp):
            op("dve", lambda e: e.tensor_scalar(out=tmp[:], in0=ss[:], scalar1=1.0 / D, scalar2=EPS,
                                                op0=ALU.mult, op1=ALU.add), [ss], [tmp])
            op("act", lambda e: e.sqrt(out=tmp[:], in_=tmp[:]), [tmp], [tmp])
            op("dve", lambda e: e.reciprocal(out=ss[:], in_=tmp[:]), [tmp], [ss])

        def dir_consts(es, prm, PO, d):
            lgB = prm[:, PO + P_LGB + 8 * d:PO + P_LGB + 8 * d + 8]
            lgP = prm[:, PO + P_LGP + 4 * d:PO + P_LGP + 4 * d + 4]
            dmT = mk(es, "dmT", [128, 8, 128], F32)
            xiT = mk(es, "xiT", [128, 4, 128], F32)
            zeta = mk(es, "zeta", [128, 8], F32)
            gch = mk(es, "gch", [128, 4], F32)
            negA = mk(es, "negA", [128, 16], F32)
            sc = mk(es, "sc", [128, 16], F32)
            if d == 0:
                op("dve", lambda e: e.tensor_copy(out=sc[:, 0:8], in_=lgB), [prm], [sc])
                op("dve", lambda e: e.tensor_copy(out=sc[:, 8:12], in_=lgP), [prm], [sc])
                op("dve", lambda e: e.tensor_copy(out=sc[:, 12:16], in_=lgP), [prm], [sc])
            else:
                op("dve", lambda e: e.tensor_scalar(out=sc[:, 0:8], in0=lgB, scalar1=-1.0, scalar2=None,
                                                    op0=ALU.mult), [prm], [sc])
                op("dve", lambda e: e.tensor_scalar(out=sc[:, 8:12], in0=lgP, scalar1=-1.0, scalar2=None,
                                                    op0=ALU.mult), [prm], [sc])
                op("dve", lambda e: e.tensor_scalar(out=sc[:, 12:16], in0=lgP, scalar1=128.0, scalar2=None,
                                                    op0=ALU.mult), [prm], [sc])
            msk = cst[:, C_U:C_U + 128] if d == 0 else cst[:, C_SGT:C_SGT + 128]
            if DBG_DC >= 1:
                for i in range(8):
                    h = 2 * (i % 4) + i // 4
                    op("act", lambda e, h=h, i=i: e.activation(out=dmT[:, i, :], in_=cst[:, C_REL:C_REL + 128], func=AF.Exp,
                                                               scale=sc[:, h:h + 1]), [cst, sc], [dmT])
            if DBG_DC >= 2:
                op("dve", lambda e: e.tensor_tensor(out=dmT[:], in0=dmT[:],
                                                    in1=msk.unsqueeze(1).broadcast_to([128, 8, 128]), op=ALU.mult),
                   [dmT, cst], [dmT])
            if DBG_DC >= 3:
                for j in range(4):
                    op("act", lambda e, j=j: e.activation(out=xiT[:, j, :], in_=cst[:, C_LROW:C_LROW + 128], func=AF.Exp,
                                                          scale=sc[:, 8 + j:9 + j], bias=sc[:, 12 + j:13 + j]),
                       [cst, sc], [xiT])
            zc = C_127P if d == 0 else C_P
            if DBG_DC >= 4:
                op("act", lambda e: e.activation(out=zeta[:], in_=lgB, func=AF.Exp, scale=cst[:, zc:zc + 1]),
                   [prm, cst], [zeta])
            if DBG_DC >= 5:
                op("act", lambda e: e.activation(out=gch[:], in_=lgP, func=AF.Exp, scale=128.0), [prm], [gch])
            if DBG_DC >= 6:
                op("act", lambda e: e.activation(out=negA[:], in_=prm[:, PO + P_ALOG + 16 * d:PO + P_ALOG + 16 * d + 16],
                                                 func=AF.Exp), [prm], [negA])
                op("dve", lambda e: e.tensor_scalar(out=negA[:], in0=negA[:], scalar1=-1.0, scalar2=None, op0=ALU.mult),
                   [negA], [negA])
            return dmT, xiT, zeta, gch, negA

        def scan_chunk(T, d, rec, dt, on_y, on_o):
            (dmT, xiT, zeta, gch, negA) = T["dc"]
            st, stb, Rs, Rb = T["state"], T["stateb"], T["R"], T["Rb"]
            dAb, G, nacs, cbm, E, eaT, MT, yin, xdt, xw, decB = (T[k] for k in (
                "dAb", "G", "nacs", "cbm", "E", "eaT", "MT", "yin", "xdt", "xw", "decB"))
            sTm, qxT, kz = T["sTm"], T["qxT"], T["kz"]
            Mb = Ub if d == 0 else Vb
            MN = MNf if d == 0 else MNb
            wend = T["wend"]
            mcol = C_U if d == 0 else C_SGT
            lsel = 127 if d == 0 else 0
            op("dve", lambda e: e.tensor_tensor(out=dAb[:], in0=dt[:, 16 * d:16 * d + 16], in1=negA[:], op=ALU.mult),
               [dt, negA], [dAb])
            op("pool", lambda e: e.tensor_tensor(out=G[:], in0=dAb[:].unsqueeze(2).broadcast_to([128, 16, 128]),
                                                 in1=Mb[:].unsqueeze(1).broadcast_to([128, 16, 128]), op=ALU.mult),
               [dAb, Mb], [G])
            bk = bank()
            mm([(bk[:, 0:16], Mb[:], dAb[:], True, True), (bk[:, 16:32], onesb[:], dAb[:], True, True)],
               [Mb, onesb, dAb], [bk])
            op("dve", lambda e, bk=bk: e.tensor_scalar(out=nacs[:], in0=bk[:, 0:16], scalar1=-1.0, scalar2=None,
                                                       op0=ALU.mult), [bk], [nacs])
            op("dve", lambda e, bk=bk: e.tensor_tensor(out=wend[:], in0=bk[:, 16:32], in1=nacs[:], op=ALU.add),
               [bk, nacs], [wend])
            op("act", lambda e, bk=bk: e.activation(out=eaT[:], in_=bk[:, 0:16], func=AF.Exp), [bk], [eaT])
            op("act", lambda e, bk=bk: e.activation(out=decB[:], in_=bk[:, 16:32], func=AF.Exp), [bk], [decB])
            op("act", lambda e: e.activation(out=wend[:], in_=wend[:], func=AF.Exp), [wend], [wend])
            if DBG_S < 2:
                return
            op("pool", lambda e: e.tensor_tensor(
                out=xdt[:].rearrange("p (h d) -> p h d", d=64),
                in0=rec[:, R_XS:R_XS + 1024].rearrange("p (h d) -> p h d", d=64),
                in1=dt[:, 16 * d:16 * d + 16].unsqueeze(2).broadcast_to([128, 16, 64]), op=ALU.mult),
               [rec, dt], [xdt])
            bk = bank()
            mm([(bk[:, g * 128:(g + 1) * 128], rec[:, R_BTT + g * 128:R_BTT + (g + 1) * 128],
                 rec[:, R_CTT + g * 128:R_CTT + (g + 1) * 128], True, True) for g in range(2)], [rec], [bk])
            op("dve", lambda e: e.tensor_tensor(
                out=cbm[:], in0=bk[:, 0:256].rearrange("p (g l) -> p g l", g=2),
                in1=cst[:, mcol:mcol + 128].unsqueeze(1).broadcast_to([128, 2, 128]), op=ALU.mult),
               [bk, cst], [cbm])
            if DBG_S < 3:
                return
            for g in range(2):
                for hf in range(2):
                    h0 = g * 8 + hf * 4
                    bk = bank()
                    mm([(bk[:], onesb[:], G[:, h0:h0 + 4, :].rearrange("p h l -> p (h l)"), True, False),
                        (bk[:], identb[:], MN[:].rearrange("p h l -> p (h l)"), False, True)],
                       [onesb, G, identb, MN], [bk])
                    for hh in range(4):
                        op("act", lambda e, hh=hh, bk=bk, h0=h0: e.activation(
                            out=E[:, h0 + hh, :], in_=bk[:, hh * 128:(hh + 1) * 128], func=AF.Exp,
                            bias=nacs[:, h0 + hh:h0 + hh + 1]), [bk, nacs], [E])
                    op("dve", lambda e, h0=h0, g=g: e.tensor_tensor(
                        out=MT[:, h0:h0 + 4, :], in0=E[:, h0:h0 + 4, :],
                        in1=cbm[:, g, :].unsqueeze(1).broadcast_to([128, 4, 128]), op=ALU.mult),
                       [E, cbm], [MT])
            if DBG_S < 4:
                return
            op("dve", lambda e: e.tensor_tensor(
                out=xw[:].rearrange("p (h d) -> p h d", d=64), in0=xdt[:].rearrange("p (h d) -> p h d", d=64),
                in1=wend[:].unsqueeze(2).broadcast_to([128, 16, 64]), op=ALU.mult), [xdt, wend], [xw])
            ybanks = []
            for g in range(2):
                bk = bank()
                items = []
                for h in range(8):
                    hg = g * 8 + h
                    items.append((bk[:, h * 64:(h + 1) * 64], MT[:, hg, :], xdt[:, hg * 64:(hg + 1) * 64], True, True))
                mm(items, [MT, xdt], [bk])
                bo = bank()
                mm([(bo[:], rec[:, R_CTT + g * 128:R_CTT + (g + 1) * 128], stb[:, g, :], True, True)], [rec, stb], [bo])
                yi = yin[g]
                op("act", lambda e, bo=bo, yi=yi: e.copy(out=yi[:], in_=bo[:]), [bo], [yi])
                op("dve", lambda e, yi=yi, g=g: e.tensor_tensor(
                    out=yi[:].rearrange("p (h d) -> p h d", d=64), in0=yi[:].rearrange("p (h d) -> p h d", d=64),
                    in1=eaT[:, g * 8:(g + 1) * 8].unsqueeze(2).broadcast_to([128, 8, 64]), op=ALU.mult), [yi, eaT], [yi])
                on_y(g, bk, yi)
            for g in range(2):
                bk = bank()
                mm([(bk[:], rec[:, R_BT + g * 128:R_BT + (g + 1) * 128], xw[:, g * 512:(g + 1) * 512], True, True)],
                   [rec, xw], [bk])
                op("dve", lambda e: e.tensor_tensor(
                    out=st[:, g, :].rearrange("p (h d) -> p h d", d=64),
                    in0=st[:, g, :].rearrange("p (h d) -> p h d", d=64),
                    in1=decB[:, g * 8:(g + 1) * 8].unsqueeze(2).broadcast_to([128, 8, 64]), op=ALU.mult),
                   [st, decB], [st])
                op("dve", lambda e: e.tensor_tensor(out=st[:, g, :], in0=st[:, g, :], in1=bk[:], op=ALU.add),
                   [st, bk], [st])
            op("act", lambda e: e.copy(out=stb[:].rearrange("p g n -> p (g n)"),
                                       in_=st[:].rearrange("p g n -> p (g n)")), [st], [stb])
            if DBG_S < 5:
                return
            op("pool", lambda e: e.tensor_tensor(
                out=qxT[:], in0=rec[:, R_QT:R_QT + 512].rearrange("p (j l) -> p j l", j=4), in1=xiT[:], op=ALU.mult),
               [rec, xiT], [qxT])
            op("pool", lambda e: e.tensor_tensor(
                out=kz[:].rearrange("p (h d) -> p h d", d=64),
                in0=rec[:, R_KR:R_KR + 512].rearrange("p (h d) -> p h d", d=64),
                in1=zeta[:].unsqueeze(2).broadcast_to([128, 8, 64]), op=ALU.mult), [rec, zeta], [kz])
            obanks = []
            for b in range(2):
                bk = bank()
                items = []
                for j in range(4):
                    items.append((bk[:, j * 128:(j + 1) * 128],
                                  rec[b * 64:(b + 1) * 64, R_KT + j * 128:R_KT + (j + 1) * 128],
                                  rec[b * 64:(b + 1) * 64, R_QT + j * 128:R_QT + (j + 1) * 128], True, True))
                mm(items, [rec], [bk])
                op("dve", lambda e, b=b, bk=bk: e.tensor_tensor(
                    out=sTm[:, b * 4:b * 4 + 4, :], in0=bk[:].rearrange("p (h l) -> p h l", h=4),
                    in1=dmT[:, b * 4:b * 4 + 4, :], op=ALU.mult), [bk, dmT], [sTm])
            for b in range(2):
                bk = bank()
                items = []
                for j in range(4):
                    h = 2 * j + b
                    items.append((bk[:, j * 128:(j + 1) * 128], sTm[:, b * 4 + j, :],
                                  rec[:, R_V + h * 128:R_V + (h + 1) * 128], True, False))
                    items.append((bk[:, j * 128:(j + 1) * 128], qxT[b * 64:(b + 1) * 64, j, :],
                                  Rb[b * 64:(b + 1) * 64, j, :], False, True))
                mm(items, [sTm, rec, qxT, Rb], [bk])
                on_o(b, bk)
            if DBG_S < 6:
                return
            bk = bank()
            items = []
            for h in range(8):
                j, b = h // 2, h % 2
                items.append((bk[b * 64:(b + 1) * 64, j * 128:(j + 1) * 128], kz[:, h * 64:(h + 1) * 64],
                              rec[:, R_V + h * 128:R_V + (h + 1) * 128], True, True))
            mm(items, [kz, rec], [bk])
            op("dve", lambda e: e.tensor_tensor(out=Rs[:], in0=Rs[:],
                                                in1=gch[:].unsqueeze(2).broadcast_to([128, 4, 128]), op=ALU.mult),
               [Rs, gch], [Rs])
            op("dve", lambda e: e.tensor_tensor(out=Rs[:].rearrange("p j e -> p (j e)"),
                                                in0=Rs[:].rearrange("p j e -> p (j e)"), in1=bk[:], op=ALU.add),
               [Rs, bk], [Rs])
            op("act", lambda e: e.copy(out=Rb[:].rearrange("p j e -> p (j e)"),
                                       in_=Rs[:].rearrange("p j e -> p (j e)")), [Rs], [Rb])

        def scan_tiles(es):
            T = {}
            T["state"] = mk(es, "state", [128, 2, 512], F32)
            T["stateb"] = mk(es, "stateb", [128, 2, 512], BF16)
            T["R"] = mk(es, "R", [128, 4, 128], F32)
            T["Rb"] = mk(es, "Rb", [128, 4, 128], BF16)
            T["dAb"] = mk(es, "dAb", [128, 16], BF16)
            T["G"] = mk(es, "G", [128, 16, 128], BF16)
            T["nacs"] = mk(es, "nacs", [128, 16], F32)
            T["cbm"] = mk(es, "cbm", [128, 2, 128], F32)
            T["E"] = mk(es, "E", [128, 16, 128], BF16)
            T["eaT"] = mk(es, "eaT", [128, 16], F32)
            T["yin"] = [mk(es, "yin", [128, 512], F32) for _ in range(2)]
            T["MT"] = mk(es, "MT", [128, 16, 128], BF16)
            T["xdt"] = mk(es, "xdt", [128, 1024], BF16)
            T["xw"] = mk(es, "xw", [128, 1024], BF16)
            T["decB"] = mk(es, "decB", [128, 16], F32)
            T["wend"] = mk(es, "wend", [128, 16], F32)
            T["sTm"] = mk(es, "sTm", [128, 8, 128], BF16)
            T["qxT"] = mk(es, "qxT", [128, 4, 128], BF16)
            T["kz"] = mk(es, "kz", [128, 512], BF16)
            return T

        def reset_state(T):
            op("pool", lambda e: e.memset(T["state"][:], 0.0), [], [T["state"]])
            op("pool", lambda e: e.memset(T["stateb"][:], 0.0), [], [T["stateb"]])
            op("pool", lambda e: e.memset(T["R"][:], 0.0), [], [T["R"]])
            op("pool", lambda e: e.memset(T["Rb"][:], 0.0), [], [T["Rb"]])

        def phase1(layer):
            with ExitStack() as es:
                prm = mk(es, "prm1", [128, PRMW - P_CW + 1024], F32)
                PO = 1024 - P_CW
                dma("sp", prm[:, 0:1024], prm_d[layer, :, P_GPRE:P_GPRE + 1024], writes=[prm])
                dma("sp", prm[:, 1024:], prm_d[layer, :, P_CW:PRMW], writes=[prm])

                if DEBUG_STOP == 8:
                    K.barrier()
                    return
                w_in = load_w(es, "w_in", win_d[layer], 8, NIN)
                if DEBUG_STOP == 9:
                    K.barrier()
                    return
                T = scan_tiles(es)
                T["dc"] = dir_consts(es, prm, PO, 0)
                if DEBUG_STOP == 10:
                    K.barrier()
                    return
                hb = [mk(es, "hb", [128, D], F32) for _ in range(2)]
                junk = mk(es, "junk", [128, D], F32)
                ss = [mk(es, "ss", [128, 1], F32) for _ in range(2)]
                tmp1 = [mk(es, "tmp1", [128, 1], F32) for _ in range(2)]
                u_bf = mk(es, "u_bf", [128, D], BF16)
                ext = [mk(es, "ext", [128, 8, 192], BF16) for _ in range(3)]
                acc = [mk(es, "acc", [128, 3, 128], F32) for _ in range(2)]
                xT = mk(es, "xT", [128, 8, 128], BF16)
                recs1 = [mk(es, "rec", [128, RECW], BF16) for _ in range(2)]
                ropet = [mk(es, "ropet", [128, 256], F32) for _ in range(2)]
                dtts = [mk(es, "dtt", [128, 32], F32) for _ in range(2)]
                dtmp = mk(es, "dtmp", [128, 32], F32)
                q_rot = mk(es, "q_rot", [128, 512], BF16)
                pT = Tile(None)

                def stageA(s, c):
                    b = hb[c % 2]
                    s1, t1 = ss[c % 2], tmp1[c % 2]
                    dma("sp", b[:], h_src(layer, s, c), writes=[b])
                    op("act", lambda e: e.activation(out=junk[:], in_=b[:], func=AF.Square, accum_out=s1[:]),
                       [b], [junk, s1])
                    rms_scale(s1, t1)
                    if c == 0:
                        op("dve", lambda e: e.tensor_tensor(out=s1[:], in0=s1[:], in1=cst[:, C_VALID:C_VALID + 1],
                                                            op=ALU.mult), [s1, cst], [s1])
                    if DBG_A < 2:
                        return
                    op("dve", lambda e: e.scalar_tensor_tensor(out=u_bf[:], in0=b[:], scalar=s1[:, 0:1],
                                                               in1=prm[:, 0:1024], op0=ALU.mult, op1=ALU.mult),
                       [b, s1, prm], [u_bf])
                    if DBG_A < 3:
                        return
                    bk = bank()
                    bkb = bk[:].bitcast(BF16).rearrange("p (k t) -> p k t", k=8)
                    tr([(bkb[:, k, :], u_bf[:, k * 128:(k + 1) * 128]) for k in range(8)], identb, [u_bf], [bk])
                    if DBG_A < 4:
                        return
                    e_c = ext[c % 3]
                    op("act", lambda e: e.copy(out=e_c[:, :, 32:160], in_=bkb), [bk], [e_c])
                    if DBG_A < 5:
                        return
                    if c > 0:
                        e_p = ext[(c - 1) % 3]
                        op("pool", lambda e: e.tensor_copy(out=e_p[:, :, 160:192], in_=e_c[:, :, 32:64]), [e_c], [e_p])
                        op("pool", lambda e: e.tensor_copy(out=e_c[:, :, 0:32], in_=e_p[:, :, 128:160]), [e_p], [e_c])
                    else:
                        op("pool", lambda e: e.memset(e_c[:, :, 0:32], 0.0), [], [e_c])
                    if c == NCH - 1:
                        op("pool", lambda e: e.memset(e_c[:, :, 160:192], 0.0), [], [e_c])

                def proj_tok(e_c, c0, n):
                    bk = bank()
                    mm([(bk[:, 0:n], e_c[:, k, 32:160], w_in[:, k, c0:c0 + n], k == 0, k == 7) for k in range(8)],
                       [e_c, w_in], [bk])
                    return bk

                def stageB1(s, c):
                    rec = recs1[c % 2]
                    dtt = dtts[c % 2]
                    e_c = ext[c % 3]
                    rp = ropet[c % 2]
                    dma("sp", rp[:], rope_d[c * 128:(c + 1) * 128, :], writes=[rp])
                    bk = proj_tok(e_c, ODT, 32)
                    op("dve", lambda e: e.tensor_tensor(out=dtmp[:], in0=bk[:, 0:32],
                                                        in1=prm[:, PO + P_DTB:PO + P_DTB + 32], op=ALU.add),
                       [bk, prm], [dtmp])
                    op("act", lambda e: e.activation(out=dtmp[:], in_=dtmp[:], func=AF.Exp), [dtmp], [dtmp])
                    op("act", lambda e: e.activation(out=dtt[:], in_=dtmp[:], func=AF.Ln, bias=1.0), [dtmp], [dtt])
                    if c == 0:
                        op("dve", lambda e: e.tensor_scalar(out=dtt[:], in0=dtt[:], scalar1=cst[:, C_VALID:C_VALID + 1],
                                                            scalar2=None, op0=ALU.mult), [dtt, cst], [dtt])
                    dma("sp", dts_d[s, c], dtt[:], reads=[dtt])
                    for (c0, r0, fn) in ((OZ, R_ZS, AF.Silu), (OZ + 512, R_ZS + 512, AF.Silu),
                                         (OG, R_GS, AF.Silu), (OG + 512, R_GS + 512, AF.Silu),
                                         (OV, R_V, AF.Copy), (OV + 512, R_V + 512, AF.Copy)):
                        bk = proj_tok(e_c, c0, 512)
                        op("act", lambda e, r0=r0, fn=fn, bk=bk: e.activation(out=rec[:, r0:r0 + 512], in_=bk[:], func=fn),
                           [bk], [rec])
                    for qi, (c0, dst, r0) in enumerate(((OQ, q_rot, 0), (OK_, rec, R_KR))):
                        bk = proj_tok(e_c, c0, 512)
                        b3 = bk[:].rearrange("p (h d) -> p h d", d=64)
                        cc = rp[:, qi * 128:qi * 128 + 64]
                        sg = rp[:, qi * 128 + 64:qi * 128 + 128]
                        A3 = junk[:, 0:512].rearrange("p (h d) -> p h d", d=64)
                        B3 = junk[:, 512:1024].rearrange("p (h d) -> p h d", d=64)
                        op("dve", lambda e: e.tensor_tensor(out=A3, in0=b3, in1=cc.unsqueeze(1).broadcast_to([128, 8, 64]),
                                                            op=ALU.mult), [bk, rp], [junk])
                        op("dve", lambda e: e.tensor_tensor(out=B3[:, :, 0:32], in0=b3[:, :, 32:64],
                                                            in1=sg[:, 0:32].unsqueeze(1).broadcast_to([128, 8, 32]),
                                                            op=ALU.mult), [bk, rp], [junk])
                        op("dve", lambda e: e.tensor_tensor(out=B3[:, :, 32:64], in0=b3[:, :, 0:32],
                                                            in1=sg[:, 32:64].unsqueeze(1).broadcast_to([128, 8, 32]),
                                                            op=ALU.mult), [bk, rp], [junk])
                        op("pool", lambda e, dst=dst, r0=r0: e.tensor_tensor(out=dst[:, r0:r0 + 512], in0=junk[:, 0:512],
                                                                             in1=junk[:, 512:1024], op=ALU.add),
                           [junk], [dst])
                    bk = bank()
                    bkb = bk[:].bitcast(BF16).rearrange("p (k t) -> p k t", k=8)
                    tr([(bkb[:, j, :], q_rot[:, j * 128:(j + 1) * 128]) for j in range(4)] +
                       [(bkb[:, 4 + j, :], rec[:, R_KR + j * 128:R_KR + (j + 1) * 128]) for j in range(4)],
                       identb, [q_rot, rec], [bk])
                    op("act", lambda e: e.copy(out=rec[:, R_QT:R_QT + 1024], in_=bk[:].bitcast(BF16)), [bk], [rec])
                    for jb in range(4):
                        bk = bank()
                        b3 = bk[:, 0:396].rearrange("p (j t) -> p j t", j=3)
                        items = []
                        for jj in range(3):
                            j = jb * 3 + jj
                            for k in range(8):
                                items.append((b3[:, jj, :], w_in[:, k, OX + j * 128:OX + (j + 1) * 128],
                                              e_c[:, k, 30:162], k == 0, k == 7))
                        mm(items, [e_c, w_in], [bk])
                        a = acc[jb % 2]
                        for jj in range(3):
                            j = jb * 3 + jj
                            cw = PO + P_CW + j * 5
                            op("dve", lambda e, jj=jj, cw=cw: e.tensor_scalar(
                                out=a[:, jj, :], in0=b3[:, jj, 0:128], scalar1=prm[:, cw:cw + 1], scalar2=None,
                                op0=ALU.mult), [bk, prm], [a])
                            for t in range(1, 5):
                                op("dve", lambda e, jj=jj, cw=cw, t=t: e.scalar_tensor_tensor(
                                    out=a[:, jj, :], in0=b3[:, jj, t:t + 128], scalar=prm[:, cw + t:cw + t + 1],
                                    in1=a[:, jj, :], op0=ALU.mult, op1=ALU.add), [bk, prm, a], [a])
                            cb = PO + P_CB + j
                            if j < 8:
                                dst_t, dst_ap = xT, xT[:, j, :]
                            elif j < 10:
                                dst_t, dst_ap = rec, rec[:, R_BTT + (j - 8) * 128:R_BTT + (j - 7) * 128]
                            else:
                                dst_t, dst_ap = rec, rec[:, R_CTT + (j - 10) * 128:R_CTT + (j - 9) * 128]
                            op("act", lambda e, jj=jj, cb=cb, dst_ap=dst_ap: e.activation(
                                out=dst_ap, in_=a[:, jj, :], func=AF.Silu, bias=prm[:, cb:cb + 1]), [a, prm], [dst_t])
                    bk = bank()
                    bkb = bk[:].bitcast(BF16).rearrange("p (k t) -> p k t", k=8)
                    tr([(bkb[:, j, :], xT[:, j, :]) for j in range(8)], identb, [xT], [bk])
                    op("act", lambda e: e.copy(out=rec[:, R_XS:R_XS + 1024], in_=bk[:].bitcast(BF16)), [bk], [rec])
                    bk = bank()
                    bkb = bk[:].bitcast(BF16)
                    tr([(bkb[:, g * 128:(g + 1) * 128], rec[:, R_BTT + g * 128:R_BTT + (g + 1) * 128]) for g in range(2)],
                       identb, [rec], [bk])
                    op("dve", lambda e: e.tensor_copy(out=rec[:, R_BT:R_BT + 256], in_=bkb[:, 0:256]), [bk], [rec])

                def stageB2(s, c):
                    rec = recs1[c % 2]
                    dtt = dtts[c % 2]

                    def on_y(g, bk, yi):
                        op("dve", lambda e: e.tensor_tensor(out=rec[:, R_YF + g * 512:R_YF + (g + 1) * 512],
                                                            in0=bk[:], in1=yi[:], op=ALU.add), [bk, yi], [rec])

                    def on_o(b_, bk):
                        op("act", lambda e: e.copy(out=rec[:, R_RF + b_ * 512:R_RF + (b_ + 1) * 512], in_=bk[:]),
                           [bk], [rec])
                    scan_chunk(T, 0, rec, dtt, on_y, on_o)
                    dma("sp", rec_d[s, c], rec[:], reads=[rec])

                for s in range(NSEQ):
                    reset_state(T)
                    stageA(s, 0)
                    if NCH > 1:
                        stageA(s, 1)
                    stageB1(s, 0)
                    for c in range(NCH):
                        if c + 2 < NCH:
                            stageA(s, c + 2)
                        if c + 1 < NCH:
                            stageB1(s, c + 1)
                        stageB2(s, c)
                K.barrier()

        def phase2(layer):
            with ExitStack() as es:
                prm = mk(es, "prm2", [128, 3 * 1024 + PRMW - P_CW], F32)
                PO = 3072 - P_CW
                dma("sp", prm[:, 0:1024], prm_d[layer, :, P_GPOST:P_GPOST + 1024], writes=[prm])
                dma("sp", prm[:, 1024:3072], prm_d[layer, :, P_SSDN:P_SSDN + 2048], writes=[prm])
                dma("sp", prm[:, 3072:], prm_d[layer, :, P_CW:PRMW], writes=[prm])
                w_out = load_w(es, "w_out", wout_d[layer], 16, D)
                T = scan_tiles(es)
                T["dc"] = dir_consts(es, prm, PO, 1)
                recs = [mk(es, "rec2", [128, RECW], BF16) for _ in range(2)]
                dts = [mk(es, "dt2", [128, 32], F32) for _ in range(2)]
                hb = [mk(es, "hb2", [128, D], F32) for _ in range(3)]
                yt = mk(es, "yt", [128, D], F32)
                yt2 = mk(es, "yt2", [128, D], F32)
                rt = mk(es, "rt", [128, D], F32)
                junk = mk(es, "junk2", [128, D], F32)
                ycats = [mk(es, "ycat", [128, 2 * D], BF16) for _ in range(2)]
                ycT = mk(es, "ycT", [128, 16, 128], BF16)
                st8 = mk(es, "st8", [128, 8], F32)
                sq8 = mk(es, "sq8", [128, 8], F32)
                mu8 = mk(es, "mu8", [128, 8], F32)
                ss2 = mk(es, "ss2", [128, 2], F32)
                tm2 = mk(es, "tm2", [128, 2], F32)
                ssm = mk(es, "ssm", [128, 2], F32)
                ss1 = mk(es, "ss1m", [128, 1], F32)
                tm1 = mk(es, "tm1m", [128, 1], F32)
                hout = [mk(es, "hout", [128, D], F32) for _ in range(2)]

                def loads(s, c):
                    r = recs[c % 2]
                    dma("sp", r[:], rec_d[s, c], writes=[r])
                    dma("sp", dts[c % 2][:], dts_d[s, c], writes=[dts[c % 2]])
                    dma("sp", hb[c % 3][:], h_src(layer, s, c), writes=[hb[c % 3]])

                def chunkC1(s, c):
                    rec, dt = recs[c % 2], dts[c % 2]
                    ycat = ycats[c % 2]
                    def on_y(g, bk, yi):
                        sl = slice(g * 512, (g + 1) * 512)
                        op("dve", lambda e: e.tensor_tensor(out=yt[:, sl], in0=bk[:], in1=yi[:], op=ALU.add), [bk, yi], [yt])
                        op("pool", lambda e: e.tensor_tensor(out=yt[:, sl], in0=yt[:, sl],
                                                             in1=rec[:, R_YF + g * 512:R_YF + (g + 1) * 512],
                                                             op=ALU.add), [yt, rec], [yt])
                        op("pool", lambda e: e.tensor_tensor(out=yt2[:, sl], in0=rec[:, R_XS + g * 512:R_XS + (g + 1) * 512],
                                                             in1=prm[:, 2048 + g * 512:2048 + (g + 1) * 512],
                                                             op=ALU.mult), [rec, prm], [yt2])
                        op("dve", lambda e: e.tensor_tensor(out=yt[:, sl], in0=yt[:, sl], in1=yt2[:, sl], op=ALU.add),
                           [yt, yt2], [yt])
                        op("dve", lambda e: e.tensor_tensor(out=yt[:, sl], in0=yt[:, sl],
                                                            in1=rec[:, R_ZS + g * 512:R_ZS + (g + 1) * 512],
                                                            op=ALU.mult), [yt, rec], [yt])
                        op("act", lambda e: e.activation(out=junk[:, sl], in_=yt[:, sl], func=AF.Square,
                                                         accum_out=ss2[:, g:g + 1]), [yt], [junk, ss2])

                    def on_o(b_, bk):
                        sl = slice(b_ * 512, (b_ + 1) * 512)
                        op("dve", lambda e: e.tensor_tensor(out=rt[:, sl], in0=bk[:],
                                                            in1=rec[:, R_RF + b_ * 512:R_RF + (b_ + 1) * 512],
                                                            op=ALU.add), [bk, rec], [rt])
                    scan_chunk(T, 1, rec, dt, on_y, on_o)
                    op("dve", lambda e: e.tensor_scalar(out=tm2[:], in0=ss2[:], scalar1=1.0 / 512, scalar2=EPS,
                                                        op0=ALU.mult, op1=ALU.add), [ss2], [tm2])
                    op("act", lambda e: e.sqrt(out=tm2[:], in_=tm2[:]), [tm2], [tm2])
                    op("dve", lambda e: e.reciprocal(out=ss2[:], in_=tm2[:]), [tm2], [ss2])
                    for g in range(2):
                        sl = slice(g * 512, (g + 1) * 512)
                        op("dve", lambda e, sl=sl, g=g: e.scalar_tensor_tensor(
                            out=ycat[:, sl], in0=yt[:, sl], scalar=ss2[:, g:g + 1],
                            in1=prm[:, 1024 + g * 512:1024 + (g + 1) * 512], op0=ALU.mult, op1=ALU.mult),
                           [yt, ss2, prm], [ycat])
                    r3 = rt[:].rearrange("p (h e) -> p h e", e=128)
                    j3 = junk[:].rearrange("p (h e) -> p h e", e=128)
                    op("dve", lambda e: e.tensor_reduce(out=st8[:], in_=r3, axis=AX.X, op=ALU.add), [rt], [st8])
                    op("pool", lambda e: e.tensor_tensor(out=junk[:], in0=rt[:], in1=rt[:], op=ALU.mult), [rt], [junk])
                    op("dve", lambda e: e.tensor_reduce(out=sq8[:], in_=j3, axis=AX.X, op=ALU.add), [junk], [sq8])
                    op("dve", lambda e: e.tensor_scalar(out=mu8[:], in0=st8[:], scalar1=1.0 / 128, scalar2=None,
                                                        op0=ALU.mult), [st8], [mu8])
                    op("dve", lambda e: e.tensor_tensor(out=st8[:], in0=mu8[:], in1=mu8[:], op=ALU.mult), [mu8], [st8])
                    op("dve", lambda e: e.scalar_tensor_tensor(out=sq8[:], in0=sq8[:], scalar=1.0 / 128, in1=st8[:],
                                                               op0=ALU.mult, op1=ALU.subtract), [sq8, st8], [sq8])
                    op("dve", lambda e: e.tensor_scalar(out=sq8[:], in0=sq8[:], scalar1=EPS, scalar2=None, op0=ALU.add),
                       [sq8], [sq8])
                    op("act", lambda e: e.sqrt(out=sq8[:], in_=sq8[:]), [sq8], [sq8])
                    op("dve", lambda e: e.reciprocal(out=st8[:], in_=sq8[:]), [sq8], [st8])
                    op("dve", lambda e: e.tensor_tensor(out=r3, in0=r3, in1=mu8[:].unsqueeze(2).broadcast_to([128, 8, 128]),
                                                        op=ALU.subtract), [rt, mu8], [rt])
                    op("dve", lambda e: e.tensor_tensor(out=r3, in0=r3, in1=st8[:].unsqueeze(2).broadcast_to([128, 8, 128]),
                                                        op=ALU.mult), [rt, st8], [rt])
                    for b in range(2):
                        op("pool", lambda e, b=b: e.tensor_tensor(
                            out=ycat[:, 1024:2048].rearrange("p (j b e) -> p j b e", b=2, e=128)[:, :, b, :],
                            in0=rt[:, b * 512:(b + 1) * 512].rearrange("p (j e) -> p j e", e=128),
                            in1=rec[:, R_GS:R_GS + 1024].rearrange("p (j b e) -> p j b e", b=2, e=128)[:, :, b, :],
                            op=ALU.mult), [rt, rec], [ycat])

                def chunkC2(s, c):
                    hbt = hb[c % 3]
                    ycat = ycats[c % 2]
                    for hf in range(2):
                        bk = bank()
                        bkb = bk[:].bitcast(BF16).rearrange("p (k t) -> p k t", k=8)
                        tr([(bkb[:, k, :], ycat[:, (hf * 8 + k) * 128:(hf * 8 + k + 1) * 128]) for k in range(8)],
                           identb, [ycat], [bk])
                        op("act", lambda e, hf=hf, bkb=bkb: e.copy(out=ycT[:, hf * 8:hf * 8 + 8, :], in_=bkb), [bk], [ycT])
                    mb = []
                    for nh in range(2):
                        bk = bank()
                        mm([(bk[:], ycT[:, k, :], w_out[:, k, nh * 512:(nh + 1) * 512], k == 0, k == 15) for k in range(16)],
                           [ycT, w_out], [bk])
                        op("act", lambda e, nh=nh, bk=bk: e.activation(out=junk[:, nh * 512:(nh + 1) * 512], in_=bk[:],
                                                                       func=AF.Square, accum_out=ssm[:, nh:nh + 1]),
                           [bk], [junk, ssm])
                        mb.append(bk)
                    op("dve", lambda e: e.tensor_tensor(out=ss1[:], in0=ssm[:, 0:1], in1=ssm[:, 1:2], op=ALU.add),
                       [ssm], [ss1])
                    rms_scale(ss1, tm1)
                    ho = hout[c % 2]
                    for nh in range(2):
                        sl = slice(nh * 512, (nh + 1) * 512)
                        op("dve", lambda e, sl=sl, nh=nh: e.scalar_tensor_tensor(
                            out=ho[:, sl], in0=mb[nh][:], scalar=ss1[:, 0:1], in1=prm[:, sl], op0=ALU.mult, op1=ALU.mult),
                           [mb[nh], ss1, prm], [ho])
                    op("pool", lambda e: e.tensor_tensor(out=ho[:], in0=ho[:], in1=hbt[:], op=ALU.add), [ho, hbt], [ho])
                    dma("sp", hmid_d[s, c * 128:(c + 1) * 128, :], ho[:], reads=[ho])

                for s in range(NSEQ):
                    reset_state(T)
                    loads(s, NCH - 1)
                    for c in range(NCH - 1, -1, -1):
                        if c - 1 >= 0:
                            loads(s, c - 1)
                        chunkC1(s, c)
                        if c + 1 <= NCH - 1:
                            chunkC2(s, c + 1)
                    chunkC2(s, 0)
                K.barrier()

        def phase3(layer, last):
            with ExitStack() as es:
                prm = mk(es, "prm3", [128, 2048], F32)
                dma("sp", prm[:], prm_d[layer, :, P_FPRE:P_FPRE + 2048], writes=[prm])
                w_g = load_w(es, "w_g", wg_d[layer], 8, DFF)
                w_u = load_w(es, "w_u", wu_d[layer], 8, DFF)
                w_d = load_w(es, "w_d", wd_d[layer], 22, D)
                hb = [mk(es, "hb3", [128, D], F32) for _ in range(3)]
                junk = mk(es, "junk3", [128, D], F32)
                ss = [mk(es, "ss3", [128, 1], F32) for _ in range(2)]
                tmp1 = [mk(es, "tmp3", [128, 1], F32) for _ in range(2)]
                f_bf = mk(es, "f_bf", [128, D], BF16)
                fT = mk(es, "fT", [128, 8, 128], BF16)
                sg = [mk(es, "sg", [128, 512], F32) for _ in range(2)]
                acts = [mk(es, "act", [128, DFF], BF16) for _ in range(2)]
                actT = mk(es, "actT", [128, 22, 128], BF16)
                ssm = mk(es, "ssm3", [128, 2], F32)
                hout = [mk(es, "hout3", [128, D], F32) for _ in range(2)]

                def load(s, c):
                    dma("sp", hb[c % 3][:], hmid_d[s, c * 128:(c + 1) * 128, :], writes=[hb[c % 3]])

                def chunkD1(s, c):
                    b = hb[c % 3]
                    act = acts[c % 2]
                    s1, t1 = ss[c % 2], tmp1[c % 2]
                    op("act", lambda e: e.activation(out=junk[:], in_=b[:], func=AF.Square, accum_out=s1[:]), [b], [junk, s1])
                    rms_scale(s1, t1)
                    op("dve", lambda e: e.scalar_tensor_tensor(out=f_bf[:], in0=b[:], scalar=s1[:, 0:1], in1=prm[:, 0:1024],
                                                               op0=ALU.mult, op1=ALU.mult), [b, s1, prm], [f_bf])
                    bk = bank()
                    bkb = bk[:].bitcast(BF16).rearrange("p (k t) -> p k t", k=8)
                    tr([(bkb[:, k, :], f_bf[:, k * 128:(k + 1) * 128]) for k in range(8)], identb, [f_bf], [bk])
                    op("act", lambda e: e.copy(out=fT[:], in_=bkb), [bk], [fT])
                    for blk in range(6):
                        c0 = blk * 512
                        n = min(512, DFF - c0)
                        bg = bank()
                        mm([(bg[:, 0:n], fT[:, k, :], w_g[:, k, c0:c0 + n], k == 0, k == 7) for k in range(8)], [fT, w_g], [bg])
                        bu = bank()
                        mm([(bu[:, 0:n], fT[:, k, :], w_u[:, k, c0:c0 + n], k == 0, k == 7) for k in range(8)], [fT, w_u], [bu])
                        sgt = sg[blk % 2]
                        op("act", lambda e, n=n, bg=bg, sgt=sgt: e.activation(out=sgt[:, 0:n], in_=bg[:, 0:n], func=AF.Silu),
                           [bg], [sgt])
                        op("dve", lambda e, n=n, c0=c0, bu=bu, sgt=sgt: e.tensor_tensor(out=act[:, c0:c0 + n], in0=bu[:, 0:n],
                                                                                     in1=sgt[:, 0:n], op=ALU.mult),
                           [bu, sgt], [act])

                def chunkD2(s, c):
                    b = hb[c % 3]
                    act = acts[c % 2]
                    s1, t1 = ss[c % 2], tmp1[c % 2]
                    for tb in range(3):
                        k0 = tb * 8
                        nk = min(8, 22 - k0)
                        bk = bank()
                        bkb = bk[:].bitcast(BF16).rearrange("p (k t) -> p k t", k=8)
                        tr([(bkb[:, k, :], act[:, (k0 + k) * 128:(k0 + k + 1) * 128]) for k in range(nk)], identb, [act], [bk])
                        if tb % 2 == 0:
                            op("act", lambda e, k0=k0, nk=nk, bkb=bkb: e.copy(out=actT[:, k0:k0 + nk, :], in_=bkb[:, 0:nk, :]),
                               [bk], [actT])
                        else:
                            op("dve", lambda e, k0=k0, nk=nk, bkb=bkb: e.tensor_copy(out=actT[:, k0:k0 + nk, :], in_=bkb[:, 0:nk, :]),
                               [bk], [actT])
                    if DBG_3 < 4:
                        return
                    mb = []
                    for nh in range(2):
                        bk = bank()
                        mm([(bk[:], actT[:, k, :], w_d[:, k, nh * 512:(nh + 1) * 512], k == 0, k == 21) for k in range(22)],
                           [actT, w_d], [bk])
                        op("act", lambda e, nh=nh, bk=bk: e.activation(out=junk[:, nh * 512:(nh + 1) * 512], in_=bk[:],
                                                                       func=AF.Square, accum_out=ssm[:, nh:nh + 1]),
                           [bk], [junk, ssm])
                        mb.append(bk)
                    op("dve", lambda e: e.tensor_tensor(out=s1[:], in0=ssm[:, 0:1], in1=ssm[:, 1:2], op=ALU.add), [ssm], [s1])
                    rms_scale(s1, t1)
                    ho = hout[c % 2]
                    for nh in range(2):
                        sl = slice(nh * 512, (nh + 1) * 512)
                        op("dve", lambda e, sl=sl, nh=nh: e.scalar_tensor_tensor(
                            out=ho[:, sl], in0=mb[nh][:], scalar=s1[:, 0:1], in1=prm[:, 1024 + nh * 512:1024 + (nh + 1) * 512],
                            op0=ALU.mult, op1=ALU.mult), [mb[nh], s1, prm], [ho])
                    op("pool", lambda e: e.tensor_tensor(out=ho[:], in0=ho[:], in1=b[:], op=ALU.add), [ho, b], [ho])
                    if DBG_3 < 5:
                        return
                    if last:
                        if c > 0:
                            if DBG_3 == 5:
                                dma("sp", hres_d[s, c * 128:(c + 1) * 128, :], ho[:], reads=[ho])
                            elif DBG_3 == 6:
                                dma("sp", hres_d[s, c * 128:(c + 1) * 128, :], ho[:], reads=[ho])
                                dma("sp", y_d[s, (c - 1) * 128:c * 128, :], ho[:], reads=[ho])
                            elif DBG_3 == 7:
                                dma("pool", y_d[s, (c - 1) * 128:c * 128, :], ho[:], reads=[ho])
                            elif DBG_3 == 8:
                                dma("sp", y_d[s, (c - 1) * 128:c * 128, :], ho[:], reads=[ho])
                            else:
                                dma("sp", y_d[s, (c - 1) * 128:c * 128, :], ho[:], reads=[ho])
                    else:
                        dma("sp", hres_d[s, c * 128:(c + 1) * 128, :], ho[:], reads=[ho])

                for s in range(NSEQ):
                    c_list = list(range(NCH)) if not last else list(range(1, NCH))
                    load(s, c_list[0])
                    if len(c_list) > 1:
                        load(s, c_list[1])
                    chunkD1(s, c_list[0])
                    for i, c in enumerate(c_list):
                        if i + 2 < len(c_list):
                            load(s, c_list[i + 2])
                        if i + 1 < len(c_list):
                            chunkD1(s, c_list[i + 1])
                        chunkD2(s, c)
                K.barrier()
                if last and DBG_3 == 9:
                    with ExitStack() as esd:
                        ct = mk(esd, "ydiag", [128, D], F32)
                        op("pool", lambda e: e.memset(ct[:], 1.0), [], [ct])
                        for s in range(NSEQ):
                            for c in range(1, NCH):
                                dma("sp", y_d[s, (c - 1) * 128:c * 128, :], ct[:], reads=[ct])
                        K.barrier()
                K.barrier()

        for layer in range(DEPTH):
            if DEBUG_STOP == 0:
                break
            phase1(layer)
            if DEBUG_P1_ONLY or DEBUG_STOP == 1:
                break
            phase2(layer)
            if DEBUG_STOP == 2:
                break
            phase3(layer, layer == DEPTH - 1)
        if DBG_YW:
            with ExitStack() as esd:
                ct = mk(esd, "ydiag2", [128, D], F32)
                op("pool", lambda e: e.memset(ct[:], 1.0), [], [ct])
                for s in range(NSEQ):
                    for c in range(1, NCH):
                        dma("sp", y_d[s, (c - 1) * 128:c * 128, :], ct[:], reads=[ct])
                K.barrier()
    return nc


def _consts():
    c = np.zeros((128, CSTW), np.float32)
    p = np.arange(128)
    c[:, C_ID:C_ID + 128] = np.eye(128)
    c[:, C_U:C_U + 128] = (p[:, None] <= p[None, :])
    c[:, C_VGE:C_VGE + 128] = (p[:, None] >= p[None, :])
    c[:, C_SGT:C_SGT + 128] = (p[:, None] > p[None, :])
    c[:, C_REL:C_REL + 128] = (p[None, :] - p[:, None])
    c[:, C_LROW:C_LROW + 128] = p[None, :]
    c[:, C_VALID] = (p >= PADR)
    c[:, C_P] = p
    c[:, C_127P] = 127 - p
    return c


def _rope(L):
    pos = np.arange(L, dtype=np.float32)
    inv = (np.float32(10000.0) ** (-np.arange(0, 64, 2, dtype=np.float32) / np.float32(64))).astype(np.float32)
    ang = (pos[:, None] * inv[None, :]).astype(np.float32)
    cs, sn = np.cos(ang).astype(np.float32), np.sin(ang).astype(np.float32)
    t = np.zeros((L, 256), np.float32)
    t[:, 0:32] = cs; t[:, 32:64] = cs
    t[:, 64:96] = -sn; t[:, 96:128] = sn
    t[:, 128:256] = t[:, 0:128] * np.float32(0.125)
    return t


def _params(DEPTH, norm_mix_pre, norm_mix_post, norm_ffn_pre, norm_ffn_post, conv_w, conv_b, dt_bias, a_log,
            d_skip, ssd_norm, ret_log_decay):
    P = np.zeros((DEPTH, 128, PRMW), np.float32)
    for l in range(DEPTH):
        P[l, :, P_GPRE:P_GPRE + 1024] = norm_mix_pre[l][None, :]
        P[l, :, P_GPOST:P_GPOST + 1024] = norm_mix_post[l][None, :]
        P[l, :, P_FPRE:P_FPRE + 1024] = norm_ffn_pre[l][None, :]
        P[l, :, P_FPOST:P_FPOST + 1024] = norm_ffn_post[l][None, :]
        P[l, :, P_SSDN:P_SSDN + 1024] = ssd_norm[l][None, :]
        P[l, :, P_DSK:P_DSK + 1024] = np.repeat(d_skip[l], 64)[None, :]
        cw = conv_w[l].reshape(5, 12, 128)
        P[l, :, P_CW:P_CW + 60] = cw.transpose(2, 1, 0).reshape(128, 60)
        P[l, :, P_CB:P_CB + 12] = conv_b[l].reshape(12, 128).T
        P[l, :, P_DTB:P_DTB + 32] = dt_bias[l].reshape(32)[None, :]
        P[l, :, P_ALOG:P_ALOG + 32] = a_log[l].reshape(32)[None, :]
        P[l, :, P_LGB:P_LGB + 16] = ret_log_decay[l].reshape(16)[None, :]
        for d in range(2):
            lg = ret_log_decay[l, d]
            P[l, 0:64, P_LGP + 4 * d:P_LGP + 4 * d + 4] = lg[0::2][None, :]
            P[l, 64:128, P_LGP + 4 * d:P_LGP + 4 * d + 4] = lg[1::2][None, :]
    return P


_NC_CACHE = {}


def run(xs_all, meta_tokens, small, w_in, w_out, w_gate, w_up, w_down, n_cores, NSEQ, DEPTH):
    S = xs_all.shape[1]
    NCH = S // 128 + 1
    key = (NSEQ, NCH, DEPTH)
    if key not in _NC_CACHE:
        _NC_CACHE[key] = build(NSEQ, NCH, DEPTH)
    nc = _NC_CACHE[key]
    prm = _params(DEPTH, *small)
    cst = _consts()
    rope = _rope(NCH * 128)
    f = lambda a: np.ascontiguousarray(a, dtype=np.float32)
    in_maps = []
    for i in range(n_cores):
        in_maps.append({"x": f(xs_all[i * NSEQ:(i + 1) * NSEQ]), "meta": f(meta_tokens), "w_in": f(w_in), "w_out": f(w_out),
                        "w_gate": f(w_gate), "w_up": f(w_up), "w_down": f(w_down), "prm": prm, "cst": cst, "rope": rope})
    res = run_bass_kernel_spmd(nc, in_maps, core_ids=list(range(n_cores)))
    return np.concatenate([np.asarray(r["y"]) for r in res.results], axis=0)


def kernel(x_prompt, x_sample, meta_tokens, norm_mix_pre, norm_mix_post, norm_ffn_pre, norm_ffn_post,
           w_in, conv_w, conv_b, dt_bias, a_log, d_skip, ssd_norm, ret_log_decay, w_out, w_gate, w_up, w_down):
    x_prompt = np.asarray(x_prompt); x_sample = np.asarray(x_sample)
    nb = x_prompt.shape[0]
    xs_all = np.concatenate([x_prompt, x_sample], axis=0)
    small = [np.asarray(a, dtype=np.float32) for a in (norm_mix_pre, norm_mix_post, norm_ffn_pre, norm_ffn_post, conv_w,
                                                        conv_b, dt_bias, a_log, d_skip, ssd_norm, ret_log_decay)]
    y = run(xs_all, np.asarray(meta_tokens), small, np.asarray(w_in), np.asarray(w_out), np.asarray(w_gate),
            np.asarray(w_up), np.asarray(w_down), 8, xs_all.shape[0] // 8, np.asarray(w_in).shape[0])
    return (np.ascontiguousarray(y[:nb]), np.ascontiguousarray(y[nb:]))
```

```python
import numpy as np
from contextlib import ExitStack
import concourse.bass as bass
import concourse.mybir as mybir
from concourse.bass_utils import run_bass_kernel_spmd
import ml_dtypes

F32 = mybir.dt.float32
BF16 = mybir.dt.bfloat16
AF = mybir.ActivationFunctionType
ALU = mybir.AluOpType
AX = mybir.AxisListType

D = 1024
NIN = 5664
DFF = 2816
NMETA = 16
PADR = 112
EPS = 1e-6
OZ, OX, OB, OC, ODT, OQ, OK_, OV, OG = 0, 1024, 2048, 2304, 2560, 2592, 3104, 3616, 4640
R_XS, R_ZS, R_GS, R_V, R_YF, R_RF, R_KR, R_BT, R_QT, R_KT, R_BTT, R_CTT = (
    0, 1024, 2048, 3072, 4096, 5120, 6144, 6656, 6912, 7424, 7936, 8192)
RECW = 8448
P_GPRE, P_GPOST, P_FPRE, P_FPOST, P_SSDN, P_DSK, P_CW, P_CB, P_DTB, P_ALOG, P_LGB, P_LGP = (
    0, 1024, 2048, 3072, 4096, 5120, 6144, 6204, 6216, 6248, 6280, 6296)
PRMW = 6304
C_ID, C_U, C_VGE, C_SGT, C_REL, C_LROW, C_VALID, C_P, C_127P = 0, 128, 256, 384, 512, 640, 768, 769, 770
CSTW = 772
ND = 16
DEBUG_P1_ONLY = False
DEBUG_STOP = 99
DBG_DC = 99
DBG_A = 99
DBG_S = 99
DBG_3 = 99
DBG_YW = 0


class Tile:
    def __init__(self, h):
        self.h = h
        self.w = None
        self.r = {}

    def __getitem__(self, k):
        return self.h[k]


class Eng:
    def __init__(self, name, h, sem):
        self.name, self.h, self.sem = name, h, sem
        self.cnt = 0
        self.seen = {}


class Kern:
    def __init__(self, nc, es):
        self.nc, self.es = nc, es
        self.E = {}
        for n, a in (("pe", "tensor"), ("act", "scalar"), ("dve", "vector"), ("pool", "gpsimd"), ("sp", "sync")):
            self.E[n] = Eng(n, getattr(nc, a), es.enter_context(nc.semaphore("s_" + n)))
        self.dq = {}
        for q in ("sp", "pool"):
            sems = [es.enter_context(nc.semaphore("d_%s%d" % (q, i))) for i in range(ND)]
            self.dq[q] = dict(sems=sems, vals=[0] * ND, nxt=0)
        self.bsem = es.enter_context(nc.semaphore("bar"))
        self.bcnt = 0
        self.nid = 0

    def _wait(self, eng, toks):
        best = {}
        for key, sem, val in toks:
            if key not in best or best[key][1] < val:
                best[key] = (sem, val)
        for key, (sem, val) in best.items():
            if eng.seen.get(key, 0) >= val:
                continue
            if key == eng.name:
                if eng.name == "pe":
                    continue
            eng.h.wait_ge(sem, val)
            eng.seen[key] = val

    @staticmethod
    def _deps(reads, writes):
        toks = []
        for t in reads:
            if t.w:
                toks.append(t.w)
        for t in writes:
            if t.w:
                toks.append(t.w)
            toks.extend(t.r.values())
        return toks

    def _mark(self, tok, reads, writes):
        for t in reads:
            t.r[tok[0]] = tok
        for t in writes:
            t.w = tok
            t.r = {}

    def op(self, en, fn, reads=(), writes=()):
        eng = self.E[en]
        self._wait(eng, self._deps(reads, writes))
        ins = fn(eng.h)
        eng.cnt += 1
        ins.then_inc(eng.sem, 1)
        self._mark((en, eng.sem, eng.cnt), reads, writes)

    def mm(self, items, reads=(), writes=()):
        eng = self.E["pe"]
        self._wait(eng, self._deps(reads, writes))
        ins = None
        for (o, l, r, st, sp) in items:
            ins = eng.h.matmul(o, lhsT=l, rhs=r, start=st, stop=sp)
        eng.cnt += 1
        ins.then_inc(eng.sem, 1)
        self._mark(("pe", eng.sem, eng.cnt), reads, writes)

    def tr(self, items, ident, reads=(), writes=()):
        eng = self.E["pe"]
        self._wait(eng, self._deps(list(reads) + [ident], writes))
        ins = None
        for (o, i) in items:
            ins = eng.h.transpose(out=o, in_=i, identity=ident[:])
        eng.cnt += 1
        ins.then_inc(eng.sem, 1)
        self._mark(("pe", eng.sem, eng.cnt), list(reads) + [ident], writes)

    def dma(self, q, out_ap, in_ap, reads=(), writes=()):
        eng = self.E[q]
        d = self.dq[q]
        j = d["nxt"]
        d["nxt"] = (j + 1) % ND
        key = "d_%s%d" % (q, j)
        toks = self._deps(reads, writes)
        if d["vals"][j] > 0:
            toks.append((key, d["sems"][j], d["vals"][j]))
        self._wait(eng, toks)
        ins = eng.h.dma_start(out=out_ap, in_=in_ap)
        d["vals"][j] += 16
        ins.then_inc(d["sems"][j], 16)
        self._mark((key, d["sems"][j], d["vals"][j]), reads, writes)

    def barrier(self):
        sp = self.E["sp"]
        toks = [(n, e.sem, e.cnt) for n, e in self.E.items() if e.cnt > 0]
        for q, d in self.dq.items():
            for j in range(ND):
                if d["vals"][j] > 0:
                    toks.append(("d_%s%d" % (q, j), d["sems"][j], d["vals"][j]))
        self._wait(sp, toks)
        self.bcnt += 1
        sp.h.sem_inc(self.bsem, 1)
        for n, e in self.E.items():
            if n != "sp":
                e.h.wait_ge(self.bsem, self.bcnt)
            for key, sem, val in toks:
                if e.seen.get(key, 0) < val:
                    e.seen[key] = val


def build(NSEQ, NCH, DEPTH):
    nc = bass.Bass("TRN2", target_bir_lowering=False)
    L = NCH * 128
    S = L - 128
    x_d = nc.dram_tensor("x", [NSEQ, S, D], F32, kind="ExternalInput").ap()
    meta_d = nc.dram_tensor("meta", [NMETA, D], F32, kind="ExternalInput").ap()
    win_d = nc.dram_tensor("w_in", [DEPTH, D, NIN], F32, kind="ExternalInput").ap()
    wout_d = nc.dram_tensor("w_out", [DEPTH, 2 * D, D], F32, kind="ExternalInput").ap()
    wg_d = nc.dram_tensor("w_gate", [DEPTH, D, DFF], F32, kind="ExternalInput").ap()
    wu_d = nc.dram_tensor("w_up", [DEPTH, D, DFF], F32, kind="ExternalInput").ap()
    wd_d = nc.dram_tensor("w_down", [DEPTH, DFF, D], F32, kind="ExternalInput").ap()
    prm_d = nc.dram_tensor("prm", [DEPTH, 128, PRMW], F32, kind="ExternalInput").ap()
    cst_d = nc.dram_tensor("cst", [128, CSTW], F32, kind="ExternalInput").ap()
    rope_d = nc.dram_tensor("rope", [L, 256], F32, kind="ExternalInput").ap()
    y_d = nc.dram_tensor("y", [NSEQ, S, D], F32, kind="ExternalOutput").ap()
    hres_d = nc.dram_tensor("hres", [NSEQ, L, D], F32, kind="Internal").ap()
    hmid_d = nc.dram_tensor("hmid", [NSEQ, L, D], F32, kind="Internal").ap()
    rec_d = nc.dram_tensor("rec", [NSEQ, NCH, 128, RECW], BF16, kind="Internal").ap()
    dts_d = nc.dram_tensor("dts", [NSEQ, NCH, 128, 32], F32, kind="Internal").ap()

    with ExitStack() as es0:
        K = Kern(nc, es0)
        op, mm, tr, dma = K.op, K.mm, K.tr, K.dma

        def mk(es, name, shape, dt):
            K.nid += 1
            return Tile(es.enter_context(nc.sbuf_tensor("%s_%d" % (name, K.nid), shape, dt)))

        cst = mk(es0, "cst", [128, CSTW], F32)
        identb = mk(es0, "identb", [128, 128], BF16)
        onesb = mk(es0, "onesb", [128, 128], BF16)
        Ub = mk(es0, "Ub", [128, 128], BF16)
        Vb = mk(es0, "Vb", [128, 128], BF16)
        MNf = mk(es0, "MNf", [128, 4, 128], BF16)
        MNb = mk(es0, "MNb", [128, 4, 128], BF16)
        est = ExitStack()
        zt = mk(est, "zt", [128, D], F32)
        banks = [Tile(es0.enter_context(nc.psum_tensor("bank%d" % i, [128, 512], F32))) for i in range(8)]
        bstate = dict(i=0)

        def bank():
            b = banks[bstate["i"] % 8]
            bstate["i"] += 1
            return b

        dma("sp", cst[:], cst_d[:, :], writes=[cst])
        op("dve", lambda e: e.tensor_copy(out=identb[:], in_=cst[:, C_ID:C_ID + 128]), [cst], [identb])
        op("dve", lambda e: e.tensor_copy(out=Ub[:], in_=cst[:, C_U:C_U + 128]), [cst], [Ub])
        op("dve", lambda e: e.tensor_copy(out=Vb[:], in_=cst[:, C_VGE:C_VGE + 128]), [cst], [Vb])
        op("pool", lambda e: e.memset(onesb[:], 1.0), [], [onesb])
        op("dve", lambda e: e.tensor_scalar(out=MNf[:], in0=cst[:, C_SGT:C_SGT + 128].unsqueeze(1).broadcast_to([128, 4, 128]),
                                            scalar1=-30000.0, scalar2=None, op0=ALU.mult), [cst], [MNf])
        op("dve", lambda e: e.tensor_scalar(out=MNb[:], in0=cst[:, C_U:C_U + 128].unsqueeze(1).broadcast_to([128, 4, 128]),
                                            scalar1=-30000.0, scalar2=None, op0=ALU.mult), [cst], [MNb])
        op("pool", lambda e: e.memset(zt[:], 0.0), [], [zt])
        for s in range(NSEQ):
            dma("sp", hres_d[s, 0:PADR, :], zt[0:PADR, :], reads=[zt])
            dma("sp", hres_d[s, PADR:128, :], meta_d[:, :])
        K.barrier()
        est.close()

        def h_src(layer, s, c):
            if layer == 0 and c > 0:
                return x_d[s, (c - 1) * 128:c * 128, :]
            return hres_d[s, c * 128:(c + 1) * 128, :]

        def load_w(es, name, src2d, kt, ncols):
            w = mk(es, name, [128, kt, ncols], BF16)
            for k in range(kt):
                c0 = 0
                while c0 < ncols:
                    cw = min(2048, ncols - c0)
                    dma("pool", w[:, k, c0:c0 + cw], src2d[k * 128:(k + 1) * 128, c0:c0 + cw], writes=[w])
                    c0 += cw
            return w

        def rms_scale(ss, tmp):
            op("dve", lambda e: e.tensor_scalar(out=tmp[:], in0=ss[:], scalar1=1.0 / D, scalar2=EPS,
                                                op0=ALU.mult, op1=ALU.add), [ss], [tmp])
            op("act", lambda e: e.sqrt(out=tmp[:], in_=tmp[:]), [tmp], [tmp])
            op("dve", lambda e: e.reciprocal(out=ss[:], in_=tmp[:]), [tmp], [ss])

        def dir_consts(es, prm, PO, d):
            lgB = prm[:, PO + P_LGB + 8 * d:PO + P_LGB + 8 * d + 8]
            lgP = prm[:, PO + P_LGP + 4 * d:PO + P_LGP + 4 * d + 4]
            dmT = mk(es, "dmT", [128, 8, 128], F32)
            xiT = mk(es, "xiT", [128, 4, 128], F32)
            zeta = mk(es, "zeta", [128, 8], F32)
            gch = mk(es, "gch", [128, 4], F32)
            negA = mk(es, "negA", [128, 16], F32)
            sc = mk(es, "sc", [128, 16], F32)
            if d == 0:
                op("dve", lambda e: e.tensor_copy(out=sc[:, 0:8], in_=lgB), [prm], [sc])
                op("dve", lambda e: e.tensor_copy(out=sc[:, 8:12], in_=lgP), [prm], [sc])
                op("dve", lambda e: e.tensor_copy(out=sc[:, 12:16], in_=lgP), [prm], [sc])
            else:
                op("dve", lambda e: e.tensor_scalar(out=sc[:, 0:8], in0=lgB, scalar1=-1.0, scalar2=None,
                                                    op0=ALU.mult), [prm], [sc])
                op("dve", lambda e: e.tensor_scalar(out=sc[:, 8:12], in0=lgP, scalar1=-1.0, scalar2=None,
                                                    op0=ALU.mult), [prm], [sc])
                op("dve", lambda e: e.tensor_scalar(out=sc[:, 12:16], in0=lgP, scalar1=128.0, scalar2=None,
                                                    op0=ALU.mult), [prm], [sc])
            msk = cst[:, C_U:C_U + 128] if d == 0 else cst[:, C_SGT:C_SGT + 128]
            if DBG_DC >= 1:
                for i in range(8):
                    h = 2 * (i % 4) + i // 4
                    op("act", lambda e, h=h, i=i: e.activation(out=dmT[:, i, :], in_=cst[:, C_REL:C_REL + 128], func=AF.Exp,
                                                               scale=sc[:, h:h + 1]), [cst, sc], [dmT])
            if DBG_DC >= 2:
                op("dve", lambda e: e.tensor_tensor(out=dmT[:], in0=dmT[:],
                                                    in1=msk.unsqueeze(1).broadcast_to([128, 8, 128]), op=ALU.mult),
                   [dmT, cst], [dmT])
            if DBG_DC >= 3:
                for j in range(4):
                    op("act", lambda e, j=j: e.activation(out=xiT[:, j, :], in_=cst[:, C_LROW:C_LROW + 128], func=AF.Exp,
                                                          scale=sc[:, 8 + j:9 + j], bias=sc[:, 12 + j:13 + j]),
                       [cst, sc], [xiT])
            zc = C_127P if d == 0 else C_P
            if DBG_DC >= 4:
                op("act", lambda e: e.activation(out=zeta[:], in_=lgB, func=AF.Exp, scale=cst[:, zc:zc + 1]),
                   [prm, cst], [zeta])
            if DBG_DC >= 5:
                op("act", lambda e: e.activation(out=gch[:], in_=lgP, func=AF.Exp, scale=128.0), [prm], [gch])
            if DBG_DC >= 6:
                op("act", lambda e: e.activation(out=negA[:], in_=prm[:, PO + P_ALOG + 16 * d:PO + P_ALOG + 16 * d + 16],
                                                 func=AF.Exp), [prm], [negA])
                op("dve", lambda e: e.tensor_scalar(out=negA[:], in0=negA[:], scalar1=-1.0, scalar2=None, op0=ALU.mult),
                   [negA], [negA])
            return dmT, xiT, zeta, gch, negA

        def scan_chunk(T, d, rec, dt, on_y, on_o):
            (dmT, xiT, zeta, gch, negA) = T["dc"]
            st, stb, Rs, Rb = T["state"], T["stateb"], T["R"], T["Rb"]
            dAb, G, nacs, cbm, E, eaT, MT, yin, xdt, xw, decB = (T[k] for k in (
                "dAb", "G", "nacs", "cbm", "E", "eaT", "MT", "yin", "xdt", "xw", "decB"))
            sTm, qxT, kz = T["sTm"], T["qxT"], T["kz"]
            Mb = Ub if d == 0 else Vb
            MN = MNf if d == 0 else MNb
            wend = T["wend"]
            mcol = C_U if d == 0 else C_SGT
            lsel = 127 if d == 0 else 0
            op("dve", lambda e: e.tensor_tensor(out=dAb[:], in0=dt[:, 16 * d:16 * d + 16], in1=negA[:], op=ALU.mult),
               [dt, negA], [dAb])
            op("pool", lambda e: e.tensor_tensor(out=G[:], in0=dAb[:].unsqueeze(2).broadcast_to([128, 16, 128]),
                                                 in1=Mb[:].unsqueeze(1).broadcast_to([128, 16, 128]), op=ALU.mult),
               [dAb, Mb], [G])
            bk = bank()
            mm([(bk[:, 0:16], Mb[:], dAb[:], True, True), (bk[:, 16:32], onesb[:], dAb[:], True, True)],
               [Mb, onesb, dAb], [bk])
            op("dve", lambda e, bk=bk: e.tensor_scalar(out=nacs[:], in0=bk[:, 0:16], scalar1=-1.0, scalar2=None,
                                                       op0=ALU.mult), [bk], [nacs])
            op("dve", lambda e, bk=bk: e.tensor_tensor(out=wend[:], in0=bk[:, 16:32], in1=nacs[:], op=ALU.add),
               [bk, nacs], [wend])
            op("act", lambda e, bk=bk: e.activation(out=eaT[:], in_=bk[:, 0:16], func=AF.Exp), [bk], [eaT])
            op("act", lambda e, bk=bk: e.activation(out=decB[:], in_=bk[:, 16:32], func=AF.Exp), [bk], [decB])
            op("act", lambda e: e.activation(out=wend[:], in_=wend[:], func=AF.Exp), [wend], [wend])
            if DBG_S < 2:
                return
            op("pool", lambda e: e.tensor_tensor(
                out=xdt[:].rearrange("p (h d) -> p h d", d=64),
                in0=rec[:, R_XS:R_XS + 1024].rearrange("p (h d) -> p h d", d=64),
                in1=dt[:, 16 * d:16 * d + 16].unsqueeze(2).broadcast_to([128, 16, 64]), op=ALU.mult),
               [rec, dt], [xdt])
            bk = bank()
            mm([(bk[:, g * 128:(g + 1) * 128], rec[:, R_BTT + g * 128:R_BTT + (g + 1) * 128],
                 rec[:, R_CTT + g * 128:R_CTT + (g + 1) * 128], True, True) for g in range(2)], [rec], [bk])
            op("dve", lambda e: e.tensor_tensor(
                out=cbm[:], in0=bk[:, 0:256].rearrange("p (g l) -> p g l", g=2),
                in1=cst[:, mcol:mcol + 128].unsqueeze(1).broadcast_to([128, 2, 128]), op=ALU.mult),
               [bk, cst], [cbm])
            if DBG_S < 3:
                return
            for g in range(2):
                for hf in range(2):
                    h0 = g * 8 + hf * 4
                    bk = bank()
                    mm([(bk[:], onesb[:], G[:, h0:h0 + 4, :].rearrange("p h l -> p (h l)"), True, False),
                        (bk[:], identb[:], MN[:].rearrange("p h l -> p (h l)"), False, True)],
                       [onesb, G, identb, MN], [bk])
                    for hh in range(4):
                        op("act", lambda e, hh=hh, bk=bk, h0=h0: e.activation(
                            out=E[:, h0 + hh, :], in_=bk[:, hh * 128:(hh + 1) * 128], func=AF.Exp,
                            bias=nacs[:, h0 + hh:h0 + hh + 1]), [bk, nacs], [E])
                    op("dve", lambda e, h0=h0, g=g: e.tensor_tensor(
                        out=MT[:, h0:h0 + 4, :], in0=E[:, h0:h0 + 4, :],
                        in1=cbm[:, g, :].unsqueeze(1).broadcast_to([128, 4, 128]), op=ALU.mult),
                       [E, cbm], [MT])
            if DBG_S < 4:
                return
            op("dve", lambda e: e.tensor_tensor(
                out=xw[:].rearrange("p (h d) -> p h d", d=64), in0=xdt[:].rearrange("p (h d) -> p h d", d=64),
                in1=wend[:].unsqueeze(2).broadcast_to([128, 16, 64]), op=ALU.mult), [xdt, wend], [xw])
            ybanks = []
            for g in range(2):
                bk = bank()
                items = []
                for h in range(8):
                    hg = g * 8 + h
                    items.append((bk[:, h * 64:(h + 1) * 64], MT[:, hg, :], xdt[:, hg * 64:(hg + 1) * 64], True, True))
                mm(items, [MT, xdt], [bk])
                bo = bank()
                mm([(bo[:], rec[:, R_CTT + g * 128:R_CTT + (g + 1) * 128], stb[:, g, :], True, True)], [rec, stb], [bo])
                yi = yin[g]
                op("act", lambda e, bo=bo, yi=yi: e.copy(out=yi[:], in_=bo[:]), [bo], [yi])
                op("dve", lambda e, yi=yi, g=g: e.tensor_tensor(
                    out=yi[:].rearrange("p (h d) -> p h d", d=64), in0=yi[:].rearrange("p (h d) -> p h d", d=64),
                    in1=eaT[:, g * 8:(g + 1) * 8].unsqueeze(2).broadcast_to([128, 8, 64]), op=ALU.mult), [yi, eaT], [yi])
                on_y(g, bk, yi)
            for g in range(2):
                bk = bank()
                mm([(bk[:], rec[:, R_BT + g * 128:R_BT + (g + 1) * 128], xw[:, g * 512:(g + 1) * 512], True, True)],
                   [rec, xw], [bk])
                op("dve", lambda e: e.tensor_tensor(
                    out=st[:, g, :].rearrange("p (h d) -> p h d", d=64),
                    in0=st[:, g, :].rearrange("p (h d) -> p h d", d=64),
                    in1=decB[:, g * 8:(g + 1) * 8].unsqueeze(2).broadcast_to([128, 8, 64]), op=ALU.mult),
                   [st, decB], [st])
                op("dve", lambda e: e.tensor_tensor(out=st[:, g, :], in0=st[:, g, :], in1=bk[:], op=ALU.add),
                   [st, bk], [st])
            op("act", lambda e: e.copy(out=stb[:].rearrange("p g n -> p (g n)"),
                                       in_=st[:].rearrange("p g n -> p (g n)")), [st], [stb])
            if DBG_S < 5:
                return
            op("pool", lambda e: e.tensor_tensor(
                out=qxT[:], in0=rec[:, R_QT:R_QT + 512].rearrange("p (j l) -> p j l", j=4), in1=xiT[:], op=ALU.mult),
               [rec, xiT], [qxT])
            op("pool", lambda e: e.tensor_tensor(
                out=kz[:].rearrange("p (h d) -> p h d", d=64),
                in0=rec[:, R_KR:R_KR + 512].rearrange("p (h d) -> p h d", d=64),
                in1=zeta[:].unsqueeze(2).broadcast_to([128, 8, 64]), op=ALU.mult), [rec, zeta], [kz])
            obanks = []
            for b in range(2):
                bk = bank()
                items = []
                for j in range(4):
                    items.append((bk[:, j * 128:(j + 1) * 128],
                                  rec[b * 64:(b + 1) * 64, R_KT + j * 128:R_KT + (j + 1) * 128],
                                  rec[b * 64:(b + 1) * 64, R_QT + j * 128:R_QT + (j + 1) * 128], True, True))
                mm(items, [rec], [bk])
                op("dve", lambda e, b=b, bk=bk: e.tensor_tensor(
                    out=sTm[:, b * 4:b * 4 + 4, :], in0=bk[:].rearrange("p (h l) -> p h l", h=4),
                    in1=dmT[:, b * 4:b * 4 + 4, :], op=ALU.mult), [bk, dmT], [sTm])
            for b in range(2):
                bk = bank()
                items = []
                for j in range(4):
                    h = 2 * j + b
                    items.append((bk[:, j * 128:(j + 1) * 128], sTm[:, b * 4 + j, :],
                                  rec[:, R_V + h * 128:R_V + (h + 1) * 128], True, False))
                    items.append((bk[:, j * 128:(j + 1) * 128], qxT[b * 64:(b + 1) * 64, j, :],
                                  Rb[b * 64:(b + 1) * 64, j, :], False, True))
                mm(items, [sTm, rec, qxT, Rb], [bk])
                on_o(b, bk)
            if DBG_S < 6:
                return
            bk = bank()
            items = []
            for h in range(8):
                j, b = h // 2, h % 2
                items.append((bk[b * 64:(b + 1) * 64, j * 128:(j + 1) * 128], kz[:, h * 64:(h + 1) * 64],
                              rec[:, R_V + h * 128:R_V + (h + 1) * 128], True, True))
            mm(items, [kz, rec], [bk])
            op("dve", lambda e: e.tensor_tensor(out=Rs[:], in0=Rs[:],
                                                in1=gch[:].unsqueeze(2).broadcast_to([128, 4, 128]), op=ALU.mult),
               [Rs, gch], [Rs])
            op("dve", lambda e: e.tensor_tensor(out=Rs[:].rearrange("p j e -> p (j e)"),
                                                in0=Rs[:].rearrange("p j e -> p (j e)"), in1=bk[:], op=ALU.add),
               [Rs, bk], [Rs])
            op("act", lambda e: e.copy(out=Rb[:].rearrange("p j e -> p (j e)"),
                                       in_=Rs[:].rearrange("p j e -> p (j e)")), [Rs], [Rb])

        def scan_tiles(es):
            T = {}
            T["state"] = mk(es, "state", [128, 2, 512], F32)
            T["stateb"] = mk(es, "stateb", [128, 2, 512], BF16)
            T["R"] = mk(es, "R", [128, 4, 128], F32)
            T["Rb"] = mk(es, "Rb", [128, 4, 128], BF16)
            T["dAb"] = mk(es, "dAb", [128, 16], BF16)
            T["G"] = mk(es, "G", [128, 16, 128], BF16)
            T["nacs"] = mk(es, "nacs", [128, 16], F32)
            T["cbm"] = mk(es, "cbm", [128, 2, 128], F32)
            T["E"] = mk(es, "E", [128, 16, 128], BF16)
            T["eaT"] = mk(es, "eaT", [128, 16], F32)
            T["yin"] = [mk(es, "yin", [128, 512], F32) for _ in range(2)]
            T["MT"] = mk(es, "MT", [128, 16, 128], BF16)
            T["xdt"] = mk(es, "xdt", [128, 1024], BF16)
            T["xw"] = mk(es, "xw", [128, 1024], BF16)
            T["decB"] = mk(es, "decB", [128, 16], F32)
            T["wend"] = mk(es, "wend", [128, 16], F32)
            T["sTm"] = mk(es, "sTm", [128, 8, 128], BF16)
            T["qxT"] = mk(es, "qxT", [128, 4, 128], BF16)
            T["kz"] = mk(es, "kz", [128, 512], BF16)
            return T

        def reset_state(T):
            op("pool", lambda e: e.memset(T["state"][:], 0.0), [], [T["state"]])
            op("pool", lambda e: e.memset(T["stateb"][:], 0.0), [], [T["stateb"]])
            op("pool", lambda e: e.memset(T["R"][:], 0.0), [], [T["R"]])
            op("pool", lambda e: e.memset(T["Rb"][:], 0.0), [], [T["Rb"]])

        def phase1(layer):
            with ExitStack() as es:
                prm = mk(es, "prm1", [128, PRMW - P_CW + 1024], F32)
                PO = 1024 - P_CW
                dma("sp", prm[:, 0:1024], prm_d[layer, :, P_GPRE:P_GPRE + 1024], writes=[prm])
                dma("sp", prm[:, 1024:], prm_d[layer, :, P_CW:PRMW], writes=[prm])

                if DEBUG_STOP == 8:
                    K.barrier()
                    return
                w_in = load_w(es, "w_in", win_d[layer], 8, NIN)
                if DEBUG_STOP == 9:
                    K.barrier()
                    return
                T = scan_tiles(es)
                T["dc"] = dir_consts(es, prm, PO, 0)
                if DEBUG_STOP == 10:
                    K.barrier()
                    return
                hb = [mk(es, "hb", [128, D], F32) for _ in range(2)]
                junk = mk(es, "junk", [128, D], F32)
                ss = [mk(es, "ss", [128, 1], F32) for _ in range(2)]
                tmp1 = [mk(es, "tmp1", [128, 1], F32) for _ in range(2)]
                u_bf = mk(es, "u_bf", [128, D], BF16)
                ext = [mk(es, "ext", [128, 8, 192], BF16) for _ in range(3)]
                acc = [[mk(es, "acc", [128, 128], F32) for _ in range(3)] for _ in range(2)]
                xT = mk(es, "xT", [128, 8, 128], BF16)
                recs1 = [mk(es, "rec", [128, RECW], BF16) for _ in range(2)]
                ropet = [mk(es, "ropet", [128, 256], F32) for _ in range(2)]
                dtts = [mk(es, "dtt", [128, 32], F32) for _ in range(2)]
                dtmp = mk(es, "dtmp", [128, 32], F32)
                q_rot = mk(es, "q_rot", [128, 512], BF16)
                pT = Tile(None)

                def stageA(s, c):
                    b = hb[c % 2]
                    s1, t1 = ss[c % 2], tmp1[c % 2]
                    dma("sp", b[:], h_src(layer, s, c), writes=[b])
                    op("act", lambda e: e.activation(out=junk[:], in_=b[:], func=AF.Square, accum_out=s1[:]),
                       [b], [junk, s1])
                    rms_scale(s1, t1)
                    if c == 0:
                        op("dve", lambda e: e.tensor_tensor(out=s1[:], in0=s1[:], in1=cst[:, C_VALID:C_VALID + 1],
                                                            op=ALU.mult), [s1, cst], [s1])
                    if DBG_A < 2:
                        return
                    op("dve", lambda e: e.scalar_tensor_tensor(out=u_bf[:], in0=b[:], scalar=s1[:, 0:1],
                                                               in1=prm[:, 0:1024], op0=ALU.mult, op1=ALU.mult),
                       [b, s1, prm], [u_bf])
                    if DBG_A < 3:
                        return
                    bk = bank()
                    bkb = bk[:].bitcast(BF16).rearrange("p (k t) -> p k t", k=8)
                    tr([(bkb[:, k, :], u_bf[:, k * 128:(k + 1) * 128]) for k in range(8)], identb, [u_bf], [bk])
                    if DBG_A < 4:
                        return
                    e_c = ext[c % 3]
                    op("act", lambda e: e.copy(out=e_c[:, :, 32:160], in_=bkb), [bk], [e_c])
                    if DBG_A < 5:
                        return
                    if c > 0:
                        e_p = ext[(c - 1) % 3]
                        op("pool", lambda e: e.tensor_copy(out=e_p[:, :, 160:192], in_=e_c[:, :, 32:64]), [e_c], [e_p])
                        op("pool", lambda e: e.tensor_copy(out=e_c[:, :, 0:32], in_=e_p[:, :, 128:160]), [e_p], [e_c])
                    else:
                        op("pool", lambda e: e.memset(e_c[:, :, 0:32], 0.0), [], [e_c])
                    if c == NCH - 1:
                        op("pool", lambda e: e.memset(e_c[:, :, 160:192], 0.0), [], [e_c])

                def proj_tok(e_c, c0, n):
                    bk = bank()
                    mm([(bk[:, 0:n], e_c[:, k, 32:160], w_in[:, k, c0:c0 + n], k == 0, k == 7) for k in range(8)],
                       [e_c, w_in], [bk])
                    return bk

                def stageB1(s, c):
                    rec = recs1[c % 2]
                    dtt = dtts[c % 2]
                    e_c = ext[c % 3]
                    rp = ropet[c % 2]
                    dma("sp", rp[:], rope_d[c * 128:(c + 1) * 128, :], writes=[rp])
                    bk = proj_tok(e_c, ODT, 32)
                    op("dve", lambda e: e.tensor_tensor(out=dtmp[:], in0=bk[:, 0:32],
                                                        in1=prm[:, PO + P_DTB:PO + P_DTB + 32], op=ALU.add),
                       [bk, prm], [dtmp])
                    op("act", lambda e: e.activation(out=dtmp[:], in_=dtmp[:], func=AF.Exp), [dtmp], [dtmp])
                    op("act", lambda e: e.activation(out=dtt[:], in_=dtmp[:], func=AF.Ln, bias=1.0), [dtmp], [dtt])
                    if c == 0:
                        op("dve", lambda e: e.tensor_scalar(out=dtt[:], in0=dtt[:], scalar1=cst[:, C_VALID:C_VALID + 1],
                                                            scalar2=None, op0=ALU.mult), [dtt, cst], [dtt])
                    dma("sp", dts_d[s, c], dtt[:], reads=[dtt])
                    for (c0, r0, fn) in ((OZ, R_ZS, AF.Silu), (OZ + 512, R_ZS + 512, AF.Silu),
                                         (OG, R_GS, AF.Silu), (OG + 512, R_GS + 512, AF.Silu),
                                         (OV, R_V, AF.Copy), (OV + 512, R_V + 512, AF.Copy)):
                        bk = proj_tok(e_c, c0, 512)
                        op("act", lambda e, r0=r0, fn=fn, bk=bk: e.activation(out=rec[:, r0:r0 + 512], in_=bk[:], func=fn),
                           [bk], [rec])
                    for qi, (c0, dst, r0) in enumerate(((OQ, q_rot, 0), (OK_, rec, R_KR))):
                        bk = proj_tok(e_c, c0, 512)
                        b3 = bk[:].rearrange("p (h d) -> p h d", d=64)
                        cc = rp[:, qi * 128:qi * 128 + 64]
                        sg = rp[:, qi * 128 + 64:qi * 128 + 128]
                        A3 = junk[:, 0:512].rearrange("p (h d) -> p h d", d=64)
                        B3 = junk[:, 512:1024].rearrange("p (h d) -> p h d", d=64)
                        op("dve", lambda e: e.tensor_tensor(out=A3, in0=b3, in1=cc.unsqueeze(1).broadcast_to([128, 8, 64]),
                                                            op=ALU.mult), [bk, rp], [junk])
                        op("dve", lambda e: e.tensor_tensor(out=B3[:, :, 0:32], in0=b3[:, :, 32:64],
                                                            in1=sg[:, 0:32].unsqueeze(1).broadcast_to([128, 8, 32]),
                                                            op=ALU.mult), [bk, rp], [junk])
                        op("dve", lambda e: e.tensor_tensor(out=B3[:, :, 32:64], in0=b3[:, :, 0:32],
                                                            in1=sg[:, 32:64].unsqueeze(1).broadcast_to([128, 8, 32]),
                                                            op=ALU.mult), [bk, rp], [junk])
                        op("pool", lambda e, dst=dst, r0=r0: e.tensor_tensor(out=dst[:, r0:r0 + 512], in0=junk[:, 0:512],
                                                                             in1=junk[:, 512:1024], op=ALU.add),
                           [junk], [dst])
                    bk = bank()
                    bkb = bk[:].bitcast(BF16).rearrange("p (k t) -> p k t", k=8)
                    tr([(bkb[:, j, :], q_rot[:, j * 128:(j + 1) * 128]) for j in range(4)] +
                       [(bkb[:, 4 + j, :], rec[:, R_KR + j * 128:R_KR + (j + 1) * 128]) for j in range(4)],
                       identb, [q_rot, rec], [bk])
                    op("act", lambda e: e.copy(out=rec[:, R_QT:R_QT + 1024], in_=bk[:].bitcast(BF16)), [bk], [rec])
                    for jb in range(4):
                        bk = bank()
                        b3 = bk[:, 0:396].rearrange("p (j t) -> p j t", j=3)
                        items = []
                        for jj in range(3):
                            j = jb * 3 + jj
                            for k in range(8):
                                items.append((b3[:, jj, :], w_in[:, k, OX + j * 128:OX + (j + 1) * 128],
                                              e_c[:, k, 30:162], k == 0, k == 7))
                        mm(items, [e_c, w_in], [bk])
                        accs = acc[jb % 2]
                        for t in range(5):
                            for jj in range(3):
                                j = jb * 3 + jj
                                cw = PO + P_CW + j * 5
                                a = accs[jj]
                                if t == 0:
                                    op("dve", lambda e, jj=jj, cw=cw, a=a: e.tensor_scalar(
                                        out=a[:], in0=b3[:, jj, 0:128], scalar1=prm[:, cw:cw + 1], scalar2=None,
                                        op0=ALU.mult), [bk, prm], [a])
                                else:
                                    op("dve", lambda e, jj=jj, cw=cw, t=t, a=a: e.scalar_tensor_tensor(
                                        out=a[:], in0=b3[:, jj, t:t + 128], scalar=prm[:, cw + t:cw + t + 1],
                                        in1=a[:], op0=ALU.mult, op1=ALU.add), [bk, prm, a], [a])
                        for jj in range(3):
                            j = jb * 3 + jj
                            cb = PO + P_CB + j
                            if j < 8:
                                dst_t, dst_ap = xT, xT[:, j, :]
                            elif j < 10:
                                dst_t, dst_ap = rec, rec[:, R_BTT + (j - 8) * 128:R_BTT + (j - 7) * 128]
                            else:
                                dst_t, dst_ap = rec, rec[:, R_CTT + (j - 10) * 128:R_CTT + (j - 9) * 128]
                            a = accs[jj]
                            op("act", lambda e, jj=jj, cb=cb, dst_ap=dst_ap, a=a: e.activation(
                                out=dst_ap, in_=a[:], func=AF.Silu, bias=prm[:, cb:cb + 1]), [a, prm], [dst_t])
                    bk = bank()
                    bkb = bk[:].bitcast(BF16).rearrange("p (k t) -> p k t", k=8)
                    tr([(bkb[:, j, :], xT[:, j, :]) for j in range(8)], identb, [xT], [bk])
                    op("act", lambda e: e.copy(out=rec[:, R_XS:R_XS + 1024], in_=bk[:].bitcast(BF16)), [bk], [rec])
                    bk = bank()
                    bkb = bk[:].bitcast(BF16)
                    tr([(bkb[:, g * 128:(g + 1) * 128], rec[:, R_BTT + g * 128:R_BTT + (g + 1) * 128]) for g in range(2)],
                       identb, [rec], [bk])
                    op("dve", lambda e: e.tensor_copy(out=rec[:, R_BT:R_BT + 256], in_=bkb[:, 0:256]), [bk], [rec])

                def stageB2(s, c):
                    rec = recs1[c % 2]
                    dtt = dtts[c % 2]

                    def on_y(g, bk, yi):
                        op("dve", lambda e: e.tensor_tensor(out=rec[:, R_YF + g * 512:R_YF + (g + 1) * 512],
                                                            in0=bk[:], in1=yi[:], op=ALU.add), [bk, yi], [rec])

                    def on_o(b_, bk):
                        op("act", lambda e: e.copy(out=rec[:, R_RF + b_ * 512:R_RF + (b_ + 1) * 512], in_=bk[:]),
                           [bk], [rec])
                    scan_chunk(T, 0, rec, dtt, on_y, on_o)
                    dma("sp", rec_d[s, c], rec[:], reads=[rec])

                for s in range(NSEQ):
                    reset_state(T)
                    stageA(s, 0)
                    if NCH > 1:
                        stageA(s, 1)
                    stageB1(s, 0)
                    for c in range(NCH):
                        if c + 2 < NCH:
                            stageA(s, c + 2)
                        if c + 1 < NCH:
                            stageB1(s, c + 1)
                        stageB2(s, c)
                K.barrier()

        def phase2(layer):
            with ExitStack() as es:
                prm = mk(es, "prm2", [128, 3 * 1024 + PRMW - P_CW], F32)
                PO = 3072 - P_CW
                dma("sp", prm[:, 0:1024], prm_d[layer, :, P_GPOST:P_GPOST + 1024], writes=[prm])
                dma("sp", prm[:, 1024:3072], prm_d[layer, :, P_SSDN:P_SSDN + 2048], writes=[prm])
                dma("sp", prm[:, 3072:], prm_d[layer, :, P_CW:PRMW], writes=[prm])
                w_out = load_w(es, "w_out", wout_d[layer], 16, D)
                T = scan_tiles(es)
                T["dc"] = dir_consts(es, prm, PO, 1)
                recs = [mk(es, "rec2", [128, RECW], BF16) for _ in range(2)]
                dts = [mk(es, "dt2", [128, 32], F32) for _ in range(2)]
                hb = [mk(es, "hb2", [128, D], F32) for _ in range(3)]
                yt = mk(es, "yt", [128, D], F32)
                yt2 = mk(es, "yt2", [128, D], F32)
                rt = mk(es, "rt", [128, D], F32)
                junk = mk(es, "junk2", [128, D], F32)
                ycats = [mk(es, "ycat", [128, 2 * D], BF16) for _ in range(2)]
                ycT = mk(es, "ycT", [128, 16, 128], BF16)
                st8 = mk(es, "st8", [128, 8], F32)
                sq8 = mk(es, "sq8", [128, 8], F32)
                mu8 = mk(es, "mu8", [128, 8], F32)
                ss2 = mk(es, "ss2", [128, 2], F32)
                tm2 = mk(es, "tm2", [128, 2], F32)
                ssm = mk(es, "ssm", [128, 2], F32)
                ss1 = mk(es, "ss1m", [128, 1], F32)
                tm1 = mk(es, "tm1m", [128, 1], F32)
                hout = [mk(es, "hout", [128, D], F32) for _ in range(2)]

                def loads(s, c):
                    r = recs[c % 2]
                    dma("sp", r[:], rec_d[s, c], writes=[r])
                    dma("sp", dts[c % 2][:], dts_d[s, c], writes=[dts[c % 2]])
                    dma("sp", hb[c % 3][:], h_src(layer, s, c), writes=[hb[c % 3]])

                def chunkC1(s, c):
                    rec, dt = recs[c % 2], dts[c % 2]
                    ycat = ycats[c % 2]
                    def on_y(g, bk, yi):
                        sl = slice(g * 512, (g + 1) * 512)
                        op("dve", lambda e: e.tensor_tensor(out=yt[:, sl], in0=bk[:], in1=yi[:], op=ALU.add), [bk, yi], [yt])
                        op("pool", lambda e: e.tensor_tensor(out=yt[:, sl], in0=yt[:, sl],
                                                             in1=rec[:, R_YF + g * 512:R_YF + (g + 1) * 512],
                                                             op=ALU.add), [yt, rec], [yt])
                        op("pool", lambda e: e.tensor_tensor(out=yt2[:, sl], in0=rec[:, R_XS + g * 512:R_XS + (g + 1) * 512],
                                                             in1=prm[:, 2048 + g * 512:2048 + (g + 1) * 512],
                                                             op=ALU.mult), [rec, prm], [yt2])
                        op("dve", lambda e: e.tensor_tensor(out=yt[:, sl], in0=yt[:, sl], in1=yt2[:, sl], op=ALU.add),
                           [yt, yt2], [yt])
                        op("dve", lambda e: e.tensor_tensor(out=yt[:, sl], in0=yt[:, sl],
                                                            in1=rec[:, R_ZS + g * 512:R_ZS + (g + 1) * 512],
                                                            op=ALU.mult), [yt, rec], [yt])
                        op("act", lambda e: e.activation(out=junk[:, sl], in_=yt[:, sl], func=AF.Square,
                                                         accum_out=ss2[:, g:g + 1]), [yt], [junk, ss2])

                    def on_o(b_, bk):
                        sl = slice(b_ * 512, (b_ + 1) * 512)
                        op("dve", lambda e: e.tensor_tensor(out=rt[:, sl], in0=bk[:],
                                                            in1=rec[:, R_RF + b_ * 512:R_RF + (b_ + 1) * 512],
                                                            op=ALU.add), [bk, rec], [rt])
                    scan_chunk(T, 1, rec, dt, on_y, on_o)
                    op("dve", lambda e: e.tensor_scalar(out=tm2[:], in0=ss2[:], scalar1=1.0 / 512, scalar2=EPS,
                                                        op0=ALU.mult, op1=ALU.add), [ss2], [tm2])
                    op("act", lambda e: e.sqrt(out=tm2[:], in_=tm2[:]), [tm2], [tm2])
                    op("dve", lambda e: e.reciprocal(out=ss2[:], in_=tm2[:]), [tm2], [ss2])
                    for g in range(2):
                        sl = slice(g * 512, (g + 1) * 512)
                        op("dve", lambda e, sl=sl, g=g: e.scalar_tensor_tensor(
                            out=ycat[:, sl], in0=yt[:, sl], scalar=ss2[:, g:g + 1],
                            in1=prm[:, 1024 + g * 512:1024 + (g + 1) * 512], op0=ALU.mult, op1=ALU.mult),
                           [yt, ss2, prm], [ycat])
                    r3 = rt[:].rearrange("p (h e) -> p h e", e=128)
                    j3 = junk[:].rearrange("p (h e) -> p h e", e=128)
                    op("dve", lambda e: e.tensor_reduce(out=st8[:], in_=r3, axis=AX.X, op=ALU.add), [rt], [st8])
                    op("pool", lambda e: e.tensor_tensor(out=junk[:], in0=rt[:], in1=rt[:], op=ALU.mult), [rt], [junk])
                    op("dve", lambda e: e.tensor_reduce(out=sq8[:], in_=j3, axis=AX.X, op=ALU.add), [junk], [sq8])
                    op("dve", lambda e: e.tensor_scalar(out=mu8[:], in0=st8[:], scalar1=1.0 / 128, scalar2=None,
                                                        op0=ALU.mult), [st8], [mu8])
                    op("dve", lambda e: e.tensor_tensor(out=st8[:], in0=mu8[:], in1=mu8[:], op=ALU.mult), [mu8], [st8])
                    op("dve", lambda e: e.scalar_tensor_tensor(out=sq8[:], in0=sq8[:], scalar=1.0 / 128, in1=st8[:],
                                                               op0=ALU.mult, op1=ALU.subtract), [sq8, st8], [sq8])
                    op("dve", lambda e: e.tensor_scalar(out=sq8[:], in0=sq8[:], scalar1=EPS, scalar2=None, op0=ALU.add),
                       [sq8], [sq8])
                    op("act", lambda e: e.sqrt(out=sq8[:], in_=sq8[:]), [sq8], [sq8])
                    op("dve", lambda e: e.reciprocal(out=st8[:], in_=sq8[:]), [sq8], [st8])
                    op("dve", lambda e: e.tensor_tensor(out=r3, in0=r3, in1=mu8[:].unsqueeze(2).broadcast_to([128, 8, 128]),
                                                        op=ALU.subtract), [rt, mu8], [rt])
                    op("dve", lambda e: e.tensor_tensor(out=r3, in0=r3, in1=st8[:].unsqueeze(2).broadcast_to([128, 8, 128]),
                                                        op=ALU.mult), [rt, st8], [rt])
                    for b in range(2):
                        op("pool", lambda e, b=b: e.tensor_tensor(
                            out=ycat[:, 1024:2048].rearrange("p (j b e) -> p j b e", b=2, e=128)[:, :, b, :],
                            in0=rt[:, b * 512:(b + 1) * 512].rearrange("p (j e) -> p j e", e=128),
                            in1=rec[:, R_GS:R_GS + 1024].rearrange("p (j b e) -> p j b e", b=2, e=128)[:, :, b, :],
                            op=ALU.mult), [rt, rec], [ycat])

                def chunkC2(s, c):
                    hbt = hb[c % 3]
                    ycat = ycats[c % 2]
                    for hf in range(2):
                        bk = bank()
                        bkb = bk[:].bitcast(BF16).rearrange("p (k t) -> p k t", k=8)
                        tr([(bkb[:, k, :], ycat[:, (hf * 8 + k) * 128:(hf * 8 + k + 1) * 128]) for k in range(8)],
                           identb, [ycat], [bk])
                        op("act", lambda e, hf=hf, bkb=bkb: e.copy(out=ycT[:, hf * 8:hf * 8 + 8, :], in_=bkb), [bk], [ycT])
                    mb = []
                    for nh in range(2):
                        bk = bank()
                        mm([(bk[:], ycT[:, k, :], w_out[:, k, nh * 512:(nh + 1) * 512], k == 0, k == 15) for k in range(16)],
                           [ycT, w_out], [bk])
                        op("act", lambda e, nh=nh, bk=bk: e.activation(out=junk[:, nh * 512:(nh + 1) * 512], in_=bk[:],
                                                                       func=AF.Square, accum_out=ssm[:, nh:nh + 1]),
                           [bk], [junk, ssm])
                        mb.append(bk)
                    op("dve", lambda e: e.tensor_tensor(out=ss1[:], in0=ssm[:, 0:1], in1=ssm[:, 1:2], op=ALU.add),
                       [ssm], [ss1])
                    rms_scale(ss1, tm1)
                    ho = hout[c % 2]
                    for nh in range(2):
                        sl = slice(nh * 512, (nh + 1) * 512)
                        op("dve", lambda e, sl=sl, nh=nh: e.scalar_tensor_tensor(
                            out=ho[:, sl], in0=mb[nh][:], scalar=ss1[:, 0:1], in1=prm[:, sl], op0=ALU.mult, op1=ALU.mult),
                           [mb[nh], ss1, prm], [ho])
                    op("pool", lambda e: e.tensor_tensor(out=ho[:], in0=ho[:], in1=hbt[:], op=ALU.add), [ho, hbt], [ho])
                    dma("sp", hmid_d[s, c * 128:(c + 1) * 128, :], ho[:], reads=[ho])

                for s in range(NSEQ):
                    reset_state(T)
                    loads(s, NCH - 1)
                    for c in range(NCH - 1, -1, -1):
                        if c - 1 >= 0:
                            loads(s, c - 1)
                        chunkC1(s, c)
                        if c + 1 <= NCH - 1:
                            chunkC2(s, c + 1)
                    chunkC2(s, 0)
                K.barrier()

        def phase3(layer, last):
            with ExitStack() as es:
                prm = mk(es, "prm3", [128, 2048], F32)
                dma("sp", prm[:], prm_d[layer, :, P_FPRE:P_FPRE + 2048], writes=[prm])
                w_g = load_w(es, "w_g", wg_d[layer], 8, DFF)
                w_u = load_w(es, "w_u", wu_d[layer], 8, DFF)
                w_d = load_w(es, "w_d", wd_d[layer], 22, D)
                hb = [mk(es, "hb3", [128, D], F32) for _ in range(3)]
                junk = mk(es, "junk3", [128, D], F32)
                ss = [mk(es, "ss3", [128, 1], F32) for _ in range(2)]
                tmp1 = [mk(es, "tmp3", [128, 1], F32) for _ in range(2)]
                f_bf = mk(es, "f_bf", [128, D], BF16)
                fT = mk(es, "fT", [128, 8, 128], BF16)
                sg = [mk(es, "sg", [128, 512], F32) for _ in range(2)]
                acts = [mk(es, "act", [128, DFF], BF16) for _ in range(2)]
                actT = mk(es, "actT", [128, 22, 128], BF16)
                ssm = mk(es, "ssm3", [128, 2], F32)
                hout = [mk(es, "hout3", [128, D], F32) for _ in range(2)]

                def load(s, c):
                    dma("sp", hb[c % 3][:], hmid_d[s, c * 128:(c + 1) * 128, :], writes=[hb[c % 3]])

                def chunkD1(s, c):
                    b = hb[c % 3]
                    act = acts[c % 2]
                    s1, t1 = ss[c % 2], tmp1[c % 2]
                    op("act", lambda e: e.activation(out=junk[:], in_=b[:], func=AF.Square, accum_out=s1[:]), [b], [junk, s1])
                    rms_scale(s1, t1)
                    op("dve", lambda e: e.scalar_tensor_tensor(out=f_bf[:], in0=b[:], scalar=s1[:, 0:1], in1=prm[:, 0:1024],
                                                               op0=ALU.mult, op1=ALU.mult), [b, s1, prm], [f_bf])
                    bk = bank()
                    bkb = bk[:].bitcast(BF16).rearrange("p (k t) -> p k t", k=8)
                    tr([(bkb[:, k, :], f_bf[:, k * 128:(k + 1) * 128]) for k in range(8)], identb, [f_bf], [bk])
                    op("act", lambda e: e.copy(out=fT[:], in_=bkb), [bk], [fT])
                    for blk in range(6):
                        c0 = blk * 512
                        n = min(512, DFF - c0)
                        bg = bank()
                        mm([(bg[:, 0:n], fT[:, k, :], w_g[:, k, c0:c0 + n], k == 0, k == 7) for k in range(8)], [fT, w_g], [bg])
                        bu = bank()
                        mm([(bu[:, 0:n], fT[:, k, :], w_u[:, k, c0:c0 + n], k == 0, k == 7) for k in range(8)], [fT, w_u], [bu])
                        sgt = sg[blk % 2]
                        op("act", lambda e, n=n, bg=bg, sgt=sgt: e.activation(out=sgt[:, 0:n], in_=bg[:, 0:n], func=AF.Silu),
                           [bg], [sgt])
                        op("dve", lambda e, n=n, c0=c0, bu=bu, sgt=sgt: e.tensor_tensor(out=act[:, c0:c0 + n], in0=bu[:, 0:n],
                                                                                     in1=sgt[:, 0:n], op=ALU.mult),
                           [bu, sgt], [act])

                def chunkD2(s, c):
                    b = hb[c % 3]
                    act = acts[c % 2]
                    s1, t1 = ss[c % 2], tmp1[c % 2]
                    for tb in range(3):
                        k0 = tb * 8
                        nk = min(8, 22 - k0)
                        bk = bank()
                        bkb = bk[:].bitcast(BF16).rearrange("p (k t) -> p k t", k=8)
                        tr([(bkb[:, k, :], act[:, (k0 + k) * 128:(k0 + k + 1) * 128]) for k in range(nk)], identb, [act], [bk])
                        if tb % 2 == 0:
                            op("act", lambda e, k0=k0, nk=nk, bkb=bkb: e.copy(out=actT[:, k0:k0 + nk, :], in_=bkb[:, 0:nk, :]),
                               [bk], [actT])
                        else:
                            op("dve", lambda e, k0=k0, nk=nk, bkb=bkb: e.tensor_copy(out=actT[:, k0:k0 + nk, :], in_=bkb[:, 0:nk, :]),
                               [bk], [actT])
                    if DBG_3 < 4:
                        return
                    mb = []
                    for nh in range(2):
                        bk = bank()
                        mm([(bk[:], actT[:, k, :], w_d[:, k, nh * 512:(nh + 1) * 512], k == 0, k == 21) for k in range(22)],
                           [actT, w_d], [bk])
                        op("act", lambda e, nh=nh, bk=bk: e.activation(out=junk[:, nh * 512:(nh + 1) * 512], in_=bk[:],
                                                                       func=AF.Square, accum_out=ssm[:, nh:nh + 1]),
                           [bk], [junk, ssm])
                        mb.append(bk)
                    op("dve", lambda e: e.tensor_tensor(out=s1[:], in0=ssm[:, 0:1], in1=ssm[:, 1:2], op=ALU.add), [ssm], [s1])
                    rms_scale(s1, t1)
                    ho = hout[c % 2]
                    for nh in range(2):
                        sl = slice(nh * 512, (nh + 1) * 512)
                        op("dve", lambda e, sl=sl, nh=nh: e.scalar_tensor_tensor(
                            out=ho[:, sl], in0=mb[nh][:], scalar=s1[:, 0:1], in1=prm[:, 1024 + nh * 512:1024 + (nh + 1) * 512],
                            op0=ALU.mult, op1=ALU.mult), [mb[nh], s1, prm], [ho])
                    op("pool", lambda e: e.tensor_tensor(out=ho[:], in0=ho[:], in1=b[:], op=ALU.add), [ho, b], [ho])
                    if DBG_3 < 5:
                        return
                    if last:
                        if c > 0:
                            if DBG_3 == 5:
                                dma("sp", hres_d[s, c * 128:(c + 1) * 128, :], ho[:], reads=[ho])
                            elif DBG_3 == 6:
                                dma("sp", hres_d[s, c * 128:(c + 1) * 128, :], ho[:], reads=[ho])
                                dma("sp", y_d[s, (c - 1) * 128:c * 128, :], ho[:], reads=[ho])
                            elif DBG_3 == 7:
                                dma("pool", y_d[s, (c - 1) * 128:c * 128, :], ho[:], reads=[ho])
                            elif DBG_3 == 8:
                                dma("sp", y_d[s, (c - 1) * 128:c * 128, :], ho[:], reads=[ho])
                            else:
                                dma("sp", y_d[s, (c - 1) * 128:c * 128, :], ho[:], reads=[ho])
                    else:
                        dma("sp", hres_d[s, c * 128:(c + 1) * 128, :], ho[:], reads=[ho])

                for s in range(NSEQ):
                    c_list = list(range(NCH)) if not last else list(range(1, NCH))
                    load(s, c_list[0])
                    if len(c_list) > 1:
                        load(s, c_list[1])
                    chunkD1(s, c_list[0])
                    for i, c in enumerate(c_list):
                        if i + 2 < len(c_list):
                            load(s, c_list[i + 2])
                        if i + 1 < len(c_list):
                            chunkD1(s, c_list[i + 1])
                        chunkD2(s, c)
                K.barrier()
                if last and DBG_3 == 9:
                    with ExitStack() as esd:
                        ct = mk(esd, "ydiag", [128, D], F32)
                        op("pool", lambda e: e.memset(ct[:], 1.0), [], [ct])
                        for s in range(NSEQ):
                            for c in range(1, NCH):
                                dma("sp", y_d[s, (c - 1) * 128:c * 128, :], ct[:], reads=[ct])
                        K.barrier()
                K.barrier()

        for layer in range(DEPTH):
            if DEBUG_STOP == 0:
                break
            phase1(layer)
            if DEBUG_P1_ONLY or DEBUG_STOP == 1:
                break
            phase2(layer)
            if DEBUG_STOP == 2:
                break
            phase3(layer, layer == DEPTH - 1)
        if DBG_YW:
            with ExitStack() as esd:
                ct = mk(esd, "ydiag2", [128, D], F32)
                op("pool", lambda e: e.memset(ct[:], 1.0), [], [ct])
                for s in range(NSEQ):
                    for c in range(1, NCH):
                        dma("sp", y_d[s, (c - 1) * 128:c * 128, :], ct[:], reads=[ct])
                K.barrier()
    return nc


def _consts():
    c = np.zeros((128, CSTW), np.float32)
    p = np.arange(128)
    c[:, C_ID:C_ID + 128] = np.eye(128)
    c[:, C_U:C_U + 128] = (p[:, None] <= p[None, :])
    c[:, C_VGE:C_VGE + 128] = (p[:, None] >= p[None, :])
    c[:, C_SGT:C_SGT + 128] = (p[:, None] > p[None, :])
    c[:, C_REL:C_REL + 128] = (p[None, :] - p[:, None])
    c[:, C_LROW:C_LROW + 128] = p[None, :]
    c[:, C_VALID] = (p >= PADR)
    c[:, C_P] = p
    c[:, C_127P] = 127 - p
    return c


def _rope(L):
    pos = np.arange(L, dtype=np.float32)
    inv = (np.float32(10000.0) ** (-np.arange(0, 64, 2, dtype=np.float32) / np.float32(64))).astype(np.float32)
    ang = (pos[:, None] * inv[None, :]).astype(np.float32)
    cs, sn = np.cos(ang).astype(np.float32), np.sin(ang).astype(np.float32)
    t = np.zeros((L, 256), np.float32)
    t[:, 0:32] = cs; t[:, 32:64] = cs
    t[:, 64:96] = -sn; t[:, 96:128] = sn
    t[:, 128:256] = t[:, 0:128] * np.float32(0.125)
    return t


def _params(DEPTH, norm_mix_pre, norm_mix_post, norm_ffn_pre, norm_ffn_post, conv_w, conv_b, dt_bias, a_log,
            d_skip, ssd_norm, ret_log_decay):
    P = np.zeros((DEPTH, 128, PRMW), np.float32)
    for l in range(DEPTH):
        P[l, :, P_GPRE:P_GPRE + 1024] = norm_mix_pre[l][None, :]
        P[l, :, P_GPOST:P_GPOST + 1024] = norm_mix_post[l][None, :]
        P[l, :, P_FPRE:P_FPRE + 1024] = norm_ffn_pre[l][None, :]
        P[l, :, P_FPOST:P_FPOST + 1024] = norm_ffn_post[l][None, :]
        P[l, :, P_SSDN:P_SSDN + 1024] = ssd_norm[l][None, :]
        P[l, :, P_DSK:P_DSK + 1024] = np.repeat(d_skip[l], 64)[None, :]
        cw = conv_w[l].reshape(5, 12, 128)
        P[l, :, P_CW:P_CW + 60] = cw.transpose(2, 1, 0).reshape(128, 60)
        P[l, :, P_CB:P_CB + 12] = conv_b[l].reshape(12, 128).T
        P[l, :, P_DTB:P_DTB + 32] = dt_bias[l].reshape(32)[None, :]
        P[l, :, P_ALOG:P_ALOG + 32] = a_log[l].reshape(32)[None, :]
        P[l, :, P_LGB:P_LGB + 16] = ret_log_decay[l].reshape(16)[None, :]
        for d in range(2):
            lg = ret_log_decay[l, d]
            P[l, 0:64, P_LGP + 4 * d:P_LGP + 4 * d + 4] = lg[0::2][None, :]
            P[l, 64:128, P_LGP + 4 * d:P_LGP + 4 * d + 4] = lg[1::2][None, :]
    return P


_NC_CACHE = {}


def run(xs_all, meta_tokens, small, w_in, w_out, w_gate, w_up, w_down, n_cores, NSEQ, DEPTH):
    S = xs_all.shape[1]
    NCH = S // 128 + 1
    key = (NSEQ, NCH, DEPTH)
    if key not in _NC_CACHE:
        _NC_CACHE[key] = build(NSEQ, NCH, DEPTH)
    nc = _NC_CACHE[key]
    prm = _params(DEPTH, *small)
    cst = _consts()
    rope = _rope(NCH * 128)
    f = lambda a: np.ascontiguousarray(a, dtype=np.float32)
    in_maps = []
    for i in range(n_cores):
        in_maps.append({"x": f(xs_all[i * NSEQ:(i + 1) * NSEQ]), "meta": f(meta_tokens), "w_in": f(w_in), "w_out": f(w_out),
                        "w_gate": f(w_gate), "w_up": f(w_up), "w_down": f(w_down), "prm": prm, "cst": cst, "rope": rope})
    res = run_bass_kernel_spmd(nc, in_maps, core_ids=list(range(n_cores)))
    return np.concatenate([np.asarray(r["y"]) for r in res.results], axis=0)


def kernel(x_prompt, x_sample, meta_tokens, norm_mix_pre, norm_mix_post, norm_ffn_pre, norm_ffn_post,
           w_in, conv_w, conv_b, dt_bias, a_log, d_skip, ssd_norm, ret_log_decay, w_out, w_gate, w_up, w_down):
    x_prompt = np.asarray(x_prompt); x_sample = np.asarray(x_sample)
    nb = x_prompt.shape[0]
    xs_all = np.concatenate([x_prompt, x_sample], axis=0)
    small = [np.asarray(a, dtype=np.float32) for a in (norm_mix_pre, norm_mix_post, norm_ffn_pre, norm_ffn_post, conv_w,
                                                        conv_b, dt_bias, a_log, d_skip, ssd_norm, ret_log_decay)]
    y = run(xs_all, np.asarray(meta_tokens), small, np.asarray(w_in), np.asarray(w_out), np.asarray(w_gate),
            np.asarray(w_up), np.asarray(w_down), 8, xs_all.shape[0] // 8, np.asarray(w_in).shape[0])
    return (np.ascontiguousarray(y[:nb]), np.ascontiguousarray(y[nb:]))
```

```python
import numpy as np
from contextlib import ExitStack
import concourse.bass as bass
import concourse.mybir as mybir
from concourse.bass_utils import run_bass_kernel_spmd
import ml_dtypes

F32 = mybir.dt.float32
BF16 = mybir.dt.bfloat16
AF = mybir.ActivationFunctionType
ALU = mybir.AluOpType
AX = mybir.AxisListType

D = 1024
NIN = 5664
DFF = 2816
NMETA = 16
PADR = 112
EPS = 1e-6
OZ, OX, OB, OC, ODT, OQ, OK_, OV, OG = 0, 1024, 2048, 2304, 2560, 2592, 3104, 3616, 4640
R_XS, R_ZS, R_GS, R_V, R_YF, R_RF, R_KR, R_BT, R_QT, R_KT, R_BTT, R_CTT = (
    0, 1024, 2048, 3072, 4096, 5120, 6144, 6656, 6912, 7424, 7936, 8192)
RECW = 8448
P_GPRE, P_GPOST, P_FPRE, P_FPOST, P_SSDN, P_DSK, P_CW, P_CB, P_DTB, P_ALOG, P_LGB, P_LGP = (
    0, 1024, 2048, 3072, 4096, 5120, 6144, 6204, 6216, 6248, 6280, 6296)
PRMW = 6304
C_ID, C_U, C_VGE, C_SGT, C_REL, C_LROW, C_VALID, C_P, C_127P = 0, 128, 256, 384, 512, 640, 768, 769, 770
CSTW = 772
ND = 16
DEBUG_P1_ONLY = False
DEBUG_STOP = 99
DBG_DC = 99
DBG_A = 99
DBG_S = 99
DBG_3 = 99
DBG_YW = 0


class Tile:
    def __init__(self, h):
        self.h = h
        self.w = None
        self.r = {}

    def __getitem__(self, k):
        return self.h[k]


REC_FIELDS = ("xs", "zs", "gs", "v", "yf", "rf", "kr", "bt", "qT", "kT", "bT", "cT")


class Rec:
    def __init__(self, h):
        self.h = h
        self.f = {k: Tile(h) for k in REC_FIELDS}
        self.all = list(self.f.values())

    def __getitem__(self, k):
        return self.h[k]


class Eng:
    def __init__(self, name, h, sem):
        self.name, self.h, self.sem = name, h, sem
        self.cnt = 0
        self.seen = {}


class Kern:
    def __init__(self, nc, es):
        self.nc, self.es = nc, es
        self.E = {}
        for n, a in (("pe", "tensor"), ("act", "scalar"), ("dve", "vector"), ("pool", "gpsimd"), ("sp", "sync")):
            self.E[n] = Eng(n, getattr(nc, a), es.enter_context(nc.semaphore("s_" + n)))
        self.dq = {}
        for q in ("sp", "pool"):
            sems = [es.enter_context(nc.semaphore("d_%s%d" % (q, i))) for i in range(ND)]
            self.dq[q] = dict(sems=sems, vals=[0] * ND, nxt=0)
        self.bsem = es.enter_context(nc.semaphore("bar"))
        self.bcnt = 0
        self.nid = 0

    def _wait(self, eng, toks):
        best = {}
        for key, sem, val in toks:
            if key not in best or best[key][1] < val:
                best[key] = (sem, val)
        for key, (sem, val) in best.items():
            if eng.seen.get(key, 0) >= val:
                continue
            if key == eng.name:
                if eng.name == "pe":
                    continue
            eng.h.wait_ge(sem, val)
            eng.seen[key] = val

    @staticmethod
    def _deps(reads, writes):
        toks = []
        for t in reads:
            if t.w:
                toks.append(t.w)
        for t in writes:
            if t.w:
                toks.append(t.w)
            toks.extend(t.r.values())
        return toks

    def _mark(self, tok, reads, writes):
        for t in reads:
            t.r[tok[0]] = tok
        for t in writes:
            t.w = tok
            t.r = {}

    def op(self, en, fn, reads=(), writes=()):
        eng = self.E[en]
        self._wait(eng, self._deps(reads, writes))
        ins = fn(eng.h)
        eng.cnt += 1
        ins.then_inc(eng.sem, 1)
        self._mark((en, eng.sem, eng.cnt), reads, writes)

    def mm(self, items, reads=(), writes=()):
        eng = self.E["pe"]
        self._wait(eng, self._deps(reads, writes))
        ins = None
        for (o, l, r, st, sp) in items:
            ins = eng.h.matmul(o, lhsT=l, rhs=r, start=st, stop=sp)
        eng.cnt += 1
        ins.then_inc(eng.sem, 1)
        self._mark(("pe", eng.sem, eng.cnt), reads, writes)

    def tr(self, items, ident, reads=(), writes=()):
        eng = self.E["pe"]
        self._wait(eng, self._deps(list(reads) + [ident], writes))
        ins = None
        for (o, i) in items:
            ins = eng.h.transpose(out=o, in_=i, identity=ident[:])
        eng.cnt += 1
        ins.then_inc(eng.sem, 1)
        self._mark(("pe", eng.sem, eng.cnt), list(reads) + [ident], writes)

    def dma(self, q, out_ap, in_ap, reads=(), writes=()):
        eng = self.E[q]
        d = self.dq[q]
        j = d["nxt"]
        d["nxt"] = (j + 1) % ND
        key = "d_%s%d" % (q, j)
        toks = self._deps(reads, writes)
        if d["vals"][j] > 0:
            toks.append((key, d["sems"][j], d["vals"][j]))
        self._wait(eng, toks)
        ins = eng.h.dma_start(out=out_ap, in_=in_ap)
        d["vals"][j] += 16
        ins.then_inc(d["sems"][j], 16)
        self._mark((key, d["sems"][j], d["vals"][j]), reads, writes)

    def barrier(self):
        sp = self.E["sp"]
        toks = [(n, e.sem, e.cnt) for n, e in self.E.items() if e.cnt > 0]
        for q, d in self.dq.items():
            for j in range(ND):
                if d["vals"][j] > 0:
                    toks.append(("d_%s%d" % (q, j), d["sems"][j], d["vals"][j]))
        self._wait(sp, toks)
        self.bcnt += 1
        sp.h.sem_inc(self.bsem, 1)
        for n, e in self.E.items():
            if n != "sp":
                e.h.wait_ge(self.bsem, self.bcnt)
            for key, sem, val in toks:
                if e.seen.get(key, 0) < val:
                    e.seen[key] = val


def build(NSEQ, NCH, DEPTH):
    nc = bass.Bass("TRN2", target_bir_lowering=False)
    L = NCH * 128
    S = L - 128
    x_d = nc.dram_tensor("x", [NSEQ, S, D], F32, kind="ExternalInput").ap()
    meta_d = nc.dram_tensor("meta", [NMETA, D], F32, kind="ExternalInput").ap()
    win_d = nc.dram_tensor("w_in", [DEPTH, D, NIN], F32, kind="ExternalInput").ap()
    wout_d = nc.dram_tensor("w_out", [DEPTH, 2 * D, D], F32, kind="ExternalInput").ap()
    wg_d = nc.dram_tensor("w_gate", [DEPTH, D, DFF], F32, kind="ExternalInput").ap()
    wu_d = nc.dram_tensor("w_up", [DEPTH, D, DFF], F32, kind="ExternalInput").ap()
    wd_d = nc.dram_tensor("w_down", [DEPTH, DFF, D], F32, kind="ExternalInput").ap()
    prm_d = nc.dram_tensor("prm", [DEPTH, 128, PRMW], F32, kind="ExternalInput").ap()
    cst_d = nc.dram_tensor("cst", [128, CSTW], F32, kind="ExternalInput").ap()
    rope_d = nc.dram_tensor("rope", [L, 256], F32, kind="ExternalInput").ap()
    y_d = nc.dram_tensor("y", [NSEQ, S, D], F32, kind="ExternalOutput").ap()
    hres_d = nc.dram_tensor("hres", [NSEQ, L, D], F32, kind="Internal").ap()
    hmid_d = nc.dram_tensor("hmid", [NSEQ, L, D], F32, kind="Internal").ap()
    rec_d = nc.dram_tensor("rec", [NSEQ, NCH, 128, RECW], BF16, kind="Internal").ap()
    dts_d = nc.dram_tensor("dts", [NSEQ, NCH, 128, 32], F32, kind="Internal").ap()

    with ExitStack() as es0:
        K = Kern(nc, es0)
        op, mm, tr, dma = K.op, K.mm, K.tr, K.dma

        def mk(es, name, shape, dt):
            K.nid += 1
            return Tile(es.enter_context(nc.sbuf_tensor("%s_%d" % (name, K.nid), shape, dt)))

        cst = mk(es0, "cst", [128, CSTW], F32)
        identb = mk(es0, "identb", [128, 128], BF16)
        onesb = mk(es0, "onesb", [128, 128], BF16)
        Ub = mk(es0, "Ub", [128, 128], BF16)
        Vb = mk(es0, "Vb", [128, 128], BF16)
        MNf = mk(es0, "MNf", [128, 4, 128], BF16)
        MNb = mk(es0, "MNb", [128, 4, 128], BF16)
        est = ExitStack()
        zt = mk(est, "zt", [128, D], F32)
        banks = [Tile(es0.enter_context(nc.psum_tensor("bank%d" % i, [128, 512], F32))) for i in range(8)]
        bstate = dict(i=0)

        def bank():
            b = banks[bstate["i"] % 8]
            bstate["i"] += 1
            return b

        dma("sp", cst[:], cst_d[:, :], writes=[cst])
        op("dve", lambda e: e.tensor_copy(out=identb[:], in_=cst[:, C_ID:C_ID + 128]), [cst], [identb])
        op("dve", lambda e: e.tensor_copy(out=Ub[:], in_=cst[:, C_U:C_U + 128]), [cst], [Ub])
        op("dve", lambda e: e.tensor_copy(out=Vb[:], in_=cst[:, C_VGE:C_VGE + 128]), [cst], [Vb])
        op("pool", lambda e: e.memset(onesb[:], 1.0), [], [onesb])
        op("dve", lambda e: e.tensor_scalar(out=MNf[:], in0=cst[:, C_SGT:C_SGT + 128].unsqueeze(1).broadcast_to([128, 4, 128]),
                                            scalar1=-30000.0, scalar2=None, op0=ALU.mult), [cst], [MNf])
        op("dve", lambda e: e.tensor_scalar(out=MNb[:], in0=cst[:, C_U:C_U + 128].unsqueeze(1).broadcast_to([128, 4, 128]),
                                            scalar1=-30000.0, scalar2=None, op0=ALU.mult), [cst], [MNb])
        op("pool", lambda e: e.memset(zt[:], 0.0), [], [zt])
        for s in range(NSEQ):
            dma("sp", hres_d[s, 0:PADR, :], zt[0:PADR, :], reads=[zt])
            dma("sp", hres_d[s, PADR:128, :], meta_d[:, :])
        K.barrier()
        est.close()

        def h_src(layer, s, c):
            if layer == 0 and c > 0:
                return x_d[s, (c - 1) * 128:c * 128, :]
            return hres_d[s, c * 128:(c + 1) * 128, :]

        def load_w(es, name, src2d, kt, ncols):
            w = mk(es, name, [128, kt, ncols], BF16)
            for k in range(kt):
                c0 = 0
                while c0 < ncols:
                    cw = min(2048, ncols - c0)
                    dma("pool", w[:, k, c0:c0 + cw], src2d[k * 128:(k + 1) * 128, c0:c0 + cw], writes=[w])
                    c0 += cw
            return w

        def rms_scale(ss, tmp):
            op("dve", lambda e: e.tensor_scalar(out=tmp[:], in0=ss[:], scalar1=1.0 / D, scalar2=EPS,
                                                op0=ALU.mult, op1=ALU.add), [ss], [tmp])
            op("act", lambda e: e.sqrt(out=tmp[:], in_=tmp[:]), [tmp], [tmp])
            op("dve", lambda e: e.reciprocal(out=ss[:], in_=tmp[:]), [tmp], [ss])

        def dir_consts(es, prm, PO, d):
            lgB = prm[:, PO + P_LGB + 8 * d:PO + P_LGB + 8 * d + 8]
            lgP = prm[:, PO + P_LGP + 4 * d:PO + P_LGP + 4 * d + 4]
            dmT = mk(es, "dmT", [128, 8, 128], F32)
            xiT = mk(es, "xiT", [128, 4, 128], F32)
            zeta = mk(es, "zeta", [128, 8], F32)
            gch = mk(es, "gch", [128, 4], F32)
            negA = mk(es, "negA", [128, 16], F32)
            sc = mk(es, "sc", [128, 16], F32)
            if d == 0:
                op("dve", lambda e: e.tensor_copy(out=sc[:, 0:8], in_=lgB), [prm], [sc])
                op("dve", lambda e: e.tensor_copy(out=sc[:, 8:12], in_=lgP), [prm], [sc])
                op("dve", lambda e: e.tensor_copy(out=sc[:, 12:16], in_=lgP), [prm], [sc])
            else:
                op("dve", lambda e: e.tensor_scalar(out=sc[:, 0:8], in0=lgB, scalar1=-1.0, scalar2=None,
                                                    op0=ALU.mult), [prm], [sc])
                op("dve", lambda e: e.tensor_scalar(out=sc[:, 8:12], in0=lgP, scalar1=-1.0, scalar2=None,
                                                    op0=ALU.mult), [prm], [sc])
                op("dve", lambda e: e.tensor_scalar(out=sc[:, 12:16], in0=lgP, scalar1=128.0, scalar2=None,
                                                    op0=ALU.mult), [prm], [sc])
            msk = cst[:, C_U:C_U + 128] if d == 0 else cst[:, C_SGT:C_SGT + 128]
            if DBG_DC >= 1:
                for i in range(8):
                    h = 2 * (i % 4) + i // 4
                    op("act", lambda e, h=h, i=i: e.activation(out=dmT[:, i, :], in_=cst[:, C_REL:C_REL + 128], func=AF.Exp,
                                                               scale=sc[:, h:h + 1]), [cst, sc], [dmT])
            if DBG_DC >= 2:
                op("dve", lambda e: e.tensor_tensor(out=dmT[:], in0=dmT[:],
                                                    in1=msk.unsqueeze(1).broadcast_to([128, 8, 128]), op=ALU.mult),
                   [dmT, cst], [dmT])
            if DBG_DC >= 3:
                for j in range(4):
                    op("act", lambda e, j=j: e.activation(out=xiT[:, j, :], in_=cst[:, C_LROW:C_LROW + 128], func=AF.Exp,
                                                          scale=sc[:, 8 + j:9 + j], bias=sc[:, 12 + j:13 + j]),
                       [cst, sc], [xiT])
            zc = C_127P if d == 0 else C_P
            if DBG_DC >= 4:
                op("act", lambda e: e.activation(out=zeta[:], in_=lgB, func=AF.Exp, scale=cst[:, zc:zc + 1]),
                   [prm, cst], [zeta])
            if DBG_DC >= 5:
                op("act", lambda e: e.activation(out=gch[:], in_=lgP, func=AF.Exp, scale=128.0), [prm], [gch])
            if DBG_DC >= 6:
                op("act", lambda e: e.activation(out=negA[:], in_=prm[:, PO + P_ALOG + 16 * d:PO + P_ALOG + 16 * d + 16],
                                                 func=AF.Exp), [prm], [negA])
                op("dve", lambda e: e.tensor_scalar(out=negA[:], in0=negA[:], scalar1=-1.0, scalar2=None, op0=ALU.mult),
                   [negA], [negA])
            return dmT, xiT, zeta, gch, negA

        def scan_chunk(T, d, rec, dt, on_y, on_o):
            (dmT, xiT, zeta, gch, negA) = T["dc"]
            st, stb, Rs, Rb = T["state"], T["stateb"], T["R"], T["Rb"]
            dAb, G, nacs, cbm, E, eaT, MT, yin, xdt, xw, decB = (T[k] for k in (
                "dAb", "G", "nacs", "cbm", "E", "eaT", "MT", "yin", "xdt", "xw", "decB"))
            sTm, qxT, kz = T["sTm"], T["qxT"], T["kz"]
            Mb = Ub if d == 0 else Vb
            MN = MNf if d == 0 else MNb
            wend = T["wend"]
            mcol = C_U if d == 0 else C_SGT
            lsel = 127 if d == 0 else 0
            op("dve", lambda e: e.tensor_tensor(out=dAb[:], in0=dt[:, 16 * d:16 * d + 16], in1=negA[:], op=ALU.mult),
               [dt, negA], [dAb])
            op("pool", lambda e: e.tensor_tensor(out=G[:], in0=dAb[:].unsqueeze(2).broadcast_to([128, 16, 128]),
                                                 in1=Mb[:].unsqueeze(1).broadcast_to([128, 16, 128]), op=ALU.mult),
               [dAb, Mb], [G])
            bk = bank()
            mm([(bk[:, 0:16], Mb[:], dAb[:], True, True), (bk[:, 16:32], onesb[:], dAb[:], True, True)],
               [Mb, onesb, dAb], [bk])
            op("dve", lambda e, bk=bk: e.tensor_scalar(out=nacs[:], in0=bk[:, 0:16], scalar1=-1.0, scalar2=None,
                                                       op0=ALU.mult), [bk], [nacs])
            op("dve", lambda e, bk=bk: e.tensor_tensor(out=wend[:], in0=bk[:, 16:32], in1=nacs[:], op=ALU.add),
               [bk, nacs], [wend])
            op("act", lambda e, bk=bk: e.activation(out=eaT[:], in_=bk[:, 0:16], func=AF.Exp), [bk], [eaT])
            op("act", lambda e, bk=bk: e.activation(out=decB[:], in_=bk[:, 16:32], func=AF.Exp), [bk], [decB])
            op("act", lambda e: e.activation(out=wend[:], in_=wend[:], func=AF.Exp), [wend], [wend])
            if DBG_S < 2:
                return
            op("pool", lambda e: e.tensor_tensor(
                out=xdt[:].rearrange("p (h d) -> p h d", d=64),
                in0=rec[:, R_XS:R_XS + 1024].rearrange("p (h d) -> p h d", d=64),
                in1=dt[:, 16 * d:16 * d + 16].unsqueeze(2).broadcast_to([128, 16, 64]), op=ALU.mult),
               [rec.f["xs"], dt], [xdt])
            bk = bank()
            mm([(bk[:, g * 128:(g + 1) * 128], rec[:, R_BTT + g * 128:R_BTT + (g + 1) * 128],
                 rec[:, R_CTT + g * 128:R_CTT + (g + 1) * 128], True, True) for g in range(2)], [rec.f["bT"], rec.f["cT"]], [bk])
            op("dve", lambda e: e.tensor_tensor(
                out=cbm[:], in0=bk[:, 0:256].rearrange("p (g l) -> p g l", g=2),
                in1=cst[:, mcol:mcol + 128].unsqueeze(1).broadcast_to([128, 2, 128]), op=ALU.mult),
               [bk, cst], [cbm])
            if DBG_S < 3:
                return
            for g in range(2):
                for hf in range(2):
                    h0 = g * 8 + hf * 4
                    bk = bank()
                    mm([(bk[:], onesb[:], G[:, h0:h0 + 4, :].rearrange("p h l -> p (h l)"), True, False),
                        (bk[:], identb[:], MN[:].rearrange("p h l -> p (h l)"), False, True)],
                       [onesb, G, identb, MN], [bk])
                    for hh in range(4):
                        op("act", lambda e, hh=hh, bk=bk, h0=h0, Eq=E[g * 2 + hf]: e.activation(
                            out=Eq[:, hh, :], in_=bk[:, hh * 128:(hh + 1) * 128], func=AF.Exp,
                            bias=nacs[:, h0 + hh:h0 + hh + 1]), [bk, nacs], [E[g * 2 + hf]])
                    op("dve", lambda e, g=g, Eq=E[g * 2 + hf], Mq=MT[g * 2 + hf]: e.tensor_tensor(
                        out=Mq[:], in0=Eq[:],
                        in1=cbm[:, g, :].unsqueeze(1).broadcast_to([128, 4, 128]), op=ALU.mult),
                       [E[g * 2 + hf], cbm], [MT[g * 2 + hf]])
            if DBG_S < 4:
                return
            op("dve", lambda e: e.tensor_tensor(
                out=xw[:].rearrange("p (h d) -> p h d", d=64), in0=xdt[:].rearrange("p (h d) -> p h d", d=64),
                in1=wend[:].unsqueeze(2).broadcast_to([128, 16, 64]), op=ALU.mult), [xdt, wend], [xw])
            ybanks = []
            for g in range(2):
                bk = bank()
                items = []
                for h in range(8):
                    hg = g * 8 + h
                    items.append((bk[:, h * 64:(h + 1) * 64], MT[hg // 4][:, hg % 4, :], xdt[:, hg * 64:(hg + 1) * 64],
                                  True, True))
                mm(items, [MT[g * 2], MT[g * 2 + 1], xdt], [bk])
                bo = bank()
                mm([(bo[:], rec[:, R_CTT + g * 128:R_CTT + (g + 1) * 128], stb[:, g, :], True, True)], [rec.f["cT"], stb], [bo])
                yi = yin[g]
                op("act", lambda e, bo=bo, yi=yi: e.copy(out=yi[:], in_=bo[:]), [bo], [yi])
                op("dve", lambda e, yi=yi, g=g: e.tensor_tensor(
                    out=yi[:].rearrange("p (h d) -> p h d", d=64), in0=yi[:].rearrange("p (h d) -> p h d", d=64),
                    in1=eaT[:, g * 8:(g + 1) * 8].unsqueeze(2).broadcast_to([128, 8, 64]), op=ALU.mult), [yi, eaT], [yi])
                on_y(g, bk, yi)
            for g in range(2):
                bk = bank()
                mm([(bk[:], rec[:, R_BT + g * 128:R_BT + (g + 1) * 128], xw[:, g * 512:(g + 1) * 512], True, True)],
                   [rec.f["bt"], xw], [bk])
                op("dve", lambda e: e.tensor_tensor(
                    out=st[:, g, :].rearrange("p (h d) -> p h d", d=64),
                    in0=st[:, g, :].rearrange("p (h d) -> p h d", d=64),
                    in1=decB[:, g * 8:(g + 1) * 8].unsqueeze(2).broadcast_to([128, 8, 64]), op=ALU.mult),
                   [st, decB], [st])
                op("dve", lambda e: e.tensor_tensor(out=st[:, g, :], in0=st[:, g, :], in1=bk[:], op=ALU.add),
                   [st, bk], [st])
            op("act", lambda e: e.copy(out=stb[:].rearrange("p g n -> p (g n)"),
                                       in_=st[:].rearrange("p g n -> p (g n)")), [st], [stb])
            if DBG_S < 5:
                return
            op("pool", lambda e: e.tensor_tensor(
                out=qxT[:], in0=rec[:, R_QT:R_QT + 512].rearrange("p (j l) -> p j l", j=4), in1=xiT[:], op=ALU.mult),
               [rec.f["qT"], xiT], [qxT])
            op("pool", lambda e: e.tensor_tensor(
                out=kz[:].rearrange("p (h d) -> p h d", d=64),
                in0=rec[:, R_KR:R_KR + 512].rearrange("p (h d) -> p h d", d=64),
                in1=zeta[:].unsqueeze(2).broadcast_to([128, 8, 64]), op=ALU.mult), [rec.f["kr"], zeta], [kz])
            obanks = []
            for b in range(2):
                bk = bank()
                items = []
                for j in range(4):
                    items.append((bk[:, j * 128:(j + 1) * 128],
                                  rec[b * 64:(b + 1) * 64, R_KT + j * 128:R_KT + (j + 1) * 128],
                                  rec[b * 64:(b + 1) * 64, R_QT + j * 128:R_QT + (j + 1) * 128], True, True))
                mm(items, [rec.f["kT"], rec.f["qT"]], [bk])
                op("dve", lambda e, b=b, bk=bk: e.tensor_tensor(
                    out=sTm[b][:], in0=bk[:].rearrange("p (h l) -> p h l", h=4),
                    in1=dmT[:, b * 4:b * 4 + 4, :], op=ALU.mult), [bk, dmT], [sTm[b]])
            for b in range(2):
                bk = bank()
                items = []
                for j in range(4):
                    h = 2 * j + b
                    items.append((bk[:, j * 128:(j + 1) * 128], sTm[b][:, j, :],
                                  rec[:, R_V + h * 128:R_V + (h + 1) * 128], True, False))
                    items.append((bk[:, j * 128:(j + 1) * 128], qxT[b * 64:(b + 1) * 64, j, :],
                                  Rb[b * 64:(b + 1) * 64, j, :], False, True))
                mm(items, [sTm[b], rec.f["v"], qxT, Rb], [bk])
                on_o(b, bk)
            if DBG_S < 6:
                return
            bk = bank()
            items = []
            for h in range(8):
                j, b = h // 2, h % 2
                items.append((bk[b * 64:(b + 1) * 64, j * 128:(j + 1) * 128], kz[:, h * 64:(h + 1) * 64],
                              rec[:, R_V + h * 128:R_V + (h + 1) * 128], True, True))
            mm(items, [kz, rec.f["v"]], [bk])
            op("dve", lambda e: e.tensor_tensor(out=Rs[:], in0=Rs[:],
                                                in1=gch[:].unsqueeze(2).broadcast_to([128, 4, 128]), op=ALU.mult),
               [Rs, gch], [Rs])
            op("dve", lambda e: e.tensor_tensor(out=Rs[:].rearrange("p j e -> p (j e)"),
                                                in0=Rs[:].rearrange("p j e -> p (j e)"), in1=bk[:], op=ALU.add),
               [Rs, bk], [Rs])
            op("act", lambda e: e.copy(out=Rb[:].rearrange("p j e -> p (j e)"),
                                       in_=Rs[:].rearrange("p j e -> p (j e)")), [Rs], [Rb])

        def scan_tiles(es):
            T = {}
            T["state"] = mk(es, "state", [128, 2, 512], F32)
            T["stateb"] = mk(es, "stateb", [128, 2, 512], BF16)
            T["R"] = mk(es, "R", [128, 4, 128], F32)
            T["Rb"] = mk(es, "Rb", [128, 4, 128], BF16)
            T["dAb"] = mk(es, "dAb", [128, 16], BF16)
            T["G"] = mk(es, "G", [128, 16, 128], BF16)
            T["nacs"] = mk(es, "nacs", [128, 16], F32)
            T["cbm"] = mk(es, "cbm", [128, 2, 128], F32)
            T["E"] = [mk(es, "E", [128, 4, 128], BF16) for _ in range(4)]
            T["eaT"] = mk(es, "eaT", [128, 16], F32)
            T["yin"] = [mk(es, "yin", [128, 512], F32) for _ in range(2)]
            T["MT"] = [mk(es, "MT", [128, 4, 128], BF16) for _ in range(4)]
            T["xdt"] = mk(es, "xdt", [128, 1024], BF16)
            T["xw"] = mk(es, "xw", [128, 1024], BF16)
            T["decB"] = mk(es, "decB", [128, 16], F32)
            T["wend"] = mk(es, "wend", [128, 16], F32)
            T["sTm"] = [mk(es, "sTm", [128, 4, 128], BF16) for _ in range(2)]
            T["qxT"] = mk(es, "qxT", [128, 4, 128], BF16)
            T["kz"] = mk(es, "kz", [128, 512], BF16)
            return T

        def reset_state(T):
            op("pool", lambda e: e.memset(T["state"][:], 0.0), [], [T["state"]])
            op("pool", lambda e: e.memset(T["stateb"][:], 0.0), [], [T["stateb"]])
            op("pool", lambda e: e.memset(T["R"][:], 0.0), [], [T["R"]])
            op("pool", lambda e: e.memset(T["Rb"][:], 0.0), [], [T["Rb"]])

        def phase1(layer):
            with ExitStack() as es:
                prm = mk(es, "prm1", [128, PRMW - P_CW + 1024], F32)
                PO = 1024 - P_CW
                dma("sp", prm[:, 0:1024], prm_d[layer, :, P_GPRE:P_GPRE + 1024], writes=[prm])
                dma("sp", prm[:, 1024:], prm_d[layer, :, P_CW:PRMW], writes=[prm])

                if DEBUG_STOP == 8:
                    K.barrier()
                    return
                w_in = load_w(es, "w_in", win_d[layer], 8, NIN)
                if DEBUG_STOP == 9:
                    K.barrier()
                    return
                T = scan_tiles(es)
                T["dc"] = dir_consts(es, prm, PO, 0)
                if DEBUG_STOP == 10:
                    K.barrier()
                    return
                hb = [mk(es, "hb", [128, D], F32) for _ in range(2)]
                junk = mk(es, "junk", [128, D], F32)
                ss = [mk(es, "ss", [128, 1], F32) for _ in range(2)]
                tmp1 = [mk(es, "tmp1", [128, 1], F32) for _ in range(2)]
                u_bf = mk(es, "u_bf", [128, D], BF16)
                ext = [mk(es, "ext", [128, 8, 192], BF16) for _ in range(3)]
                acc = [[mk(es, "acc", [128, 128], F32) for _ in range(3)] for _ in range(2)]
                xT = mk(es, "xT", [128, 8, 128], BF16)
                recs1 = [Rec(mk(es, "rec", [128, RECW], BF16).h) for _ in range(2)]
                ropet = [mk(es, "ropet", [128, 256], F32) for _ in range(2)]
                dtts = [mk(es, "dtt", [128, 32], F32) for _ in range(2)]
                dtmp = mk(es, "dtmp", [128, 32], F32)
                q_rot = mk(es, "q_rot", [128, 512], BF16)
                pT = Tile(None)

                def stageA(s, c):
                    b = hb[c % 2]
                    s1, t1 = ss[c % 2], tmp1[c % 2]
                    dma("sp", b[:], h_src(layer, s, c), writes=[b])
                    op("act", lambda e: e.activation(out=u_bf[:], in_=b[:], func=AF.Square, accum_out=s1[:]),
                       [b], [u_bf, s1])
                    rms_scale(s1, t1)
                    if c == 0:
                        op("dve", lambda e: e.tensor_tensor(out=s1[:], in0=s1[:], in1=cst[:, C_VALID:C_VALID + 1],
                                                            op=ALU.mult), [s1, cst], [s1])
                    if DBG_A < 2:
                        return
                    op("dve", lambda e: e.scalar_tensor_tensor(out=u_bf[:], in0=b[:], scalar=s1[:, 0:1],
                                                               in1=prm[:, 0:1024], op0=ALU.mult, op1=ALU.mult),
                       [b, s1, prm], [u_bf])
                    if DBG_A < 3:
                        return
                    bk = bank()
                    bkb = bk[:].bitcast(BF16).rearrange("p (k t) -> p k t", k=8)
                    tr([(bkb[:, k, :], u_bf[:, k * 128:(k + 1) * 128]) for k in range(8)], identb, [u_bf], [bk])
                    if DBG_A < 4:
                        return
                    e_c = ext[c % 3]
                    op("act", lambda e: e.copy(out=e_c[:, :, 32:160], in_=bkb), [bk], [e_c])
                    if DBG_A < 5:
                        return
                    if c > 0:
                        e_p = ext[(c - 1) % 3]
                        op("pool", lambda e: e.tensor_copy(out=e_p[:, :, 160:192], in_=e_c[:, :, 32:64]), [e_c], [e_p])
                        op("pool", lambda e: e.tensor_copy(out=e_c[:, :, 0:32], in_=e_p[:, :, 128:160]), [e_p], [e_c])
                    else:
                        op("pool", lambda e: e.memset(e_c[:, :, 0:32], 0.0), [], [e_c])
                    if c == NCH - 1:
                        op("pool", lambda e: e.memset(e_c[:, :, 160:192], 0.0), [], [e_c])

                def proj_tok(e_c, c0, n):
                    bk = bank()
                    mm([(bk[:, 0:n], e_c[:, k, 32:160], w_in[:, k, c0:c0 + n], k == 0, k == 7) for k in range(8)],
                       [e_c, w_in], [bk])
                    return bk

                def stageB1(s, c):
                    rec = recs1[c % 2]
                    dtt = dtts[c % 2]
                    e_c = ext[c % 3]
                    rp = ropet[c % 2]
                    dma("sp", rp[:], rope_d[c * 128:(c + 1) * 128, :], writes=[rp])
                    bk = proj_tok(e_c, ODT, 32)
                    op("dve", lambda e: e.tensor_tensor(out=dtmp[:], in0=bk[:, 0:32],
                                                        in1=prm[:, PO + P_DTB:PO + P_DTB + 32], op=ALU.add),
                       [bk, prm], [dtmp])
                    op("act", lambda e: e.activation(out=dtmp[:], in_=dtmp[:], func=AF.Exp), [dtmp], [dtmp])
                    op("act", lambda e: e.activation(out=dtt[:], in_=dtmp[:], func=AF.Ln, bias=1.0), [dtmp], [dtt])
                    if c == 0:
                        op("dve", lambda e: e.tensor_scalar(out=dtt[:], in0=dtt[:], scalar1=cst[:, C_VALID:C_VALID + 1],
                                                            scalar2=None, op0=ALU.mult), [dtt, cst], [dtt])
                    dma("sp", dts_d[s, c], dtt[:], reads=[dtt])
                    for (c0, r0, fn) in ((OZ, R_ZS, AF.Silu), (OZ + 512, R_ZS + 512, AF.Silu),
                                         (OG, R_GS, AF.Silu), (OG + 512, R_GS + 512, AF.Silu),
                                         (OV, R_V, AF.Copy), (OV + 512, R_V + 512, AF.Copy)):
                        bk = proj_tok(e_c, c0, 512)
                        fld = rec.f["zs" if r0 < R_GS else ("gs" if r0 < R_V else "v")]
                        op("act", lambda e, r0=r0, fn=fn, bk=bk: e.activation(out=rec[:, r0:r0 + 512], in_=bk[:], func=fn),
                           [bk], [fld])
                    for qi, (c0, dst, r0) in enumerate(((OQ, q_rot, 0), (OK_, rec, R_KR))):
                        bk = proj_tok(e_c, c0, 512)
                        b3 = bk[:].rearrange("p (h d) -> p h d", d=64)
                        cc = rp[:, qi * 128:qi * 128 + 64]
                        sg = rp[:, qi * 128 + 64:qi * 128 + 128]
                        A3 = junk[:, 0:512].rearrange("p (h d) -> p h d", d=64)
                        B3 = junk[:, 512:1024].rearrange("p (h d) -> p h d", d=64)
                        op("dve", lambda e: e.tensor_tensor(out=A3, in0=b3, in1=cc.unsqueeze(1).broadcast_to([128, 8, 64]),
                                                            op=ALU.mult), [bk, rp], [junk])
                        op("dve", lambda e: e.tensor_tensor(out=B3[:, :, 0:32], in0=b3[:, :, 32:64],
                                                            in1=sg[:, 0:32].unsqueeze(1).broadcast_to([128, 8, 32]),
                                                            op=ALU.mult), [bk, rp], [junk])
                        op("dve", lambda e: e.tensor_tensor(out=B3[:, :, 32:64], in0=b3[:, :, 0:32],
                                                            in1=sg[:, 32:64].unsqueeze(1).broadcast_to([128, 8, 32]),
                                                            op=ALU.mult), [bk, rp], [junk])
                        op("pool", lambda e, dst=dst, r0=r0: e.tensor_tensor(out=dst[:, r0:r0 + 512], in0=junk[:, 0:512],
                                                                             in1=junk[:, 512:1024], op=ALU.add),
                           [junk], [dst if dst is q_rot else rec.f["kr"]])
                    bk = bank()
                    bkb = bk[:].bitcast(BF16).rearrange("p (k t) -> p k t", k=8)
                    tr([(bkb[:, j, :], q_rot[:, j * 128:(j + 1) * 128]) for j in range(4)] +
                       [(bkb[:, 4 + j, :], rec[:, R_KR + j * 128:R_KR + (j + 1) * 128]) for j in range(4)],
                       identb, [q_rot, rec.f["kr"]], [bk])
                    op("act", lambda e: e.copy(out=rec[:, R_QT:R_QT + 1024], in_=bk[:].bitcast(BF16)), [bk],
                       [rec.f["qT"], rec.f["kT"]])
                    for jb in range(4):
                        bk = bank()
                        b3 = bk[:, 0:396].rearrange("p (j t) -> p j t", j=3)
                        items = []
                        for jj in range(3):
                            j = jb * 3 + jj
                            for k in range(8):
                                items.append((b3[:, jj, :], w_in[:, k, OX + j * 128:OX + (j + 1) * 128],
                                              e_c[:, k, 30:162], k == 0, k == 7))
                        mm(items, [e_c, w_in], [bk])
                        accs = acc[jb % 2]
                        for t in range(5):
                            for jj in range(3):
                                j = jb * 3 + jj
                                cw = PO + P_CW + j * 5
                                a = accs[jj]
                                if t == 0:
                                    op("dve", lambda e, jj=jj, cw=cw, a=a: e.tensor_scalar(
                                        out=a[:], in0=b3[:, jj, 0:128], scalar1=prm[:, cw:cw + 1], scalar2=None,
                                        op0=ALU.mult), [bk, prm], [a])
                                else:
                                    op("dve", lambda e, jj=jj, cw=cw, t=t, a=a: e.scalar_tensor_tensor(
                                        out=a[:], in0=b3[:, jj, t:t + 128], scalar=prm[:, cw + t:cw + t + 1],
                                        in1=a[:], op0=ALU.mult, op1=ALU.add), [bk, prm, a], [a])
                        for jj in range(3):
                            j = jb * 3 + jj
                            cb = PO + P_CB + j
                            if j < 8:
                                dst_t, dst_ap = xT, xT[:, j, :]
                            elif j < 10:
                                dst_t, dst_ap = rec.f["bT"], rec[:, R_BTT + (j - 8) * 128:R_BTT + (j - 7) * 128]
                            else:
                                dst_t, dst_ap = rec.f["cT"], rec[:, R_CTT + (j - 10) * 128:R_CTT + (j - 9) * 128]
                            a = accs[jj]
                            op("act", lambda e, jj=jj, cb=cb, dst_ap=dst_ap, a=a: e.activation(
                                out=dst_ap, in_=a[:], func=AF.Silu, bias=prm[:, cb:cb + 1]), [a, prm], [dst_t])
                    bk = bank()
                    bkb = bk[:].bitcast(BF16).rearrange("p (k t) -> p k t", k=8)
                    tr([(bkb[:, j, :], xT[:, j, :]) for j in range(8)], identb, [xT], [bk])
                    op("act", lambda e: e.copy(out=rec[:, R_XS:R_XS + 1024], in_=bk[:].bitcast(BF16)), [bk], [rec.f["xs"]])
                    bk = bank()
                    bkb = bk[:].bitcast(BF16)
                    tr([(bkb[:, g * 128:(g + 1) * 128], rec[:, R_BTT + g * 128:R_BTT + (g + 1) * 128]) for g in range(2)],
                       identb, [rec.f["bT"]], [bk])
                    op("dve", lambda e: e.tensor_copy(out=rec[:, R_BT:R_BT + 256], in_=bkb[:, 0:256]), [bk], [rec.f["bt"]])

                def stageB2(s, c):
                    rec = recs1[c % 2]
                    dtt = dtts[c % 2]

                    def on_y(g, bk, yi):
                        op("dve", lambda e: e.tensor_tensor(out=rec[:, R_YF + g * 512:R_YF + (g + 1) * 512],
                                                            in0=bk[:], in1=yi[:], op=ALU.add), [bk, yi], [rec.f["yf"]])

                    def on_o(b_, bk):
                        op("act", lambda e: e.copy(out=rec[:, R_RF + b_ * 512:R_RF + (b_ + 1) * 512], in_=bk[:]),
                           [bk], [rec.f["rf"]])
                    scan_chunk(T, 0, rec, dtt, on_y, on_o)
                    dma("sp", rec_d[s, c], rec[:], reads=rec.all)

                for s in range(NSEQ):
                    reset_state(T)
                    stageA(s, 0)
                    if NCH > 1:
                        stageA(s, 1)
                    stageB1(s, 0)
                    for c in range(NCH):
                        if c + 2 < NCH:
                            stageA(s, c + 2)
                        if c + 1 < NCH:
                            stageB1(s, c + 1)
                        stageB2(s, c)
                K.barrier()

        def phase2(layer):
            with ExitStack() as es:
                prm = mk(es, "prm2", [128, 3 * 1024 + PRMW - P_CW], F32)
                PO = 3072 - P_CW
                dma("sp", prm[:, 0:1024], prm_d[layer, :, P_GPOST:P_GPOST + 1024], writes=[prm])
                dma("sp", prm[:, 1024:3072], prm_d[layer, :, P_SSDN:P_SSDN + 2048], writes=[prm])
                dma("sp", prm[:, 3072:], prm_d[layer, :, P_CW:PRMW], writes=[prm])
                w_out = load_w(es, "w_out", wout_d[layer], 16, D)
                T = scan_tiles(es)
                T["dc"] = dir_consts(es, prm, PO, 1)
                recs = [Rec(mk(es, "rec2", [128, RECW], BF16).h) for _ in range(2)]
                dts = [mk(es, "dt2", [128, 32], F32) for _ in range(2)]
                hb = [mk(es, "hb2", [128, D], F32) for _ in range(3)]
                yt = mk(es, "yt", [128, D], F32)
                yt2 = mk(es, "yt2", [128, D], F32)
                rt = mk(es, "rt", [128, D], F32)
                junk = mk(es, "junk2", [128, D], F32)
                ycats = [mk(es, "ycat", [128, 2 * D], BF16) for _ in range(2)]
                ycT = mk(es, "ycT", [128, 16, 128], BF16)
                st8 = mk(es, "st8", [128, 8], F32)
                sq8 = mk(es, "sq8", [128, 8], F32)
                mu8 = mk(es, "mu8", [128, 8], F32)
                ss2 = mk(es, "ss2", [128, 2], F32)
                tm2 = mk(es, "tm2", [128, 2], F32)
                ssm = mk(es, "ssm", [128, 2], F32)
                ss1 = mk(es, "ss1m", [128, 1], F32)
                tm1 = mk(es, "tm1m", [128, 1], F32)
                hout = [mk(es, "hout", [128, D], F32) for _ in range(2)]

                def loads(s, c):
                    r = recs[c % 2]
                    dma("sp", r[:], rec_d[s, c], writes=r.all)
                    dma("sp", dts[c % 2][:], dts_d[s, c], writes=[dts[c % 2]])
                    dma("sp", hb[c % 3][:], h_src(layer, s, c), writes=[hb[c % 3]])

                def chunkC1(s, c):
                    rec, dt = recs[c % 2], dts[c % 2]
                    ycat = ycats[c % 2]
                    def on_y(g, bk, yi):
                        sl = slice(g * 512, (g + 1) * 512)
                        op("dve", lambda e: e.tensor_tensor(out=yt[:, sl], in0=bk[:], in1=yi[:], op=ALU.add), [bk, yi], [yt])
                        op("pool", lambda e: e.tensor_tensor(out=yt[:, sl], in0=yt[:, sl],
                                                             in1=rec[:, R_YF + g * 512:R_YF + (g + 1) * 512],
                                                             op=ALU.add), [yt, rec.f["yf"]], [yt])
                        op("pool", lambda e: e.tensor_tensor(out=yt2[:, sl], in0=rec[:, R_XS + g * 512:R_XS + (g + 1) * 512],
                                                             in1=prm[:, 2048 + g * 512:2048 + (g + 1) * 512],
                                                             op=ALU.mult), [rec.f["xs"], prm], [yt2])
                        op("dve", lambda e: e.tensor_tensor(out=yt[:, sl], in0=yt[:, sl], in1=yt2[:, sl], op=ALU.add),
                           [yt, yt2], [yt])
                        op("dve", lambda e: e.tensor_tensor(out=yt[:, sl], in0=yt[:, sl],
                                                            in1=rec[:, R_ZS + g * 512:R_ZS + (g + 1) * 512],
                                                            op=ALU.mult), [yt, rec.f["zs"]], [yt])
                        op("act", lambda e: e.activation(out=junk[:, sl], in_=yt[:, sl], func=AF.Square,
                                                         accum_out=ss2[:, g:g + 1]), [yt], [junk, ss2])

                    def on_o(b_, bk):
                        sl = slice(b_ * 512, (b_ + 1) * 512)
                        op("dve", lambda e: e.tensor_tensor(out=rt[:, sl], in0=bk[:],
                                                            in1=rec[:, R_RF + b_ * 512:R_RF + (b_ + 1) * 512],
                                                            op=ALU.add), [bk, rec.f["rf"]], [rt])
                    scan_chunk(T, 1, rec, dt, on_y, on_o)
                    op("dve", lambda e: e.tensor_scalar(out=tm2[:], in0=ss2[:], scalar1=1.0 / 512, scalar2=EPS,
                                                        op0=ALU.mult, op1=ALU.add), [ss2], [tm2])
                    op("act", lambda e: e.sqrt(out=tm2[:], in_=tm2[:]), [tm2], [tm2])
                    op("dve", lambda e: e.reciprocal(out=ss2[:], in_=tm2[:]), [tm2], [ss2])
                    for g in range(2):
                        sl = slice(g * 512, (g + 1) * 512)
                        op("dve", lambda e, sl=sl, g=g: e.scalar_tensor_tensor(
                            out=ycat[:, sl], in0=yt[:, sl], scalar=ss2[:, g:g + 1],
                            in1=prm[:, 1024 + g * 512:1024 + (g + 1) * 512], op0=ALU.mult, op1=ALU.mult),
                           [yt, ss2, prm], [ycat])
                    r3 = rt[:].rearrange("p (h e) -> p h e", e=128)
                    j3 = junk[:].rearrange("p (h e) -> p h e", e=128)
                    op("dve", lambda e: e.tensor_reduce(out=st8[:], in_=r3, axis=AX.X, op=ALU.add), [rt], [st8])
                    op("pool", lambda e: e.tensor_tensor(out=junk[:], in0=rt[:], in1=rt[:], op=ALU.mult), [rt], [junk])
                    op("dve", lambda e: e.tensor_reduce(out=sq8[:], in_=j3, axis=AX.X, op=ALU.add), [junk], [sq8])
                    op("dve", lambda e: e.tensor_scalar(out=mu8[:], in0=st8[:], scalar1=1.0 / 128, scalar2=None,
                                                        op0=ALU.mult), [st8], [mu8])
                    op("dve", lambda e: e.tensor_tensor(out=st8[:], in0=mu8[:], in1=mu8[:], op=ALU.mult), [mu8], [st8])
                    op("dve", lambda e: e.scalar_tensor_tensor(out=sq8[:], in0=sq8[:], scalar=1.0 / 128, in1=st8[:],
                                                               op0=ALU.mult, op1=ALU.subtract), [sq8, st8], [sq8])
                    op("dve", lambda e: e.tensor_scalar(out=sq8[:], in0=sq8[:], scalar1=EPS, scalar2=None, op0=ALU.add),
                       [sq8], [sq8])
                    op("act", lambda e: e.sqrt(out=sq8[:], in_=sq8[:]), [sq8], [sq8])
                    op("dve", lambda e: e.reciprocal(out=st8[:], in_=sq8[:]), [sq8], [st8])
                    op("dve", lambda e: e.tensor_tensor(out=r3, in0=r3, in1=mu8[:].unsqueeze(2).broadcast_to([128, 8, 128]),
                                                        op=ALU.subtract), [rt, mu8], [rt])
                    op("dve", lambda e: e.tensor_tensor(out=r3, in0=r3, in1=st8[:].unsqueeze(2).broadcast_to([128, 8, 128]),
                                                        op=ALU.mult), [rt, st8], [rt])
                    for b in range(2):
                        op("pool", lambda e, b=b: e.tensor_tensor(
                            out=ycat[:, 1024:2048].rearrange("p (j b e) -> p j b e", b=2, e=128)[:, :, b, :],
                            in0=rt[:, b * 512:(b + 1) * 512].rearrange("p (j e) -> p j e", e=128),
                            in1=rec[:, R_GS:R_GS + 1024].rearrange("p (j b e) -> p j b e", b=2, e=128)[:, :, b, :],
                            op=ALU.mult), [rt, rec.f["gs"]], [ycat])

                def chunkC2(s, c):
                    hbt = hb[c % 3]
                    ycat = ycats[c % 2]
                    for hf in range(2):
                        bk = bank()
                        bkb = bk[:].bitcast(BF16).rearrange("p (k t) -> p k t", k=8)
                        tr([(bkb[:, k, :], ycat[:, (hf * 8 + k) * 128:(hf * 8 + k + 1) * 128]) for k in range(8)],
                           identb, [ycat], [bk])
                        op("act", lambda e, hf=hf, bkb=bkb: e.copy(out=ycT[:, hf * 8:hf * 8 + 8, :], in_=bkb), [bk], [ycT])
                    mb = []
                    for nh in range(2):
                        bk = bank()
                        mm([(bk[:], ycT[:, k, :], w_out[:, k, nh * 512:(nh + 1) * 512], k == 0, k == 15) for k in range(16)],
                           [ycT, w_out], [bk])
                        op("act", lambda e, nh=nh, bk=bk: e.activation(out=junk[:, nh * 512:(nh + 1) * 512], in_=bk[:],
                                                                       func=AF.Square, accum_out=ssm[:, nh:nh + 1]),
                           [bk], [junk, ssm])
                        mb.append(bk)
                    op("dve", lambda e: e.tensor_tensor(out=ss1[:], in0=ssm[:, 0:1], in1=ssm[:, 1:2], op=ALU.add),
                       [ssm], [ss1])
                    rms_scale(ss1, tm1)
                    ho = hout[c % 2]
                    for nh in range(2):
                        sl = slice(nh * 512, (nh + 1) * 512)
                        op("dve", lambda e, sl=sl, nh=nh: e.scalar_tensor_tensor(
                            out=ho[:, sl], in0=mb[nh][:], scalar=ss1[:, 0:1], in1=prm[:, sl], op0=ALU.mult, op1=ALU.mult),
                           [mb[nh], ss1, prm], [ho])
                    op("pool", lambda e: e.tensor_tensor(out=ho[:], in0=ho[:], in1=hbt[:], op=ALU.add), [ho, hbt], [ho])
                    dma("sp", hmid_d[s, c * 128:(c + 1) * 128, :], ho[:], reads=[ho])

                for s in range(NSEQ):
                    reset_state(T)
                    loads(s, NCH - 1)
                    for c in range(NCH - 1, -1, -1):
                        if c - 1 >= 0:
                            loads(s, c - 1)
                        chunkC1(s, c)
                        if c + 1 <= NCH - 1:
                            chunkC2(s, c + 1)
                    chunkC2(s, 0)
                K.barrier()

        def phase3(layer, last):
            with ExitStack() as es:
                prm = mk(es, "prm3", [128, 2048], F32)
                dma("sp", prm[:], prm_d[layer, :, P_FPRE:P_FPRE + 2048], writes=[prm])
                w_g = load_w(es, "w_g", wg_d[layer], 8, DFF)
                w_u = load_w(es, "w_u", wu_d[layer], 8, DFF)
                w_d = load_w(es, "w_d", wd_d[layer], 22, D)
                hb = [mk(es, "hb3", [128, D], F32) for _ in range(3)]
                junk = mk(es, "junk3", [128, D], F32)
                ss = [mk(es, "ss3", [128, 1], F32) for _ in range(2)]
                tmp1 = [mk(es, "tmp3", [128, 1], F32) for _ in range(2)]
                f_bf = mk(es, "f_bf", [128, D], BF16)
                fT = mk(es, "fT", [128, 8, 128], BF16)
                sg = [mk(es, "sg", [128, 512], F32) for _ in range(2)]
                acts = [mk(es, "act", [128, DFF], BF16) for _ in range(2)]
                actT = mk(es, "actT", [128, 22, 128], BF16)
                ssm = mk(es, "ssm3", [128, 2], F32)
                hout = [mk(es, "hout3", [128, D], F32) for _ in range(2)]

                def load(s, c):
                    dma("sp", hb[c % 3][:], hmid_d[s, c * 128:(c + 1) * 128, :], writes=[hb[c % 3]])

                def chunkD1(s, c):
                    b = hb[c % 3]
                    act = acts[c % 2]
                    s1, t1 = ss[c % 2], tmp1[c % 2]
                    op("act", lambda e: e.activation(out=junk[:], in_=b[:], func=AF.Square, accum_out=s1[:]), [b], [junk, s1])
                    rms_scale(s1, t1)
                    op("dve", lambda e: e.scalar_tensor_tensor(out=f_bf[:], in0=b[:], scalar=s1[:, 0:1], in1=prm[:, 0:1024],
                                                               op0=ALU.mult, op1=ALU.mult), [b, s1, prm], [f_bf])
                    bk = bank()
                    bkb = bk[:].bitcast(BF16).rearrange("p (k t) -> p k t", k=8)
                    tr([(bkb[:, k, :], f_bf[:, k * 128:(k + 1) * 128]) for k in range(8)], identb, [f_bf], [bk])
                    op("act", lambda e: e.copy(out=fT[:], in_=bkb), [bk], [fT])
                    for blk in range(6):
                        c0 = blk * 512
                        n = min(512, DFF - c0)
                        bg = bank()
                        mm([(bg[:, 0:n], fT[:, k, :], w_g[:, k, c0:c0 + n], k == 0, k == 7) for k in range(8)], [fT, w_g], [bg])
                        bu = bank()
                        mm([(bu[:, 0:n], fT[:, k, :], w_u[:, k, c0:c0 + n], k == 0, k == 7) for k in range(8)], [fT, w_u], [bu])
                        sgt = sg[blk % 2]
                        op("act", lambda e, n=n, bg=bg, sgt=sgt: e.activation(out=sgt[:, 0:n], in_=bg[:, 0:n], func=AF.Silu),
                           [bg], [sgt])
                        op("dve", lambda e, n=n, c0=c0, bu=bu, sgt=sgt: e.tensor_tensor(out=act[:, c0:c0 + n], in0=bu[:, 0:n],
                                                                                     in1=sgt[:, 0:n], op=ALU.mult),
                           [bu, sgt], [act])

                def chunkD2(s, c):
                    b = hb[c % 3]
                    act = acts[c % 2]
                    s1, t1 = ss[c % 2], tmp1[c % 2]
                    for tb in range(3):
                        k0 = tb * 8
                        nk = min(8, 22 - k0)
                        bk = bank()
                        bkb = bk[:].bitcast(BF16).rearrange("p (k t) -> p k t", k=8)
                        tr([(bkb[:, k, :], act[:, (k0 + k) * 128:(k0 + k + 1) * 128]) for k in range(nk)], identb, [act], [bk])
                        if tb % 2 == 0:
                            op("act", lambda e, k0=k0, nk=nk, bkb=bkb: e.copy(out=actT[:, k0:k0 + nk, :], in_=bkb[:, 0:nk, :]),
                               [bk], [actT])
                        else:
                            op("dve", lambda e, k0=k0, nk=nk, bkb=bkb: e.tensor_copy(out=actT[:, k0:k0 + nk, :], in_=bkb[:, 0:nk, :]),
                               [bk], [actT])
                    if DBG_3 < 4:
                        return
                    mb = []
                    for nh in range(2):
                        bk = bank()
                        mm([(bk[:], actT[:, k, :], w_d[:, k, nh * 512:(nh + 1) * 512], k == 0, k == 21) for k in range(22)],
                           [actT, w_d], [bk])
                        op("act", lambda e, nh=nh, bk=bk: e.activation(out=junk[:, nh * 512:(nh + 1) * 512], in_=bk[:],
                                                                       func=AF.Square, accum_out=ssm[:, nh:nh + 1]),
                           [bk], [junk, ssm])
                        mb.append(bk)
                    op("dve", lambda e: e.tensor_tensor(out=s1[:], in0=ssm[:, 0:1], in1=ssm[:, 1:2], op=ALU.add), [ssm], [s1])
                    rms_scale(s1, t1)
                    ho = hout[c % 2]
                    for nh in range(2):
                        sl = slice(nh * 512, (nh + 1) * 512)
                        op("dve", lambda e, sl=sl, nh=nh: e.scalar_tensor_tensor(
                            out=ho[:, sl], in0=mb[nh][:], scalar=s1[:, 0:1], in1=prm[:, 1024 + nh * 512:1024 + (nh + 1) * 512],
                            op0=ALU.mult, op1=ALU.mult), [mb[nh], s1, prm], [ho])
                    op("pool", lambda e: e.tensor_tensor(out=ho[:], in0=ho[:], in1=b[:], op=ALU.add), [ho, b], [ho])
                    if DBG_3 < 5:
                        return
                    if last:
                        if c > 0:
                            if DBG_3 == 5:
                                dma("sp", hres_d[s, c * 128:(c + 1) * 128, :], ho[:], reads=[ho])
                            elif DBG_3 == 6:
                                dma("sp", hres_d[s, c * 128:(c + 1) * 128, :], ho[:], reads=[ho])
                                dma("sp", y_d[s, (c - 1) * 128:c * 128, :], ho[:], reads=[ho])
                            elif DBG_3 == 7:
                                dma("pool", y_d[s, (c - 1) * 128:c * 128, :], ho[:], reads=[ho])
                            elif DBG_3 == 8:
                                dma("sp", y_d[s, (c - 1) * 128:c * 128, :], ho[:], reads=[ho])
                            else:
                                dma("sp", y_d[s, (c - 1) * 128:c * 128, :], ho[:], reads=[ho])
                    else:
                        dma("sp", hres_d[s, c * 128:(c + 1) * 128, :], ho[:], reads=[ho])

                for s in range(NSEQ):
                    c_list = list(range(NCH)) if not last else list(range(1, NCH))
                    load(s, c_list[0])
                    if len(c_list) > 1:
                        load(s, c_list[1])
                    chunkD1(s, c_list[0])
                    for i, c in enumerate(c_list):
                        if i + 2 < len(c_list):
                            load(s, c_list[i + 2])
                        if i + 1 < len(c_list):
                            chunkD1(s, c_list[i + 1])
                        chunkD2(s, c)
                K.barrier()
                if last and DBG_3 == 9:
                    with ExitStack() as esd:
                        ct = mk(esd, "ydiag", [128, D], F32)
                        op("pool", lambda e: e.memset(ct[:], 1.0), [], [ct])
                        for s in range(NSEQ):
                            for c in range(1, NCH):
                                dma("sp", y_d[s, (c - 1) * 128:c * 128, :], ct[:], reads=[ct])
                        K.barrier()
                K.barrier()

        for layer in range(DEPTH):
            if DEBUG_STOP == 0:
                break
            phase1(layer)
            if DEBUG_P1_ONLY or DEBUG_STOP == 1:
                break
            phase2(layer)
            if DEBUG_STOP == 2:
                break
            phase3(layer, layer == DEPTH - 1)
        if DBG_YW:
            with ExitStack() as esd:
                ct = mk(esd, "ydiag2", [128, D], F32)
                op("pool", lambda e: e.memset(ct[:], 1.0), [], [ct])
                for s in range(NSEQ):
                    for c in range(1, NCH):
                        dma("sp", y_d[s, (c - 1) * 128:c * 128, :], ct[:], reads=[ct])
                K.barrier()
    return nc


def _consts():
    c = np.zeros((128, CSTW), np.float32)
    p = np.arange(128)
    c[:, C_ID:C_ID + 128] = np.eye(128)
    c[:, C_U:C_U + 128] = (p[:, None] <= p[None, :])
    c[:, C_VGE:C_VGE + 128] = (p[:, None] >= p[None, :])
    c[:, C_SGT:C_SGT + 128] = (p[:, None] > p[None, :])
    c[:, C_REL:C_REL + 128] = (p[None, :] - p[:, None])
    c[:, C_LROW:C_LROW + 128] = p[None, :]
    c[:, C_VALID] = (p >= PADR)
    c[:, C_P] = p
    c[:, C_127P] = 127 - p
    return c


def _rope(L):
    pos = np.arange(L, dtype=np.float32)
    inv = (np.float32(10000.0) ** (-np.arange(0, 64, 2, dtype=np.float32) / np.float32(64))).astype(np.float32)
    ang = (pos[:, None] * inv[None, :]).astype(np.float32)
    cs, sn = np.cos(ang).astype(np.float32), np.sin(ang).astype(np.float32)
    t = np.zeros((L, 256), np.float32)
    t[:, 0:32] = cs; t[:, 32:64] = cs
    t[:, 64:96] = -sn; t[:, 96:128] = sn
    t[:, 128:256] = t[:, 0:128] * np.float32(0.125)
    return t


def _params(DEPTH, norm_mix_pre, norm_mix_post, norm_ffn_pre, norm_ffn_post, conv_w, conv_b, dt_bias, a_log,
            d_skip, ssd_norm, ret_log_decay):
    P = np.zeros((DEPTH, 128, PRMW), np.float32)
    for l in range(DEPTH):
        P[l, :, P_GPRE:P_GPRE + 1024] = norm_mix_pre[l][None, :]
        P[l, :, P_GPOST:P_GPOST + 1024] = norm_mix_post[l][None, :]
        P[l, :, P_FPRE:P_FPRE + 1024] = norm_ffn_pre[l][None, :]
        P[l, :, P_FPOST:P_FPOST + 1024] = norm_ffn_post[l][None, :]
        P[l, :, P_SSDN:P_SSDN + 1024] = ssd_norm[l][None, :]
        P[l, :, P_DSK:P_DSK + 1024] = np.repeat(d_skip[l], 64)[None, :]
        cw = conv_w[l].reshape(5, 12, 128)
        P[l, :, P_CW:P_CW + 60] = cw.transpose(2, 1, 0).reshape(128, 60)
        P[l, :, P_CB:P_CB + 12] = conv_b[l].reshape(12, 128).T
        P[l, :, P_DTB:P_DTB + 32] = dt_bias[l].reshape(32)[None, :]
        P[l, :, P_ALOG:P_ALOG + 32] = a_log[l].reshape(32)[None, :]
        P[l, :, P_LGB:P_LGB + 16] = ret_log_decay[l].reshape(16)[None, :]
        for d in range(2):
            lg = ret_log_decay[l, d]
            P[l, 0:64, P_LGP + 4 * d:P_LGP + 4 * d + 4] = lg[0::2][None, :]
            P[l, 64:128, P_LGP + 4 * d:P_LGP + 4 * d + 4] = lg[1::2][None, :]
    return P


_NC_CACHE = {}


def run(xs_all, meta_tokens, small, w_in, w_out, w_gate, w_up, w_down, n_cores, NSEQ, DEPTH):
    S = xs_all.shape[1]
    NCH = S // 128 + 1
    key = (NSEQ, NCH, DEPTH)
    if key not in _NC_CACHE:
        _NC_CACHE[key] = build(NSEQ, NCH, DEPTH)
    nc = _NC_CACHE[key]
    prm = _params(DEPTH, *small)
    cst = _consts()
    rope = _rope(NCH * 128)
    f = lambda a: np.ascontiguousarray(a, dtype=np.float32)
    in_maps = []
    for i in range(n_cores):
        in_maps.append({"x": f(xs_all[i * NSEQ:(i + 1) * NSEQ]), "meta": f(meta_tokens), "w_in": f(w_in), "w_out": f(w_out),
                        "w_gate": f(w_gate), "w_up": f(w_up), "w_down": f(w_down), "prm": prm, "cst": cst, "rope": rope})
    res = run_bass_kernel_spmd(nc, in_maps, core_ids=list(range(n_cores)))
    return np.concatenate([np.asarray(r["y"]) for r in res.results], axis=0)


def kernel(x_prompt, x_sample, meta_tokens, norm_mix_pre, norm_mix_post, norm_ffn_pre, norm_ffn_post,
           w_in, conv_w, conv_b, dt_bias, a_log, d_skip, ssd_norm, ret_log_decay, w_out, w_gate, w_up, w_down):
    x_prompt = np.asarray(x_prompt); x_sample = np.asarray(x_sample)
    nb = x_prompt.shape[0]
    xs_all = np.concatenate([x_prompt, x_sample], axis=0)
    small = [np.asarray(a, dtype=np.float32) for a in (norm_mix_pre, norm_mix_post, norm_ffn_pre, norm_ffn_post, conv_w,
                                                        conv_b, dt_bias, a_log, d_skip, ssd_norm, ret_log_decay)]
    y = run(xs_all, np.asarray(meta_tokens), small, np.asarray(w_in), np.asarray(w_out), np.asarray(w_gate),
            np.asarray(w_up), np.asarray(w_down), 8, xs_all.shape[0] // 8, np.asarray(w_in).shape[0])
    return (np.ascontiguousarray(y[:nb]), np.ascontiguousarray(y[nb:]))
```
